# Optimizing a Trainium2 kernel written in Bass

```python
import jax, jax.numpy as jnp
from jax import lax
import numpy as np

D_MODEL = 1024
BATCH = 2
SEQ = 8192
DEPTH = 1
DEC_BATCH = 128
DEC_SEQ = 4
PAST_LEN = 8192
PAGE_SIZE = 128

MIX_W = D_MODEL
M_WIDTH = MIX_W // 2
M_HEADS = 4
M_DV = M_WIDTH // M_HEADS
M_DK = M_DV // 2
QK_W = 2 * M_HEADS * M_DK
CONV_W = 4
CHUNK = 128
A_WIDTH = MIX_W - M_WIDTH
A_HEADS = 8
A_HD = A_WIDTH // A_HEADS
A_KV = 2
A_GROUP = A_HEADS // A_KV
KV_W = A_KV * A_HD
WINDOW = 128
D_FF = 4 * D_MODEL
P_DIM = 256
PROJ_W = QK_W + 2 * M_WIDTH + 2 * M_HEADS + A_WIDTH + 2 * KV_W
EPS = 1e-6

kernel_name = "hymba_mlstm_swa_sink_decode_step"

F32 = jnp.float32


def rms_norm(x, g):
    xf = x.astype(F32)
    y = xf * lax.rsqrt(jnp.mean(xf * xf, axis=-1, keepdims=True) + EPS)
    return (y * g.astype(F32)).astype(x.dtype)


def split_projection(z):
    sizes = (QK_W, M_WIDTH, M_WIDTH, 2 * M_HEADS, A_WIDTH, KV_W, KV_W)
    cuts = [int(c) for c in np.cumsum(sizes)[:-1]]
    return jnp.split(z, cuts, axis=-1)


def short_conv(u, buf, w):
    T = u.shape[1]
    up = jnp.concatenate([buf.astype(u.dtype), u], axis=1)
    out = sum(w[j] * up[:, j:j + T] for j in range(CONV_W))
    return jax.nn.silu(out), up[:, T:]


def mlstm_heads(qk, v, gates, b_gates):
    B, T, _ = qk.shape
    q, k = jnp.split(qk, 2, axis=-1)
    q = q.reshape(B, T, M_HEADS, M_DK).transpose(0, 2, 1, 3).astype(F32) * (M_DK ** -0.5)
    k = k.reshape(B, T, M_HEADS, M_DK).transpose(0, 2, 1, 3).astype(F32)
    v = v.reshape(B, T, M_HEADS, M_DV).transpose(0, 2, 1, 3).astype(F32)
    g = gates.astype(F32) + b_gates.astype(F32)
    ig = g[..., :M_HEADS].transpose(0, 2, 1)
    lf = jax.nn.log_sigmoid(g[..., M_HEADS:]).transpose(0, 2, 1)
    return q, k, v, ig, lf


def mlstm_chunk(state, blk):
    c_prev, n_prev, m_prev = state
    q, k, v, ig, lf = blk
    L = q.shape[2]
    b = jnp.cumsum(lf, axis=-1)
    causal = jnp.tril(jnp.ones((L, L), dtype=bool))
    log_d = jnp.where(causal, b[..., :, None] - b[..., None, :] + ig[..., None, :], -jnp.inf)
    log_inter = b + m_prev[..., None]
    m_t = jnp.maximum(log_inter, jnp.max(log_d, axis=-1))
    w_intra = jnp.exp(log_d - m_t[..., None])
    w_inter = jnp.exp(log_inter - m_t)
    s = jnp.einsum('bhtd,bhsd->bhts', q, k) * w_intra
    num = jnp.einsum('bhts,bhsv->bhtv', s, v) + w_inter[..., None] * jnp.einsum('bhtd,bhdv->bhtv', q, c_prev)
    den = jnp.sum(s, axis=-1) + w_inter * jnp.einsum('bhtd,bhd->bht', q, n_prev)
    h = num / jnp.maximum(jnp.abs(den), jnp.exp(-m_t))[..., None]
    b_last = b[..., -1]
    log_w = b_last[..., None] - b + ig
    m_new = jnp.maximum(b_last + m_prev, jnp.max(log_w, axis=-1))
    w_k = jnp.exp(log_w - m_new[..., None])
    decay = jnp.exp(b_last + m_prev - m_new)
    c_new = decay[..., None, None] * c_prev + jnp.einsum('bhs,bhsd,bhsv->bhdv', w_k, k, v)
    n_new = decay[..., None] * n_prev + jnp.einsum('bhs,bhsd->bhd', w_k, k)
    return (c_new, n_new, m_new), h


def mlstm_prompt(q, k, v, ig, lf):
    B, H, T, _ = q.shape
    nc = T // CHUNK

    def to_chunks(a):
        return jnp.moveaxis(a.reshape((B, H, nc, CHUNK) + a.shape[3:]), 2, 0)

    init = (jnp.zeros((B, H, M_DK, M_DV), F32), jnp.zeros((B, H, M_DK), F32), jnp.zeros((B, H), F32))
    final, h = lax.scan(mlstm_chunk, init, tuple(to_chunks(a) for a in (q, k, v, ig, lf)))
    h = jnp.moveaxis(h, 0, 2).reshape(B, H, T, M_DV)
    return final, h


def mlstm_output(h, o, g):
    B, H, T, _ = h.shape
    hn = h * lax.rsqrt(jnp.mean(h * h, axis=-1, keepdims=True) + EPS)
    hn = hn * g.astype(F32).reshape(M_HEADS, 1, M_DV)
    hn = hn.transpose(0, 2, 1, 3).reshape(B, T, M_WIDTH)
    return (jax.nn.sigmoid(o.astype(F32)) * hn).astype(o.dtype)


def alibi_slopes():
    return jnp.exp2(-8.0 * jnp.arange(1, A_HEADS + 1, dtype=F32) / A_HEADS).reshape(A_KV, A_GROUP)


def sink_softmax(scores, sinks):
    s = sinks.astype(F32).reshape(A_KV, A_GROUP)[:, :, None, None]
    m = jnp.maximum(jnp.max(scores, axis=-1, keepdims=True), s)
    e = jnp.exp(scores - m)
    return e / (jnp.sum(e, axis=-1, keepdims=True) + jnp.exp(s - m))


def swa_prompt(q, k, v, sinks):
    B, T = q.shape[:2]
    nb = T // WINDOW
    qb = q.reshape(B, nb, WINDOW, A_KV, A_GROUP, A_HD).astype(F32)
    kb = k.reshape(B, nb, WINDOW, A_KV, A_HD).astype(F32)
    vb = v.reshape(B, nb, WINDOW, A_KV, A_HD).astype(F32)
    pad = ((0, 0), (1, 0), (0, 0), (0, 0), (0, 0))
    kk = jnp.concatenate([jnp.pad(kb, pad)[:, :-1], kb], axis=2)
    vv = jnp.concatenate([jnp.pad(vb, pad)[:, :-1], vb], axis=2)
    qi = jnp.arange(WINDOW)[:, None] + WINDOW
    si = jnp.arange(2 * WINDOW)[None, :]
    d = qi - si
    band = (d >= 0) & (d <= WINDOW)
    not_first = jnp.arange(nb)[:, None, None] > 0
    valid = band[None] & (not_first | (si >= WINDOW)[None])
    scores = jnp.einsum('bnqkgd,bnskd->bnkgqs', qb, kk) * (A_HD ** -0.5)
    scores = scores - alibi_slopes()[:, :, None, None] * d.astype(F32)
    scores = jnp.where(valid[None, :, None, None], scores, -jnp.inf)
    p = sink_softmax(scores, sinks)
    out = jnp.einsum('bnkgqs,bnskd->bnqkgd', p, vv)
    return out.reshape(B, T, A_WIDTH)


def swa_sample(q, k, v, k_buf, v_buf, sinks):
    B, T = q.shape[:2]
    k_all = jnp.concatenate([k_buf.astype(k.dtype), k], axis=1)
    v_all = jnp.concatenate([v_buf.astype(v.dtype), v], axis=1)
    qi = jnp.arange(T)[:, None] + WINDOW
    si = jnp.arange(WINDOW + T)[None, :]
    d = qi - si
    valid = (d >= 0) & (d <= WINDOW)
    qh = q.reshape(B, T, A_KV, A_GROUP, A_HD).astype(F32)
    scores = jnp.einsum('bqkgd,bskd->bkgqs', qh, k_all.astype(F32)) * (A_HD ** -0.5)
    scores = scores - alibi_slopes()[:, :, None, None] * d.astype(F32)
    scores = jnp.where(valid, scores, -jnp.inf)
    p = sink_softmax(scores, sinks)
    out = jnp.einsum('bkgqs,bskd->bqkgd', p, v_all.astype(F32)).reshape(B, T, A_WIDTH)
    return out, k_all[:, T:], v_all[:, T:]


def decoder_layer(x, p, lp, state):
    B, T, _ = x.shape
    xn = rms_norm(x, lp['norm_mix_pre'])
    qk_raw, v_m, o_m, gates, q_a, k_a, v_a = split_projection(xn @ lp['w_in'])
    q_a = q_a.reshape(B, T, A_HEADS, A_HD)
    k_a = k_a.reshape(B, T, A_KV, A_HD)
    v_a = v_a.reshape(B, T, A_KV, A_HD)
    if state is None:
        conv_buf = jnp.zeros((B, CONV_W - 1, QK_W), x.dtype)
    else:
        c0, n0, m0, conv_buf, k_buf, v_buf = state
    qk, conv_new = short_conv(qk_raw, conv_buf, lp['conv_w'])
    q, k, v, ig, lf = mlstm_heads(qk, v_m, gates, lp['b_gates'])
    if state is None:
        (c_new, n_new, m_new), h = mlstm_prompt(q, k, v, ig, lf)
        att = swa_prompt(q_a, k_a, v_a, lp['attn_sinks'])
        k_new, v_new = k_a[:, -WINDOW:], v_a[:, -WINDOW:]
    else:
        init = (c0.astype(F32), n0.astype(F32), m0.astype(F32))
        (c_new, n_new, m_new), h = mlstm_chunk(init, (q, k, v, ig, lf))
        att, k_new, v_new = swa_sample(q_a, k_a, v_a, k_buf, v_buf, lp['attn_sinks'])
    y_m = mlstm_output(h, o_m, lp['mlstm_norm'])
    y_a = rms_norm(att.astype(x.dtype), lp['attn_norm'])
    mix = jnp.concatenate([y_m, y_a], axis=-1) @ lp['w_out']
    x = x + rms_norm(mix, lp['norm_mix_post'])
    u = rms_norm(x, lp['norm_ffn_pre'])
    f = jnp.square(jax.nn.relu(u @ lp['w_up'])) @ lp['w_down']
    x = x + rms_norm(f, lp['norm_ffn_post'])
    x = x + jax.nn.sigmoid(x @ lp['w_pgate']) * (p @ lp['w_pproj'])
    return x, (c_new, n_new, m_new, conv_new, k_new, v_new)


def setup_inputs(seed: int = 0) -> dict:
    key = jax.random.key(seed)
    ks = jax.random.split(key, 26)

    def nrm(k, shape, s):
        return jax.random.normal(k, shape, F32) * s

    def gain(k, width):
        return 1.0 + 0.1 * jax.random.normal(k, (DEPTH, width), F32)

    b_i = nrm(ks[0], (DEPTH, M_HEADS), 0.1)
    b_f = 3.0 + nrm(ks[1], (DEPTH, M_HEADS), 0.5)
    return {
        'x_prompt': nrm(ks[2], (BATCH, SEQ, D_MODEL), 1.0),
        'x_sample': nrm(ks[3], (DEC_BATCH, DEC_SEQ, D_MODEL), 1.0),
        'p_prompt': nrm(ks[4], (DEPTH, BATCH, SEQ, P_DIM), 1.0),
        'p_sample': nrm(ks[5], (DEPTH, DEC_BATCH, DEC_SEQ, P_DIM), 1.0),
        'state_mlstm_c': nrm(ks[6], (DEPTH, DEC_BATCH, M_HEADS, M_DK, M_DV), 0.5),
        'state_mlstm_n': nrm(ks[7], (DEPTH, DEC_BATCH, M_HEADS, M_DK), 1.0),
        'state_mlstm_m': nrm(ks[8], (DEPTH, DEC_BATCH, M_HEADS), 1.0),
        'state_mlstm_conv': nrm(ks[9], (DEPTH, DEC_BATCH, CONV_W - 1, QK_W), 1.0),
        'cache_swa_k': nrm(ks[10], (DEPTH, DEC_BATCH, WINDOW, A_KV, A_HD), 1.0),
        'cache_swa_v': nrm(ks[11], (DEPTH, DEC_BATCH, WINDOW, A_KV, A_HD), 1.0),
        'norm_mix_pre': gain(ks[12], D_MODEL),
        'w_in': nrm(ks[13], (DEPTH, D_MODEL, PROJ_W), D_MODEL ** -0.5),
        'b_gates': jnp.concatenate([b_i, b_f], axis=-1),
        'conv_w': nrm(ks[14], (DEPTH, CONV_W, QK_W), CONV_W ** -0.5),
        'mlstm_norm': gain(ks[15], M_WIDTH),
        'attn_sinks': nrm(ks[16], (DEPTH, A_HEADS), 1.0),
        'attn_norm': gain(ks[17], A_WIDTH),
        'w_out': nrm(ks[18], (DEPTH, MIX_W, D_MODEL), MIX_W ** -0.5),
        'norm_mix_post': gain(ks[19], D_MODEL),
        'norm_ffn_pre': gain(ks[20], D_MODEL),
        'w_up': nrm(ks[21], (DEPTH, D_MODEL, D_FF), D_MODEL ** -0.5),
        'w_down': nrm(ks[22], (DEPTH, D_FF, D_MODEL), D_FF ** -0.5),
        'norm_ffn_post': gain(ks[23], D_MODEL),
        'w_pgate': nrm(ks[24], (DEPTH, D_MODEL, D_MODEL), D_MODEL ** -0.5),
        'w_pproj': nrm(ks[25], (DEPTH, P_DIM, D_MODEL), P_DIM ** -0.5),
    }


def reference(x_prompt, x_sample, p_prompt, p_sample, state_mlstm_c, state_mlstm_n, state_mlstm_m,
              state_mlstm_conv, cache_swa_k, cache_swa_v, norm_mix_pre, w_in, b_gates, conv_w,
              mlstm_norm, attn_sinks, attn_norm, w_out, norm_mix_post, norm_ffn_pre, w_up, w_down,
              norm_ffn_post, w_pgate, w_pproj):
    hp, hs = x_prompt, x_sample
    new_p = [[] for _ in range(6)]
    new_s = [[] for _ in range(6)]
    for i in range(DEPTH):
        lp = {
            'norm_mix_pre': norm_mix_pre[i], 'w_in': w_in[i], 'b_gates': b_gates[i],
            'conv_w': conv_w[i], 'mlstm_norm': mlstm_norm[i], 'attn_sinks': attn_sinks[i],
            'attn_norm': attn_norm[i], 'w_out': w_out[i], 'norm_mix_post': norm_mix_post[i],
            'norm_ffn_pre': norm_ffn_pre[i], 'w_up': w_up[i], 'w_down': w_down[i],
            'norm_ffn_post': norm_ffn_post[i], 'w_pgate': w_pgate[i], 'w_pproj': w_pproj[i],
        }
        hp, sp = decoder_layer(hp, p_prompt[i], lp, None)
        st = (state_mlstm_c[i], state_mlstm_n[i], state_mlstm_m[i], state_mlstm_conv[i],
              cache_swa_k[i], cache_swa_v[i])
        hs, ss = decoder_layer(hs, p_sample[i], lp, st)
        for lst, a in zip(new_p, sp):
            lst.append(a)
        for lst, a in zip(new_s, ss):
            lst.append(a)
    c_p, n_p, m_p, conv_p, k_p, v_p = [jnp.stack(l) for l in new_p]
    c_s, n_s, m_s, conv_s, k_s, v_s = [jnp.stack(l) for l in new_s]
    return (hp, hs, c_p, n_p, m_p, conv_p, k_p, v_p, c_s, n_s, m_s, conv_s, k_s, v_s)
```

```python
import contextlib
import numpy as np
import concourse.bass as bass
import concourse.mybir as mybir
from concourse.bass_utils import run_bass_kernel_spmd

F32 = mybir.dt.float32
BF16 = mybir.dt.bfloat16
ALU = mybir.AluOpType
AF = mybir.ActivationFunctionType
AX = mybir.AxisListType

ENGS = ("pe", "act", "dve", "pool", "sp")
DMA_NS = 8

D = 1024
PROJ = 2312
NCORES = 8
SEG = 2048
NEG = -1.0e30
EPS = 1e-6


class Op:
    __slots__ = ("eng", "fn", "dma", "deps", "mile", "slot", "idx", "needed")

    def __init__(self, eng, fn, dma):
        self.eng = eng
        self.fn = fn
        self.dma = dma
        self.deps = ()
        self.mile = None
        self.slot = None
        self.idx = None
        self.needed = False


class Prog:
    def __init__(self, nc):
        self.nc = nc
        self.ops = []
        self.last_w = {}
        self.rd_eng = {}
        self.rd_dma = {}
        self.ranges = {}
        self.by_base = {}
        self.ovl = {}

    @staticmethod
    def base(k):
        return k[0] if isinstance(k, tuple) else k

    def overlapping(self, b):
        if b not in self.ovl:
            r = self.ranges.get(b)
            res = []
            if r is not None:
                for b2, r2 in self.ranges.items():
                    if b2 != b and r2[0] < r[1] and r[0] < r2[1]:
                        res.append(b2)
            self.ovl[b] = res
        return self.ovl[b]

    def op(self, eng, fn, reads=(), writes=(), dma=False):
        o = Op(eng, fn, dma)
        o.idx = len(self.ops)
        deps = set()
        ps_r = [k for k in reads if isinstance(k, str) and (k.startswith("pb") or k == "ptr")]
        if ps_r:
            writes = list(writes) + [k for k in ps_r if k not in writes]

        def dep_w(k):
            w = self.last_w.get(k)
            if w is not None:
                deps.add(w)

        def dep_r(k):
            for v in self.rd_eng.get(k, {}).values():
                deps.add(v)
            for v in self.rd_dma.get(k, ()):
                deps.add(v)

        for r in reads:
            dep_w(r)
            for b2 in self.overlapping(self.base(r)):
                for k2 in self.by_base.get(b2, ()):
                    dep_w(k2)
        for k in writes:
            dep_w(k)
            dep_r(k)
            for b2 in self.overlapping(self.base(k)):
                for k2 in self.by_base.get(b2, ()):
                    dep_w(k2)
                    dep_r(k2)
        o.deps = tuple(sorted(deps))
        for r in reads:
            self.by_base.setdefault(self.base(r), set()).add(r)
            if dma:
                self.rd_dma.setdefault(r, []).append(o.idx)
            else:
                self.rd_eng.setdefault(r, {})[eng] = o.idx
        for k in writes:
            self.by_base.setdefault(self.base(k), set()).add(k)
            self.last_w[k] = o.idx
            self.rd_eng[k] = {}
            self.rd_dma[k] = []
        self.ops.append(o)
        return o

    def pe(self, fn, reads=(), writes=()):
        return self.op("pe", fn, reads, writes)

    def act(self, fn, reads=(), writes=()):
        return self.op("act", fn, reads, writes)

    def dve(self, fn, reads=(), writes=()):
        return self.op("dve", fn, reads, writes)

    def pool(self, fn, reads=(), writes=()):
        return self.op("pool", fn, reads, writes)

    def dma(self, q, fn, reads=(), writes=()):
        return self.op(q, fn, reads, writes, dma=True)

    @staticmethod
    def _skip(p, o):
        return (not p.dma) and (not o.dma) and p.eng == "pe" and o.eng == "pe"

    def emit(self):
        nc = self.nc
        ops = self.ops
        import os as _os
        _lim = int(_os.environ.get("MK_STOP_AFTER", "0"))
        if _lim > 0:
            ops = ops[:_lim]
        print("emitting %d of %d ops" % (len(ops), len(self.ops)))
        for o in ops:
            for d in o.deps:
                p = ops[d]
                if p.dma or self._skip(p, o):
                    continue
                p.needed = True
        cnt = {e: 0 for e in ENGS}
        dcnt = {e: 0 for e in ENGS}
        per_eng = {e: [] for e in ENGS}
        for o in ops:
            if o.dma:
                o.slot = dcnt[o.eng]
                dcnt[o.eng] += 1
            elif o.needed:
                cnt[o.eng] += 1
                o.mile = cnt[o.eng]
            per_eng[o.eng].append(o)
        with contextlib.ExitStack() as st:
            csem = {e: st.enter_context(nc.semaphore("c_" + e)) for e in ENGS if e != "sp"}
            dsem = {
                e: [st.enter_context(nc.semaphore("d_%s_%d" % (e, i))) for i in range(DMA_NS)]
                for e in ("sp", "act", "pool")
                if dcnt[e] > 0
            }
            block = st.enter_context(nc.Block())

            def need_of(p):
                if p.dma:
                    return dsem[p.eng][p.slot % DMA_NS], 16 * (p.slot // DMA_NS + 1)
                return csem[p.eng], p.mile

            def run_engine(ename, engobj):
                seen = {}

                def wait(sem, val):
                    k = id(sem)
                    if seen.get(k, 0) >= val:
                        return
                    seen[k] = val
                    engobj.wait_ge(sem, val)

                for o in per_eng[ename]:
                    for d in o.deps:
                        p = ops[d]
                        if self._skip(p, o):
                            continue
                        s, v = need_of(p)
                        wait(s, v)
                    if o.dma:
                        if o.slot >= DMA_NS:
                            wait(dsem[ename][o.slot % DMA_NS], 16 * (o.slot // DMA_NS))
                        ins = o.fn(engobj)
                        ins.then_inc(dsem[ename][o.slot % DMA_NS], 16)
                    else:
                        ins = o.fn(engobj)
                        if o.mile is not None:
                            ins.then_inc(csem[ename], 1)
                n = dcnt.get(ename, 0)
                for i in range(min(DMA_NS, n)):
                    uses = (n - i + DMA_NS - 1) // DMA_NS
                    wait(dsem[ename][i], 16 * uses)

            @block.tensor
            def _(e):
                run_engine("pe", e)

            @block.scalar
            def _(e):
                run_engine("act", e)

            @block.vector
            def _(e):
                run_engine("dve", e)

            @block.gpsimd
            def _(e):
                run_engine("pool", e)

            @block.sync
            def _(e):
                run_engine("sp", e)
        return cnt, dcnt


def _slopes():
    return np.exp2(-8.0 * np.arange(1, 9, dtype=np.float64) / 8.0)


def make_consts():
    c = {}
    c["c_ident"] = np.eye(128, dtype=np.float32)
    s = np.arange(128)
    tri = (s[:, None] <= s[None, :]).astype(np.float32)
    mp = np.stack([tri, np.ones((128, 128), np.float32)], axis=1)
    c["c_mskp"] = np.ascontiguousarray(mp)
    s64 = np.arange(64)
    same = (s64[:, None] // 4 == s64[None, :] // 4)
    ms = np.stack([(same & (s64[:, None] <= s64[None, :])).astype(np.float32), same.astype(np.float32)], axis=1)
    c["c_msks"] = np.ascontiguousarray(ms)
    sl = _slopes()
    t = np.arange(128)[:, None]
    j = np.arange(256)[None, :]
    dist = t + 128 - j
    valid = (dist >= 0) & (dist <= 128)
    ab = np.where(valid[:, None, :], -sl[None, :, None] * dist[:, None, :], NEG)
    c["c_abias"] = ab.astype(np.float32)
    r = (np.arange(64) % 4)[:, None]
    jj = np.arange(128)[None, :]
    dist = 128 + r - jj
    valid = dist <= 128
    sb = np.where(valid[:, None, :], -sl[None, :, None] * dist[:, None, :], NEG)
    tq = np.arange(64)[:, None]
    tk = np.arange(64)[None, :]
    dist2 = (tq % 4) - (tk % 4)
    valid2 = (tq // 4 == tk // 4) & (dist2 >= 0)
    sn = np.where(valid2[:, None, :], -sl[None, :, None] * dist2[:, None, :], NEG)
    c["c_sbias"] = np.concatenate([sb, sn], axis=2).astype(np.float32)
    seq_of = np.arange(64) // 4
    rowsel = (seq_of[:, None] == np.arange(16)[None, :]).astype(np.float32)
    c["c_rowsel"] = rowsel
    colsel = np.broadcast_to(rowsel.T[None, :, :], (128, 16, 64)).astype(np.float32)
    c["c_colsel"] = np.ascontiguousarray(colsel)
    esel = np.zeros((4, 128), np.float32)
    for k in range(4):
        esel[k, (k % 2) * 64:(k % 2) * 64 + 64] = 1.0
    c["c_esel"] = esel
    sel2 = np.zeros((4, 2), np.float32)
    for k in range(4):
        sel2[k, k // 2] = 1.0
    c["c_sel2"] = sel2
    return c


def build_program(n_groups=16, n_prefix=12, do_sample=True):
    nc = bass.Bass("TRN2", target_bir_lowering=False)
    P = Prog(nc)
    st = contextlib.ExitStack()

    def din(name, shape):
        return nc.dram_tensor(name, list(shape), F32, kind="ExternalInput").ap()

    def dout(name, shape):
        return nc.dram_tensor(name, list(shape), F32, kind="ExternalOutput").ap()

    NTOK = n_groups * 512
    NMAIN = (n_groups - n_prefix) * 512
    xs_d = din("xs", [NTOK, D])
    pp_d = din("pp", [NMAIN, 256])
    xsm_d = din("xsm", [64, D])
    psm_d = din("psm", [64, 256])
    stc_d = din("st_c", [16, 4, 64, 128])
    stn_d = din("st_n", [16, 4, 64])
    stm_d = din("st_m", [16, 4])
    stv_d = din("st_conv", [48, 512])
    ck_d = din("ck", [16, 128, 128])
    cv_d = din("cv", [16, 128, 128])
    prevb_d = din("prevbias", [128, 1])
    g_pre_d = din("norm_mix_pre", [1, D])
    w_in_d = din("w_in", [D, PROJ])
    bg_d = din("b_gates", [1, 8])
    cw_d = din("conv_w", [4, 512])
    g_ml_d = din("mlstm_norm", [1, 512])
    sink_d = din("attn_sinks", [1, 8])
    g_at_d = din("attn_norm", [1, 512])
    w_out_d = din("w_out", [D, D])
    g_post_d = din("norm_mix_post", [1, D])
    g_fpre_d = din("norm_ffn_pre", [1, D])
    w_up_d = din("w_up", [D, 4096])
    w_dn_d = din("w_down", [4096, D])
    g_fpost_d = din("norm_ffn_post", [1, D])
    w_pg_d = din("w_pgate", [D, D])
    w_pp_d = din("w_pproj", [256, D])
    cst = {k: din(k, v.shape) for k, v in make_consts().items()}

    y_d = dout("y", [NMAIN, D])
    ys_d = dout("ysm", [64, D])
    c_p_d = dout("c_p", [4, 64, 128])
    n_p_d = dout("n_p", [4, 64])
    m_p_d = dout("m_p", [4, 1])
    conv_p_d = dout("conv_p", [3, 512])
    k_p_d = dout("k_p", [128, 128])
    v_p_d = dout("v_p", [128, 128])
    c_s_d = dout("c_s", [16, 4, 64, 128])
    n_s_d = dout("n_s", [16, 4, 64])
    m_s_d = dout("m_s", [16, 4])
    conv_s_d = dout("conv_s", [48, 512])
    k_s_d = dout("k_s", [16, 128, 128])
    v_s_d = dout("v_s", [16, 128, 128])

    ARENA_WORDS = 53200
    with st:
        arena = st.enter_context(nc.sbuf_tensor("arena", [128, ARENA_WORDS], F32))
        off = [0]

        def alloc(name, shape, dt=F32, at=None):
            nb = 4 if dt == F32 else 2
            n = 1
            for d_ in shape[1:]:
                n *= d_
            size = (n * nb + 31) // 32 * 32
            o = off[0] if at is None else at
            assert o % 32 == 0
            if at is None:
                off[0] = o + size
            assert o + size <= ARENA_WORDS * 4, (name, o, size)
            P.ranges[name] = (o, o + size)
            v = arena[0:shape[0], o // 4:(o + size) // 4]
            if dt != F32:
                v = v.bitcast(dt)
            v = v[:, 0:n]
            if len(shape) == 3:
                v = v.rearrange("p (a b) -> p a b", a=shape[1])
            elif len(shape) == 4:
                v = v.rearrange("p (a b c) -> p a b c", a=shape[1], b=shape[2])
            return v

        def psum(name, shape, dt=F32):
            return st.enter_context(nc.psum_tensor(name, list(shape), dt))

        identf = alloc("identf", [128, 128])
        ident = alloc("ident", [128, 128], BF16)
        g_pre = alloc("g_pre", [128, D])
        g_post = alloc("g_post", [128, D])
        g_fpre = alloc("g_fpre", [128, D])
        g_fpost = alloc("g_fpost", [128, D])
        g_ml = alloc("g_ml", [128, 512])
        g_at = alloc("g_at", [128, 512])
        bgate = alloc("bgate", [128, 8])
        sinks = alloc("sinks", [128, 8])
        cw = alloc("cw", [128, 4, 4])
        esel = alloc("esel", [4, 128])
        sel2 = alloc("sel2", [4, 2])
        prevb = alloc("prevb", [128, 1])
        mhalf = alloc("mhalf", [128, 16])
        w_res = alloc("w_res", [128, 8, 776], BF16)
        w_pp_sb = alloc("w_pp", [128, 2, D], BF16)
        wu = [alloc("wu%d" % i, [128, 8, 512], BF16) for i in range(2)]
        wd = [alloc("wd%d" % i, [128, 4, 512], BF16) for i in range(2)]
        xt0 = alloc("xt0", [128, D])
        xnb = alloc("xnb", [128, D], BF16)
        xnb2 = alloc("xnb2", [128, D], BF16)
        xT = alloc("xT", [128, 8, 512], BF16)
        tmp = alloc("tmp", [128, D])
        junk = alloc("junk", [128, D], BF16, at=P.ranges["tmp"][0])
        cvtP = alloc("cvtP", [3, 512], at=P.ranges["tmp"][0])
        cvoP = alloc("cvoP", [128, 4, 3], at=P.ranges["tmp"][0] + 2048)
        raw = alloc("raw", [128, 4, 515])
        kaT = [[alloc("kaT%d_%d" % (i, kv), [128, 512], BF16) for kv in range(2)] for i in range(2)]
        va = [alloc("va%d" % i, [128, 2, 64], BF16) for i in range(8)]
        kvout = alloc("kvout", [128, 256])
        ptb = alloc("ptb", [128, 4, 256], BF16)
        ppT = alloc("ppT", [128, 2, 512], BF16)
        ytm = [alloc("ytm%d" % i, [128, D], BF16) for i in range(4)]
        sm = alloc("sm", [128, 128])
        lfn = alloc("lfn", [128, 4, 4])
        av = alloc("av", [128, 4, 4])
        bneg = alloc("bneg", [128, 4, 4])
        ev = alloc("ev", [128, 4, 4])
        fl = alloc("fl", [128, 4, 4])
        wi = alloc("wi", [128, 4, 4])
        tk = alloc("tk", [128, 4, 8])
        tsm = alloc("tsm", [4, 160])
        rhsd = alloc("rhsd", [4, 16, 2])
        dstt = alloc("dstt", [128, 16, 2])
        gtA = alloc("gtA", [128, 4, 8])
        gtB = alloc("gtB", [128, 4, 8])
        mTp = alloc("mT", [4, 1])
        mTs = alloc("mTs", [4, 16])
        Cst = alloc("Cst", [128, 2, 129])
        Cbf = alloc("Cbf", [128, 2, 129], BF16)
        ktmB = alloc("ktmB", [128, 4, 2, 128], BF16)
        p0 = off[0]
        qaT = alloc("qaT", [128, 4, 512], BF16)
        vaug = [alloc("vaug%d" % i, [128, 4, 129], BF16) for i in range(4)]
        sig = [alloc("sig%d" % i, [128, 512], BF16) for i in range(4)]
        qz = alloc("qz", [128, 4, 512], BF16)
        vaugB = [alloc("vaugB%d" % i, [128, 4, 129], BF16, at=P.ranges["sig0"][0] + i * 1056) for i in range(4)]
        kT2 = alloc("kT2", [128, 2, 512], BF16)
        p1 = off[0]
        w_rest = alloc("w_rest", [128, 8, 1536], BF16)
        pW = off[0]
        off[0] = p1
        cacc = alloc("cacc", [128, 512])
        csg = alloc("csg", [128, 512])
        sc = alloc("sc", [128, 4, 256])
        pex = alloc("pex", [128, 4, 256], BF16)
        pT = alloc("pT", [128, 8, 128], BF16)
        att = alloc("att", [128, 512])
        wT = alloc("wT", [128, 4, 128], BF16)
        tot = alloc("tot", [128, 4, 129])
        gs = alloc("gs", [128, 512])
        gx = alloc("gx", [4, 2, 512], at=P.ranges["tot"][0])
        evaug = alloc("evaug", [128, 4, 129], BF16)
        ktm = alloc("ktm", [128, 4, 2, 128], BF16)
        pM = off[0]
        off[0] = p0
        hT = alloc("hT", [128, 32, 512], BF16)
        hr = alloc("hr", [128, 512])
        f1 = alloc("f1", [128, 4, 512])
        pF = off[0]
        off[0] = max(pW, pM, pF)
        pS = off[0]
        abias = alloc("abias", [128, 8, 256])
        mskp = alloc("mskp", [128, 2, 128])
        xt123 = [alloc("xt%d" % i, [128, D]) for i in (1, 2, 3)]
        xt = [xt0] + xt123
        pPend = off[0]
        sc2 = alloc("sc2", [128, 4, 256], at=pPend)
        pex2 = alloc("pex2", [128, 4, 256], BF16, at=pPend + 4096)
        pT2 = alloc("pT2", [128, 8, 128], BF16, at=pPend + 6144)
        xtB = [alloc("xtB%d" % i, [128, D], at=pPend + 8192 + i * 4096) for i in range(4)]
        off[0] = pS
        Css = alloc("Css", [128, 16, 2, 129])
        kcT = alloc("kcT", [128, 16, 128], BF16)
        vcb = alloc("vcb", [128, 16, 128], BF16)
        kcb = alloc("kcb", [128, 16, 128], BF16, at=p0)
        sbias = alloc("sbias", [64, 8, 192])
        colsel = alloc("colsel", [128, 16, 64], BF16)
        qam = [alloc("qam%d" % i, [128, 16, 64], BF16) for i in range(2)]
        qm = alloc("qm", [128, 2, 16, 64], BF16)
        pTm = alloc("pTm", [128, 16, 64], BF16)
        evm = alloc("evm", [128, 4, 129], BF16)
        raws = alloc("raws", [128, 4, 16, 7])
        hist = alloc("hist", [48, 512], at=p0 + 4096)
        cvt = alloc("cvt", [48, 512])
        cvo = alloc("cvo", [128, 4, 48])
        msks = alloc("msks", [64, 2, 64])
        rowsel = alloc("rowsel", [64, 16])
        Cbr = [alloc("Cbr%d" % i, [128, 2, 129], BF16) for i in range(4)]
        pSend = off[0]
        print("arena bytes: G+phase=%d prompt_end=%d sample_end=%d limit=%d" % (pS, pPend, pSend, ARENA_WORDS * 4))
        pb = [psum("pb%d" % i, [128, 512]) for i in range(7)]
        ptr = psum("ptr", [128, 8, 128], BF16)

        def ld(q, dst, src, key, **kw):
            P.dma(q, lambda e: e.dma_start(out=dst, in_=src, **kw), writes=[key])

        ld("sp", identf, cst["c_ident"], "identf")
        ld("sp", g_pre, g_pre_d[0].partition_broadcast(128), "g_pre")
        ld("sp", bgate, bg_d[0].partition_broadcast(128), "bgate")
        for j in range(4):
            ld("sp", cw[:, :, j], cw_d[j].rearrange("(c p) -> p c", p=128), "cw", allow_slow_non_contiguous=True)
        ld("sp", mskp, cst["c_mskp"], "mskp")
        ld("sp", esel, cst["c_esel"], "esel")
        ld("sp", sel2, cst["c_sel2"], "sel2")
        ld("sp", prevb, prevb_d, "prevb")
        ld("sp", sinks, sink_d[0].partition_broadcast(128), "sinks")
        ld("sp", g_post, g_post_d[0].partition_broadcast(128), "g_post")
        ld("sp", g_fpre, g_fpre_d[0].partition_broadcast(128), "g_fpre")
        ld("sp", g_fpost, g_fpost_d[0].partition_broadcast(128), "g_fpost")
        ld("sp", g_ml, g_ml_d[0].partition_broadcast(128), "g_ml")
        ld("sp", g_at, g_at_d[0].partition_broadcast(128), "g_at")
        ld("sp", abias, cst["c_abias"], "abias")
        P.dve(lambda e: e.tensor_copy(out=ident, in_=identf), ["identf"], ["ident"])
        w_in_v = w_in_d.rearrange("(kc p) n -> p kc n", p=128)
        for kc in range(8):
            ld("pool", w_res[:, kc, 0:768], w_in_v[:, kc, 256:1024], ("w_res", kc, 0))
            ld("pool", w_res[:, kc, 768:776], w_in_v[:, kc, 1536:1544], ("w_res", kc, 1))
        W_RES_KEYS = [("w_res", kc, i) for kc in range(8) for i in range(2)]
        ld("pool", w_pp_sb, w_pp_d.rearrange("(kc p) n -> p kc n", p=128), "w_pp")
        w_out_v = w_out_d.rearrange("(kc p) n -> p kc n", p=128)
        w_pg_v = w_pg_d.rearrange("(kc p) n -> p kc n", p=128)
        w_up_v = w_up_d.rearrange("(kc p) n -> p kc n", p=128)
        w_dn_v = w_dn_d.rearrange("(jc p) n -> p jc n", p=128)

        w_rest_scr = nc.dram_tensor("w_rest_scr", [128, 8, 1536], BF16).ap()
        w_rest_state = {"prefetched": False}
        def emit_w_rest_scr():
            for kc in range(8):
                P.dma("pool", lambda e, kc=kc: e.dma_start(out=w_rest_scr[:, kc, 0:256], in_=w_in_v[:, kc, 0:256]), [], [("w_rest_scr", kc, 0)])
                P.dma("pool", lambda e, kc=kc: e.dma_start(out=w_rest_scr[:, kc, 256:768], in_=w_in_v[:, kc, 1024:1536]), [], [("w_rest_scr", kc, 1)])
                for j in range(4):
                    P.dma("pool", lambda e, kc=kc, j=j: e.dma_start(
                        out=w_rest_scr[:, kc, 768 + j * 128:768 + (j + 1) * 128].rearrange("p (kv d) -> p kv d", kv=2),
                        in_=w_in_v[:, kc, 1544:2056].rearrange("p (kv j d) -> p j kv d", kv=2, j=4)[:, j]), [], [("w_rest_scr", kc, 2, j)])
                P.dma("pool", lambda e, kc=kc: e.dma_start(out=w_rest_scr[:, kc, 1280:1536], in_=w_in_v[:, kc, 2056:2312]), [], [("w_rest_scr", kc, 3)])

        W_SCR_KEYS = [("w_rest_scr", kc, i) for kc in range(8) for i in (0, 1, 3)] + [("w_rest_scr", kc, 2, j) for kc in range(8) for j in range(4)]

        def load_w_rest():
            P.dma("sp", lambda e: e.dma_start(out=w_rest, in_=w_rest_scr), W_SCR_KEYS, W_REST_KEYS)

        W_REST_KEYS = [("w_rest", kc, i) for kc in range(8) for i in (0, 1, 3)] + [("w_rest", kc, 2, j) for kc in range(8) for j in range(4)]

        wu_ctr = [0]
        scr_big = nc.dram_tensor("scr_big", [12, 128, 8 * 512], BF16).ap()
        scr_dn = nc.dram_tensor("scr_dn", [16, 128, 4 * 512], BF16).ap()
        WMODE = {"save": False, "load": False}
        BIGTAG = {"wo": 0, "up": 2, "pg": 10}

        def WUK(b):
            return [("wu", b, 0), ("wu", b, 1)]

        def next_wu(src, tag):
            b = wu_ctr[0] % 2
            wu_ctr[0] += 1
            slot = BIGTAG[tag[0]] + tag[1]
            if WMODE["load"]:
                P.dma("sp", lambda e: e.dma_start(out=wu[b].rearrange("p a b -> p (a b)"), in_=scr_big[slot]), [("scr_big", slot)], WUK(b))
                return b
            P.dma("pool", lambda e: e.dma_start(out=wu[b], in_=src), [], WUK(b))
            if WMODE["save"]:
                P.dma("sp", lambda e: e.dma_start(out=scr_big[slot], in_=wu[b].rearrange("p a b -> p (a b)")), WUK(b), [("scr_big", slot)])
            return b

        def load_wd(wbuf, wkey, src, slot):
            if WMODE["load"]:
                P.dma("sp", lambda e: e.dma_start(out=wbuf.rearrange("p a b -> p (a b)"), in_=scr_dn[slot]), [("scr_dn", slot)], [wkey])
                return
            P.dma("pool", lambda e: e.dma_start(out=wbuf, in_=src), [], [wkey])
            if WMODE["save"]:
                P.dma("sp", lambda e: e.dma_start(out=scr_dn[slot], in_=wbuf.rearrange("p a b -> p (a b)")), [wkey], [("scr_dn", slot)])

        P.pool(lambda e: e.memset(mhalf, -0.5), [], ["mhalf"])
        P.dve(lambda e: e.tensor_scalar(out=cw[:, 0:2, :], in0=cw[:, 0:2, :], scalar1=0.0625, scalar2=None, op0=ALU.mult), ["cw"], ["cw"])
        P.dve(lambda e: e.tensor_scalar(out=cw[:, 2:4, :], in0=cw[:, 2:4, :], scalar1=0.5, scalar2=None, op0=ALU.mult), ["cw"], ["cw"])
        P.dve(lambda e: e.tensor_scalar(out=g_ml, in0=g_ml, scalar1=0.5, scalar2=None, op0=ALU.mult), ["g_ml"], ["g_ml"])
        P.pool(lambda e: e.memset(raw, 0.0), [], ["raw"])
        P.pool(lambda e: e.memset(Cst, 0.0), [], ["Cst"])
        P.pool(lambda e: e.memset(Cbf, 0.0), [], ["Cbf"])
        P.pool(lambda e: e.memset(mTp, 0.0), [], ["mT"])
        for i in range(2):
            for kv in range(2):
                P.pool(lambda e, i=i, kv=kv: e.memset(kaT[i][kv], 0.0), [], ["kaT%d_%d" % (i, kv)])
        for i in range(8):
            P.pool(lambda e, i=i: e.memset(va[i], 0.0), [], [("va%d" % i)])

        def rms_rstd(ss_col, out_col, n, keys_r, key_w):
            P.dve(lambda e: e.tensor_scalar(out=out_col, in0=ss_col, scalar1=1.0 / n, scalar2=EPS,
                                            op0=ALU.mult, op1=ALU.add), keys_r, [key_w])
            shp = list(out_col.shape)
            P.pool(lambda e: e.tensor_tensor(out=out_col, in0=out_col, in1=mhalf[0:shp[0], 0:shp[1]], op=ALU.pow),
                   [key_w, "mhalf"], [key_w])

        def capture(fn):
            buf = []
            P.op = lambda eng, f, reads=(), writes=(), dma=False: buf.append((eng, f, tuple(reads), tuple(writes), dma))
            try:
                fn()
            finally:
                del P.op
            return buf

        def interleave(*bufs):
            pos = [0] * len(bufs)
            total = sum(len(b) for b in bufs)
            import os as _os2
            _ck = int(_os2.environ.get("MK_CHUNK", "0"))
            if _ck > 0:
                while any(pos[k] < len(b) for k, b in enumerate(bufs)):
                    for k, b in enumerate(bufs):
                        n_ = max(1, int(round(_ck * len(b) / max(len(x) for x in bufs))))
                        for item in b[pos[k]:pos[k] + n_]:
                            P.op(*item)
                        pos[k] = min(len(b), pos[k] + n_)
                return
            for _ in range(total):
                best, bestf = None, None
                for k, b in enumerate(bufs):
                    if pos[k] < len(b):
                        fr = pos[k] / len(b)
                        if best is None or fr < bestf:
                            best, bestf = k, fr
                item = bufs[best][pos[best]]
                pos[best] += 1
                P.op(*item)

        def transpose_to(srcb, srckeys, Tt, dstT, c0, dstkey, nk=8):
            for kc in range(nk):
                P.pe(lambda e, kc=kc: e.transpose(out=ptr[:, kc, 0:Tt], in_=srcb[0:Tt, kc * 128:(kc + 1) * 128],
                                                  identity=ident[0:Tt, 0:Tt]), list(srckeys) + ["ident"], ["ptr"])
            P.act(lambda e: e.copy(out=dstT[:, 0:nk, c0:c0 + Tt], in_=ptr[:, 0:nk, 0:Tt]), ["ptr"], [dstkey])

        def norm_transpose(src, srckey, gain, gkey, Tt, c0, dstkey):
            P.act(lambda e: e.activation(out=junk[0:Tt, :], in_=src, func=AF.Square, accum_out=sm[0:Tt, 0:1]),
                  [srckey], ["junk", ("sm", 0)])
            rms_rstd(sm[0:Tt, 0:1], sm[0:Tt, 1:2], D, [("sm", 0)], ("sm", 1))
            P.dve(lambda e: e.scalar_tensor_tensor(out=xnb[0:Tt, :], in0=src, scalar=sm[0:Tt, 1:2], in1=gain[0:Tt, :],
                                                   op0=ALU.mult, op1=ALU.mult), [srckey, ("sm", 1), gkey], ["xnb"])
            transpose_to(xnb, ["xnb"], Tt, xT, c0, dstkey)

        def norm_group(Tt, NT, srcs, srckeys, gain, gkey):
            for i in range(NT):
                P.act(lambda e, i=i: e.activation(out=junk[0:Tt, :], in_=srcs[i], func=AF.Square, accum_out=sm[0:Tt, 76 + i:77 + i]),
                      [srckeys[i]], ["junk", ("sm", "ns", i)])
            rms_rstd(sm[0:Tt, 76:76 + NT], sm[0:Tt, 80:80 + NT], D, [("sm", "ns", i) for i in range(NT)], ("sm", "nr"))
            for i in range(NT):
                xb, xbk = (xnb, "xnb") if i % 2 == 0 else (xnb2, "xnb2")
                P.dve(lambda e, i=i, xb=xb: e.scalar_tensor_tensor(out=xb[0:Tt, :], in0=srcs[i], scalar=sm[0:Tt, 80 + i:81 + i],
                                                                   in1=gain[0:Tt, :], op0=ALU.mult, op1=ALU.mult),
                      [srckeys[i], ("sm", "nr"), gkey], [xbk])
                transpose_to(xb, [xbk], Tt, xT, i * Tt, ("xT", i))

        def resid_group(Tt, NT, xs_, xkeys, srcA, keysA, srcB, keysB, gain, gkey):
            for i in range(NT):
                for half, (src, k) in enumerate(((srcA[i], keysA[i]), (srcB[i], keysB[i]))):
                    P.act(lambda e, i=i, half=half, src=src: e.activation(
                        out=junk[0:Tt, 0:512], in_=src[0:Tt, 0:512], func=AF.Square,
                        accum_out=sm[0:Tt, 64 + 2 * i + half:65 + 2 * i + half]), [k], ["junk", ("sm", "rs", i, half)])
            P.dve(lambda e: e.tensor_reduce(out=sm[0:Tt, 72:72 + NT], in_=sm[0:Tt, 64:64 + 2 * NT].rearrange("p (i h) -> p i h", h=2),
                                            axis=AX.X, op=ALU.add), [("sm", "rs", i, h) for i in range(NT) for h in range(2)],
                  [("sm", "rr2")])
            rms_rstd(sm[0:Tt, 72:72 + NT], sm[0:Tt, 72:72 + NT], D, [("sm", "rr2")], ("sm", "rr2"))
            for i in range(NT):
                for half, (src, k) in enumerate(((srcA[i], keysA[i]), (srcB[i], keysB[i]))):
                    P.dve(lambda e, i=i, half=half, src=src: e.scalar_tensor_tensor(
                        out=tmp[0:Tt, half * 512:(half + 1) * 512], in0=src[0:Tt, 0:512], scalar=sm[0:Tt, 72 + i:73 + i],
                        in1=gain[0:Tt, half * 512:(half + 1) * 512], op0=ALU.mult, op1=ALU.mult),
                        [k, ("sm", "rr2"), gkey], [("tmp", half)])
                P.dve(lambda e, i=i: e.tensor_tensor(out=xs_[i], in0=xs_[i], in1=tmp[0:Tt, :], op=ALU.add),
                      [xkeys[i], ("tmp", 0), ("tmp", 1)], [xkeys[i]])

        def gate_stage(Tt, NT, nseq, msk, mskkey, mT, mkey, ctx):
            L = Tt // nseq
            NC = NT * nseq
            gtA_ = ctx["gt"]
            gkeys = [(ctx["gk"], i) for i in range(NT)]
            bA, bB = pb[5], pb[6]
            P.act(lambda e: e.activation(out=lfn[0:Tt, 0:NT, :], in_=gtA_[0:Tt, 0:NT, 4:8], func=AF.Exp, scale=-1.0), gkeys, ["lfn"])
            P.dve(lambda e: e.tensor_scalar(out=lfn[0:Tt, 0:NT, :], in0=lfn[0:Tt, 0:NT, :], scalar1=1.0, scalar2=None, op0=ALU.add),
                  ["lfn"], ["lfn"])
            P.act(lambda e: e.activation(out=lfn[0:Tt, 0:NT, :], in_=lfn[0:Tt, 0:NT, :], func=AF.Ln), ["lfn"], ["lfn"])
            P.pe(lambda e: e.matmul(bA[0:Tt, 0:4 * NT], lhsT=msk[0:Tt, 0, :], rhs=lfn[0:Tt, 0:NT, :].rearrange("p a b -> p (a b)"),
                                    start=True, stop=True), ["lfn", mskkey], ["pb5"])
            P.dve(lambda e: e.tensor_copy(out=bneg[0:Tt, 0:NT, :].rearrange("p a b -> p (a b)"), in_=bA[0:Tt, 0:4 * NT]), ["pb5"], ["bneg"])
            P.dve(lambda e: e.tensor_tensor(out=av[0:Tt, 0:NT, :], in0=gtA_[0:Tt, 0:NT, 0:4], in1=bneg[0:Tt, 0:NT, :], op=ALU.add),
                  gkeys + ["bneg"], ["av"])
            for t in range(NT):
                P.pe(lambda e, t=t: e.matmul(bA[0:4, t * Tt:(t + 1) * Tt], lhsT=av[0:Tt, t, :], rhs=identf[0:Tt, 0:Tt], start=True, stop=True),
                     ["av", "identf"], ["pb5"])
            for t in range(NT):
                P.pe(lambda e, t=t: e.matmul(bB[0:4, t * Tt:(t + 1) * Tt], lhsT=lfn[0:Tt, t, :], rhs=identf[0:Tt, 0:Tt], start=True, stop=True),
                     ["lfn", "identf"], ["pb6"])
            amax = tsm[:, 0:NC]
            bsn = tsm[:, 16:16 + NC]
            GT = tsm[:, 32:32 + NC]
            dT = tsm[:, 48:48 + NC]
            mseq = tsm[:, 64:64 + (NT + 1) * nseq]
            P.dve(lambda e: e.tensor_reduce(out=amax, in_=bA[0:4, 0:NT * Tt].rearrange("p (s l) -> p s l", s=NC),
                                            axis=AX.X, op=ALU.max), ["pb5"], [("tsm", "a")])
            P.dve(lambda e: e.tensor_reduce(out=bsn, in_=bB[0:4, 0:NT * Tt].rearrange("p (s l) -> p s l", s=NC),
                                            axis=AX.X, op=ALU.add), ["pb6"], [("tsm", "b")])
            P.dve(lambda e: e.tensor_copy(out=mseq[:, 0:nseq], in_=mT), [mkey], [("tsm", "m")])
            for t in range(NT):
                c0_, c1_ = t * nseq, (t + 1) * nseq
                P.dve(lambda e, c0_=c0_, c1_=c1_: e.tensor_tensor(out=GT[:, c0_:c1_], in0=amax[:, c0_:c1_], in1=mseq[:, c0_:c1_], op=ALU.max),
                      [("tsm", "a"), ("tsm", "m")], [("tsm", "g")])
                P.dve(lambda e, c0_=c0_, c1_=c1_: e.tensor_tensor(out=mseq[:, c1_:c1_ + nseq], in0=GT[:, c0_:c1_], in1=bsn[:, c0_:c1_],
                                                                 op=ALU.subtract), [("tsm", "g"), ("tsm", "b")], [("tsm", "m")])
            P.dve(lambda e: e.tensor_tensor(out=dT, in0=mseq[:, 0:NC], in1=GT, op=ALU.subtract), [("tsm", "g"), ("tsm", "m")], [("tsm", "d")])
            P.act(lambda e: e.activation(out=dT, in_=dT, func=AF.Exp), [("tsm", "d")], [("tsm", "d")])
            P.dve(lambda e: e.tensor_copy(out=mT, in_=mseq[:, NC:NC + nseq]), [("tsm", "m")], [mkey])
            if L > 4:
                P.dve(lambda e: e.tensor_copy(out=gx[:, 0, 0:NT * Tt].rearrange("p (s l) -> p s l", s=NC),
                                              in_=GT.unsqueeze(2).to_broadcast([4, NC, L])), [("tsm", "g")], [("gx", 0)])
                P.dve(lambda e: e.tensor_copy(out=gx[:, 1, 0:NT * Tt].rearrange("p (s l) -> p s l", s=NC),
                                              in_=dT.unsqueeze(2).to_broadcast([4, NC, L])), [("tsm", "d")], [("gx", 1)])
            else:
                for l in range(L):
                    P.dve(lambda e, l=l: e.tensor_copy(out=gx[:, 0, 0:NT * Tt].rearrange("p (s l) -> p s l", s=NC)[:, :, l], in_=GT),
                          [("tsm", "g")], [("gx", 0)])
                    P.dve(lambda e, l=l: e.tensor_copy(out=gx[:, 1, 0:NT * Tt].rearrange("p (s l) -> p s l", s=NC)[:, :, l], in_=dT),
                          [("tsm", "d")], [("gx", 1)])
            for t in range(NT):
                P.pe(lambda e, t=t: e.matmul(bB[0:Tt, t * 8:t * 8 + 4], lhsT=gx[:, 0, t * Tt:(t + 1) * Tt], rhs=identf[0:4, 0:4],
                                             start=True, stop=True), [("gx", 0), "identf"], ["pb6"])
                P.pe(lambda e, t=t: e.matmul(bB[0:Tt, t * 8 + 4:t * 8 + 8], lhsT=gx[:, 1, t * Tt:(t + 1) * Tt], rhs=identf[0:4, 0:4],
                                             start=True, stop=True), [("gx", 1), "identf"], ["pb6"])
            bBv = bB[0:Tt, 0:8 * NT].rearrange("p (a b) -> p a b", a=NT)
            P.dve(lambda e: e.tensor_tensor(out=tk[0:Tt, 0:NT, 0:4], in0=av[0:Tt, 0:NT, :], in1=bBv[:, :, 0:4], op=ALU.subtract),
                  ["av", "pb6"], [("tk", 0)])
            P.dve(lambda e: e.tensor_tensor(out=tk[0:Tt, 0:NT, 4:8], in0=bneg[0:Tt, 0:NT, :], in1=bBv[:, :, 0:4], op=ALU.subtract),
                  ["bneg", "pb6"], [("tk", 1)])
            P.dve(lambda e: e.tensor_copy(out=wi[0:Tt, 0:NT, :], in_=bBv[:, :, 4:8]), ["pb6"], ["wi"])
            P.act(lambda e: e.activation(out=ev[0:Tt, 0:NT, :], in_=tk[0:Tt, 0:NT, 0:4], func=AF.Exp), [("tk", 0)], ["ev"])
            P.act(lambda e: e.activation(out=fl[0:Tt, 0:NT, :], in_=tk[0:Tt, 0:NT, 4:8], func=AF.Exp), [("tk", 1)], ["fl"])
            for pr_ in range(2):
                P.dve(lambda e, pr_=pr_: e.tensor_scalar(out=rhsd[:, 0:NC, pr_], in0=dT, scalar1=sel2[:, pr_:pr_ + 1], scalar2=None,
                                                         op0=ALU.mult), [("tsm", "d"), "sel2"], ["rhsd"])
            P.pe(lambda e: e.matmul(bA[:, 0:2 * NC], lhsT=esel, rhs=rhsd[:, 0:NC, :].rearrange("p s t -> p (s t)"),
                                    start=True, stop=True), ["rhsd", "esel"], ["pb5"])
            P.dve(lambda e: e.tensor_copy(out=dstt[:, 0:NC, :].rearrange("p s t -> p (s t)"), in_=bA[:, 0:2 * NC]),
                  ["pb5"], ["dstt"])

        def ktm_stage(Tt, NT, sample, ctx):
            ktm_, ktmk = ctx["ktm"], ctx["ktmk"]
            if sample:
                P.pool(lambda e: e.memset(ktm_, 0.0), [], [ktmk])
            for t in range(NT):
                for c in range(2):
                    P.pe(lambda e, c=c, t=t: e.transpose(out=ptr[0:Tt, 2 * t + c, :], in_=kT2[:, c, t * Tt:(t + 1) * Tt], identity=ident),
                         ["kT2", "ident"], ["ptr"])
            P.act(lambda e: e.copy(out=ktm_[0:Tt, 0:NT].rearrange("p t c f -> p (t c) f"), in_=ptr[0:Tt, 0:2 * NT, :]), ["ptr"], [ktmk])

        def mlstm_tile(Tt, nseq, c0, ti, full, msk, mskkey, sample, last_tile, yt, ytkey, ctx):
            cols = slice(c0, c0 + Tt)
            IB = [1, 2]
            vk = ctx["vk"][ti]
            vaug_t = ctx["vaug"][ti]
            ktm_, ktmk = ctx["ktm"], ctx["ktmk"]
            PU = ctx["pu"]
            ev_t = ev[0:Tt, ti, :]
            wi_t = wi[0:Tt, ti, :]
            fl_t = fl[0:Tt, ti, :]
            P.dve(lambda e: e.tensor_tensor(out=evaug[0:Tt], in0=vaug_t[0:Tt],
                                            in1=ev_t.unsqueeze(2).to_broadcast([Tt, 4, 129]), op=ALU.mult),
                  [vk, "ev"], ["evaug"])
            if full:
                for h in range(4):
                    r0 = (h % 2) * 64
                    P.pe(lambda e, h=h: e.matmul(pb[0][0:Tt, h * Tt:(h + 1) * Tt], lhsT=kT2[:, h // 2, cols],
                                                 rhs=qz[:, h, cols], start=True, stop=True),
                         ["kT2", "qz"], ["pb0"])
                for h in range(4):
                    P.dve(lambda e, h=h: e.scalar_tensor_tensor(out=wT[0:Tt, h, 0:Tt], in0=pb[0][0:Tt, h * Tt:(h + 1) * Tt],
                                                                scalar=ev_t[:, h:h + 1], in1=msk[0:Tt, 0, :],
                                                                op0=ALU.mult, op1=ALU.mult), ["pb0", "ev", mskkey], ["wT"])
                for h in range(4):
                    P.pe(lambda e, h=h: e.matmul(pb[1 + h // 2][0:Tt, (h % 2) * 129:(h % 2) * 129 + 129], lhsT=wT[0:Tt, h, 0:Tt],
                                                 rhs=vaug_t[0:Tt, h, :], start=True, stop=True),
                         ["wT", vk], ["pb%d" % (1 + h // 2)])
                for pr in range(2):
                    P.act(lambda e, pr=pr: e.copy(out=tot[0:Tt, 2 * pr:2 * pr + 2, :].rearrange("p a b -> p (a b)"),
                                                  in_=pb[1 + pr][0:Tt, 0:258]), ["pb%d" % (1 + pr)], ["tot"])
                if sample:
                    for pr in range(2):
                        for hp in range(2):
                            P.dve(lambda e, pr=pr, hp=hp: e.tensor_tensor(
                                out=qm[:, hp], in0=qz[:, 2 * pr + hp, cols].unsqueeze(1).to_broadcast([128, 16, 64]),
                                in1=colsel, op=ALU.mult), ["qz", "colsel"], ["qm"])
                        for s in range(nseq):
                            cb = Cbr[s % 4]
                            cbk = "Cbr%d" % (s % 4)
                            P.act(lambda e, s=s, cb=cb: e.copy(out=cb, in_=Css[:, s]), ["Css"], [cbk])
                            for hp in range(2):
                                P.pe(lambda e, pr=pr, hp=hp, s=s, cb=cb: e.matmul(
                                    pb[IB[pr]][0:Tt, hp * 129:hp * 129 + 129], lhsT=qm[:, hp, s, :],
                                    rhs=cb[:, pr, :], start=(s == 0 and hp == 0), stop=(s == nseq - 1),
                                    skip_group_check=True),
                                    ["qm", cbk], ["pb%d" % IB[pr]])
                else:
                    for h in range(4):
                        r0 = (h % 2) * 64
                        P.pe(lambda e, h=h: e.matmul(
                            pb[IB[h // 2]][0:Tt, (h % 2) * 129:(h % 2) * 129 + 129], lhsT=qz[:, h, cols],
                            rhs=Cbf[:, h // 2, :], start=True, stop=True), ["qz", "Cbf"], ["pb%d" % IB[h // 2]])
                for h in range(4):
                    P.dve(lambda e, h=h: e.scalar_tensor_tensor(
                        out=tot[0:Tt, h, :], in0=pb[IB[h // 2]][0:Tt, (h % 2) * 129:(h % 2) * 129 + 129],
                        scalar=wi_t[:, h:h + 1], in1=tot[0:Tt, h, :], op0=ALU.mult, op1=ALU.add),
                        ["pb%d" % IB[h // 2], "wi", "tot"], ["tot"])
                dd = sm[0:Tt, 8:12]
                rr = sm[0:Tt, 12:16]
                ssq = sm[0:Tt, 16:20]
                t1 = sm[0:Tt, 20:24]
                scl = sm[0:Tt, 24:28]
                P.dve(lambda e: e.tensor_scalar(out=dd, in0=tot[0:Tt, :, 128], scalar1=-1.0, scalar2=None, op0=ALU.mult),
                      ["tot"], [("sm", "dd")])
                P.dve(lambda e: e.tensor_tensor(out=dd, in0=dd, in1=tot[0:Tt, :, 128], op=ALU.max),
                      ["tot", ("sm", "dd")], [("sm", "dd")])
                P.dve(lambda e: e.tensor_tensor(out=dd, in0=dd, in1=fl_t, op=ALU.max),
                      [("sm", "dd"), "fl"], [("sm", "dd")])
                P.dve(lambda e: e.reciprocal(out=rr, in_=dd), [("sm", "dd")], [("sm", "rr")])
                for h in range(4):
                    P.act(lambda e, h=h: e.activation(out=junk[0:Tt, 0:128], in_=tot[0:Tt, h, 0:128], func=AF.Square,
                                                      accum_out=sm[0:Tt, 16 + h:17 + h]), ["tot"], ["junk", ("sm", "ssq")])
                P.dve(lambda e: e.tensor_tensor(out=t1, in0=rr, in1=rr, op=ALU.mult), [("sm", "rr")], [("sm", "t1")])
                P.dve(lambda e: e.tensor_tensor(out=t1, in0=t1, in1=ssq, op=ALU.mult), [("sm", "t1"), ("sm", "ssq")], [("sm", "t1")])
                rms_rstd(t1, t1, 128, [("sm", "t1")], ("sm", "t1"))
                P.dve(lambda e: e.tensor_tensor(out=scl, in0=t1, in1=rr, op=ALU.mult), [("sm", "t1"), ("sm", "rr")], [("sm", "scl")])
                P.dve(lambda e: e.scalar_tensor_tensor(out=gs[0:Tt, :], in0=sig[ti][0:Tt, :], scalar=1.0, in1=g_ml[0:Tt, :],
                                                       op0=ALU.add, op1=ALU.mult), ["sig%d" % ti, "g_ml"], ["gs"])
                for h in range(4):
                    P.dve(lambda e, h=h: e.scalar_tensor_tensor(
                        out=yt[0:Tt, h * 128:(h + 1) * 128], in0=tot[0:Tt, h, 0:128], scalar=sm[0:Tt, 24 + h:25 + h],
                        in1=gs[0:Tt, h * 128:(h + 1) * 128], op0=ALU.mult, op1=ALU.mult),
                        ["tot", ("sm", "scl"), "gs"], [(ytkey, 0)])
            for s in range(nseq):
                if sample:
                    P.dve(lambda e, s=s: e.tensor_scalar(out=evm[0:Tt], in0=evaug[0:Tt], scalar1=rowsel[:, s:s + 1], scalar2=None,
                                                         op0=ALU.mult), ["evaug", "rowsel"], ["evm"])
                    rsrc, rkey = evm, "evm"
                else:
                    rsrc, rkey = evaug, "evaug"
                for pr in range(2):
                    P.pe(lambda e, pr=pr, rsrc=rsrc: e.matmul(pb[PU][:, 0:258], lhsT=ktm_[:, ti, pr, :],
                                                             rhs=rsrc[:, 2 * pr:2 * pr + 2, :].rearrange("p a b -> p (a b)"),
                                                             start=True, stop=True), [ktmk, rkey], ["pb%d" % PU])
                    for hp in range(2):
                        r0 = hp * 64
                        if sample:
                            cs = Css[r0:r0 + 64, s, pr, :]
                            ckey = "Css"
                        else:
                            cs = Cst[r0:r0 + 64, pr, :]
                            ckey = "Cst"
                        P.dve(lambda e, cs=cs, r0=r0, hp=hp, s=s, pr=pr: e.scalar_tensor_tensor(
                            out=cs, in0=cs, scalar=dstt[r0:r0 + 64, ti * nseq + s, pr:pr + 1], in1=pb[PU][r0:r0 + 64, hp * 129:hp * 129 + 129],
                            op0=ALU.mult, op1=ALU.add), [ckey, "dstt", "pb%d" % PU], [ckey])
            if (not sample) and (not last_tile) and ctx.get("need_cbf", True):
                P.act(lambda e: e.copy(out=Cbf, in_=Cst), ["Cst"], ["Cbf"])

        def attn_tile(Tt, c0, sample, kprev, kprev_key, kcur, kcur_key, vprev, vprev_key, vcur, vcur_key, tbl, tblkey,
                      first_tile, yt, ytkey, split=False):
            NK = 128 + Tt
            cols = slice(c0, c0 + Tt)
            RES0 = {"sb": (3, 4), "tr": ptr, "trk": "ptr", "pv": 3, "sc": sc, "sck": "sc", "pex": pex, "pexk": "pex",
                    "pT": pT, "pTk": "pT", "smo": 0, "smt": "k0"}
            RES1 = {"sb": (5, 6), "tr": pb[6][:, :].bitcast(BF16).rearrange("p (a b) -> p a b", a=8), "trk": "pb6", "pv": 5,
                    "sc": sc2, "sck": "sc2", "pex": pex2, "pexk": "pex2", "pT": pT2, "pTk": "pT2", "smo": 52, "smt": "k1"}

            def do_kv(kv, res):
                r0 = kv * 64
                SB = res["sb"]
                sc_, sck, pex_, pexk, pT_, pTk = res["sc"], res["sck"], res["pex"], res["pexk"], res["pT"], res["pTk"]
                trv, trk, PV, so, smt = res["tr"], res["trk"], res["pv"], res["smo"], res["smt"]
                for j in range(4):
                    bank = pb[SB[j // 2]]
                    o0 = (j % 2) * 256
                    bkey = "pb%d" % SB[j // 2]
                    if sample:
                        qa = qam[j % 2]
                        qak = "qam%d" % (j % 2)
                        P.pool(lambda e, qa=qa: e.memset(qa[(1 - kv) * 64:(2 - kv) * 64], 0.0), [], [qak])
                        P.dve(lambda e, j=j, qa=qa: e.tensor_tensor(
                            out=qa[r0:r0 + 64], in0=qaT[r0:r0 + 64, j, cols].unsqueeze(1).to_broadcast([64, 16, 64]),
                            in1=colsel[r0:r0 + 64], op=ALU.mult), ["qaT", "colsel"], [qak])
                        for s in range(16):
                            P.pe(lambda e, s=s, bank=bank, o0=o0, qa=qa: e.matmul(
                                bank[0:Tt, o0:o0 + 128], lhsT=qa[:, s, :], rhs=kcT[:, s, :],
                                start=(s == 0), stop=(s == 15)), [qak, "kcT"], [bkey])
                    else:
                        P.pe(lambda e, j=j, bank=bank, o0=o0: e.matmul(bank[0:Tt, o0:o0 + 128], lhsT=qaT[:, j, cols],
                                                                       rhs=kprev[kv], start=True, stop=True),
                             ["qaT", kprev_key[kv]], [bkey])
                    P.pe(lambda e, j=j, bank=bank, o0=o0: e.matmul(bank[0:Tt, o0 + 128:o0 + 128 + Tt], lhsT=qaT[:, j, cols],
                                                                   rhs=kcur[kv], start=True, stop=True),
                         ["qaT", kcur_key[kv]], [bkey])
                for half in range(2):
                    P.dve(lambda e, half=half: e.tensor_tensor(
                        out=sc_[0:Tt, 2 * half:2 * half + 2, 0:NK],
                        in0=pb[SB[half]][0:Tt, :].rearrange("p (a b) -> p a b", a=2)[:, :, 0:NK],
                        in1=tbl[0:Tt, 4 * kv + 2 * half:4 * kv + 2 * half + 2, 0:NK], op=ALU.add),
                        ["pb%d" % SB[half], tblkey], [sck])
                if first_tile:
                    P.dve(lambda e: e.tensor_scalar(out=sc_[0:Tt, :, 0:128], in0=sc_[0:Tt, :, 0:128], scalar1=prevb[0:Tt, 0:1],
                                                    scalar2=None, op0=ALU.add), [sck, "prevb"], [sck])
                mx = sm[0:Tt, 32 + so:36 + so]
                nmx = sm[0:Tt, 36 + so:40 + so]
                ssum = sm[0:Tt, 40 + so:44 + so]
                es = sm[0:Tt, 44 + so:48 + so]
                P.dve(lambda e: e.tensor_reduce(out=mx, in_=sc_[0:Tt, :, 0:NK], axis=AX.X, op=ALU.max), [sck], [("sm", "mx", smt)])
                P.dve(lambda e: e.tensor_tensor(out=mx, in0=mx, in1=sinks[0:Tt, 4 * kv:4 * kv + 4], op=ALU.max),
                      [("sm", "mx", smt), "sinks"], [("sm", "mx", smt)])
                P.dve(lambda e: e.tensor_scalar(out=nmx, in0=mx, scalar1=-1.0, scalar2=None, op0=ALU.mult),
                      [("sm", "mx", smt)], [("sm", "nmx", smt)])
                for j in range(4):
                    P.act(lambda e, j=j: e.activation(out=pex_[0:Tt, j, 0:NK], in_=sc_[0:Tt, j, 0:NK], func=AF.Exp,
                                                      bias=sm[0:Tt, 36 + so + j:37 + so + j], accum_out=sm[0:Tt, 40 + so + j:41 + so + j]),
                          [sck, ("sm", "nmx", smt)], [pexk, ("sm", "ssum", smt)])
                P.dve(lambda e: e.tensor_tensor(out=es, in0=sinks[0:Tt, 4 * kv:4 * kv + 4], in1=nmx, op=ALU.add),
                      ["sinks", ("sm", "nmx", smt)], [("sm", "es", smt)])
                P.act(lambda e: e.activation(out=es, in_=es, func=AF.Exp), [("sm", "es", smt)], [("sm", "es", smt)])
                P.dve(lambda e: e.tensor_tensor(out=ssum, in0=ssum, in1=es, op=ALU.add), [("sm", "ssum", smt), ("sm", "es", smt)],
                      [("sm", "ssum", smt)])
                P.dve(lambda e: e.reciprocal(out=ssum, in_=ssum), [("sm", "ssum", smt)], [("sm", "ssum", smt)])
                for j in range(4):
                    P.pe(lambda e, j=j: e.transpose(out=trv[:, 2 * j, 0:Tt], in_=pex_[0:Tt, j, 0:128], identity=ident[0:Tt, 0:Tt]),
                         [pexk, "ident"], [trk])
                    P.pe(lambda e, j=j: e.transpose(out=trv[0:Tt, 2 * j + 1, 0:Tt], in_=pex_[0:Tt, j, 128:128 + Tt],
                                                    identity=ident[0:Tt, 0:Tt]), [pexk, "ident"], [trk])
                P.act(lambda e: e.copy(out=pT_[:, :, 0:Tt], in_=trv[:, :, 0:Tt]), [trk], [pTk])
                for j in range(4):
                    oap = pb[PV][0:Tt, j * 64:(j + 1) * 64]
                    if sample:
                        P.dve(lambda e, j=j: e.tensor_tensor(out=pTm, in0=pT_[:, 2 * j, 0:64].unsqueeze(1).to_broadcast([128, 16, 64]),
                                                             in1=colsel, op=ALU.mult), [pTk, "colsel"], ["pTm"])
                        for s in range(16):
                            P.pe(lambda e, s=s, oap=oap: e.matmul(oap, lhsT=pTm[:, s, :], rhs=vcb[:, s, r0:r0 + 64],
                                                                  start=(s == 0), stop=False), ["pTm", "vcb"], ["pb%d" % PV])
                    else:
                        P.pe(lambda e, j=j, oap=oap: e.matmul(oap, lhsT=pT_[:, 2 * j, 0:Tt], rhs=vprev[:, kv, :], start=True, stop=False),
                             [pTk, vprev_key], ["pb%d" % PV])
                    P.pe(lambda e, j=j, oap=oap: e.matmul(oap, lhsT=pT_[0:Tt, 2 * j + 1, 0:Tt], rhs=vcur[0:Tt, kv, :],
                                                          start=False, stop=True), [pTk, vcur_key], ["pb%d" % PV])
                P.dve(lambda e: e.tensor_tensor(out=att[0:Tt, kv * 256:(kv + 1) * 256].rearrange("p (a b) -> p a b", a=4),
                                                in0=pb[PV][0:Tt, 0:256].rearrange("p (a b) -> p a b", a=4),
                                                in1=ssum.unsqueeze(2).to_broadcast([Tt, 4, 64]), op=ALU.mult),
                      ["pb%d" % PV, ("sm", "ssum", smt)], [("att", kv)])

            def tail():
                P.act(lambda e: e.activation(out=junk[0:Tt, 0:512], in_=att[0:Tt, :], func=AF.Square, accum_out=sm[0:Tt, 48:49]),
                      [("att", 0), ("att", 1)], ["junk", ("sm", "a0")])
                rms_rstd(sm[0:Tt, 48:49], sm[0:Tt, 49:50], 512, [("sm", "a0")], ("sm", "a1"))
                P.dve(lambda e: e.scalar_tensor_tensor(out=yt[0:Tt, 512:1024], in0=att[0:Tt, :], scalar=sm[0:Tt, 49:50],
                                                       in1=g_at[0:Tt, :], op0=ALU.mult, op1=ALU.mult),
                      [("att", 0), ("att", 1), ("sm", "a1"), "g_at"], [(ytkey, 1)])

            if split:
                return (lambda: do_kv(0, RES0)), (lambda: do_kv(1, RES1)), tail
            do_kv(0, RES0)
            do_kv(1, RES0)
            tail()
            return None

        def resid_norm_add(Tt, x, xkey, srcs, skeys, gain, gkey):
            for half in range(2):
                P.act(lambda e, half=half: e.activation(out=junk[0:Tt, 0:512], in_=srcs[half][0:Tt, 0:512], func=AF.Square,
                                                        accum_out=sm[0:Tt, 52 + half:53 + half]), [skeys[half]],
                      ["junk", ("sm", "r%d" % half)])
            P.dve(lambda e: e.tensor_tensor(out=sm[0:Tt, 54:55], in0=sm[0:Tt, 52:53], in1=sm[0:Tt, 53:54], op=ALU.add),
                  [("sm", "r0"), ("sm", "r1")], [("sm", "r2")])
            rms_rstd(sm[0:Tt, 54:55], sm[0:Tt, 55:56], D, [("sm", "r2")], ("sm", "r3"))
            for half in range(2):
                P.dve(lambda e, half=half: e.scalar_tensor_tensor(
                    out=tmp[0:Tt, half * 512:(half + 1) * 512], in0=srcs[half][0:Tt, 0:512], scalar=sm[0:Tt, 55:56],
                    in1=gain[0:Tt, half * 512:(half + 1) * 512], op0=ALU.mult, op1=ALU.mult),
                    [skeys[half], ("sm", "r3"), gkey], [("tmp", half)])
            P.dve(lambda e: e.tensor_tensor(out=x, in0=x, in1=tmp[0:Tt, :], op=ALU.add), [xkey, ("tmp", 0), ("tmp", 1)], [xkey])

        def wout_group(Tt, NT, xs_, xkeys):
            for i in range(NT):
                transpose_to(ytm[i], [("ytm%d" % i, 0), ("ytm%d" % i, 1)], Tt, xT, i * Tt, ("xT", i))
            banks0 = [(pb[i], "pb%d" % i) for i in range(4)]
            banks1 = [(pb[4], "pb4"), (pb[5], "pb5"), (pb[6], "pb6"), (pb[0], "pb0")]
            for half, banks in ((0, banks0), (1, banks1)):
                b = next_wu(w_out_v[:, :, half * 512:(half + 1) * 512], ("wo", half))
                for i in range(NT):
                    bk, bkey = banks[i]
                    for kc in range(8):
                        P.pe(lambda e, i=i, kc=kc, b=b, bk=bk: e.matmul(bk[0:Tt, :], lhsT=xT[:, kc, i * Tt:(i + 1) * Tt],
                                                                       rhs=wu[b][:, kc, :], start=(kc == 0), stop=(kc == 7)),
                             [("xT", i)] + WUK(b), [bkey])
                    if half == 0:
                        P.act(lambda e, i=i, bk=bk: e.copy(out=f1[0:Tt, i, :], in_=bk[0:Tt, :]), [bkey], [("f1", i)])
            resid_group(Tt, NT, xs_, xkeys, [f1[:, i, :] for i in range(NT)], [("f1", i) for i in range(NT)],
                        [banks1[i][0] for i in range(NT)], [banks1[i][1] for i in range(NT)], g_post, "g_post")

        def ffn_group(Tt, NT, xs_, xkeys):
            GN = Tt * NT
            norm_group(Tt, NT, xs_, xkeys, g_fpre, "g_fpre")
            xTkeys = [("xT", i) for i in range(NT)]
            fbank = [pb[0], pb[1], pb[2], pb[3]]
            fkeys = ["pb0", "pb1", "pb2", "pb3"]

            def f_mm(j, wslot, banks, keys):
                wbuf, wkey = wslot
                for i in range(NT):
                    P.pe(lambda e, j=j, i=i, wbuf=wbuf: e.matmul(banks[i][0:Tt, :], lhsT=hT[:, j, i * Tt:(i + 1) * Tt],
                                                                 rhs=wbuf[:, j % 4, :], start=(j == 0), stop=(j == 31),
                                                                 skip_group_check=True),
                         [("hT", j), wkey], [keys[i]])

            pending = None
            for s in range(8):
                b = next_wu(w_up_v[:, :, s * 512:(s + 1) * 512], ("up", s))
                bd = s % 2
                load_wd(wd[bd], ("wd", bd), w_dn_v[:, 4 * s:4 * s + 4, 0:512], s)
                for jc in range(4):
                    j = 4 * s + jc
                    hb = pb[5 + (jc % 2)]
                    hk = "pb%d" % (5 + (jc % 2))
                    for kc in range(8):
                        P.pe(lambda e, jc=jc, kc=kc, b=b, hb=hb: e.matmul(hb[:, 0:GN], lhsT=wu[b][:, kc, jc * 128:(jc + 1) * 128],
                                                                          rhs=xT[:, kc, 0:GN], start=(kc == 0), stop=(kc == 7)),
                             xTkeys + WUK(b), [hk])
                    if pending is not None:
                        f_mm(pending[0], pending[1], fbank, fkeys)
                    P.act(lambda e, hb=hb: e.activation(out=hr[:, 0:GN], in_=hb[:, 0:GN], func=AF.Relu), [hk], ["hr"])
                    P.dve(lambda e, j=j: e.tensor_tensor(out=hT[:, j, 0:GN], in0=hr[:, 0:GN], in1=hr[:, 0:GN], op=ALU.mult),
                          ["hr"], [("hT", j)])
                    pending = (j, (wd[bd], ("wd", bd)))
            f_mm(pending[0], pending[1], fbank, fkeys)
            for i in range(NT):
                P.act(lambda e, i=i: e.copy(out=f1[0:Tt, i, :], in_=fbank[i][0:Tt, :]), [fkeys[i]], [("f1", i)])
            f2bank = [pb[4], pb[5], pb[6], pb[0]]
            f2keys = ["pb4", "pb5", "pb6", "pb0"]
            slots2 = [(wd[0], ("wd", 0)), (wd[1], ("wd", 1)), (wu[0][:, 0:4, :], ("wu", 0, 0)), (wu[0][:, 4:8, :], ("wu", 0, 1)),
                      (wu[1][:, 0:4, :], ("wu", 1, 0)), (wu[1][:, 4:8, :], ("wu", 1, 1))]
            for s in range(8):
                wbuf, wkey = slots2[s % 6]
                load_wd(wbuf, wkey, w_dn_v[:, 4 * s:4 * s + 4, 512:1024], 8 + s)
                for jc in range(4):
                    f_mm(4 * s + jc, (wbuf, wkey), f2bank, f2keys)
            resid_group(Tt, NT, xs_, xkeys, [f1[:, i, :] for i in range(NT)], [("f1", i) for i in range(NT)],
                        f2bank[0:NT], f2keys[0:NT], g_fpost, "g_fpost")

        def pgate_group(Tt, NT, xs_, xkeys, p_srcs, y_dsts):
            for i in range(NT):
                P.dma("pool", lambda e, i=i: e.dma_start(out=ptb[0:Tt, i, :], in_=p_srcs[i]), [], [("ptb", i)])
                P.dve(lambda e, i=i: e.tensor_copy(out=xnb[0:Tt, :], in_=xs_[i]), [xkeys[i]], ["xnb"])
                transpose_to(xnb, ["xnb"], Tt, xT, i * Tt, ("xT", i))
                transpose_to(ptb[:, i, :], [("ptb", i)], Tt, ppT, i * Tt, ("ppT", i), nk=2)
            for half in range(2):
                b = next_wu(w_pg_v[:, :, half * 512:(half + 1) * 512], ("pg", half))
                for i in range(NT):
                    gb, gk_ = pb[i % 2], "pb%d" % (i % 2)
                    qb, qk_ = pb[2 + i % 2], "pb%d" % (2 + i % 2)
                    for kc in range(8):
                        P.pe(lambda e, i=i, kc=kc, b=b, gb=gb: e.matmul(gb[0:Tt, :], lhsT=xT[:, kc, i * Tt:(i + 1) * Tt],
                                                                       rhs=wu[b][:, kc, :], start=(kc == 0), stop=(kc == 7)),
                             [("xT", i)] + WUK(b), [gk_])
                    for kc in range(2):
                        P.pe(lambda e, i=i, kc=kc, half=half, qb=qb: e.matmul(qb[0:Tt, :], lhsT=ppT[:, kc, i * Tt:(i + 1) * Tt],
                                                                             rhs=w_pp_sb[:, kc, half * 512:(half + 1) * 512],
                                                                             start=(kc == 0), stop=(kc == 1)),
                             [("ppT", i), "w_pp"], [qk_])
                    th = i % 2
                    tv = tmp[0:Tt, th * 512:(th + 1) * 512]
                    P.act(lambda e, gb=gb, tv=tv: e.activation(out=tv, in_=gb[0:Tt, :], func=AF.Tanh, scale=0.5), [gk_], [("tmp", th)])
                    P.dve(lambda e, qb=qb, tv=tv: e.scalar_tensor_tensor(out=tv, in0=tv, scalar=1.0, in1=qb[0:Tt, :],
                                                                         op0=ALU.add, op1=ALU.mult),
                          [("tmp", th), qk_], [("tmp", th)])
                    xv = xs_[i][:, half * 512:(half + 1) * 512]
                    P.dve(lambda e, xv=xv, tv=tv: e.scalar_tensor_tensor(out=xv, in0=tv, scalar=0.5, in1=xv, op0=ALU.mult, op1=ALU.add),
                          [xkeys[i], ("tmp", th)], [xkeys[i]])
            for i in range(NT):
                P.dma("sp", lambda e, i=i: e.dma_start(out=y_dsts[i], in_=xs_[i]), [xkeys[i]], [])

        def win_group(Tt, NT, chunks, tok_tiles, sample, kslot, vslots, ctx):
            GN = Tt * NT
            xTkeys = [("xT", i) for i in range(NT)]
            for n, c in enumerate(chunks):
                bank = pb[n % 2]
                bkey = "pb%d" % (n % 2)
                if c < 2:
                    wsrc, c0, wkeys = w_rest, c * 128, W_REST_KEYS
                elif c < 4:
                    wsrc, c0, wkeys = w_res, (c - 2) * 128, W_RES_KEYS
                elif c < 8:
                    wsrc, c0, wkeys = w_rest, 768 + (c - 4) * 128, W_REST_KEYS
                else:
                    wsrc, c0, wkeys = w_rest, 1280, W_REST_KEYS
                for kc in range(8):
                    P.pe(lambda e, kc=kc, c0=c0, bank=bank, wsrc=wsrc: e.matmul(bank[:, 0:GN], lhsT=wsrc[:, kc, c0:c0 + 128],
                                                                                 rhs=xT[:, kc, 0:GN], start=(kc == 0), stop=(kc == 7)),
                         xTkeys + wkeys, [bkey])
                if c < 4:
                    if sample:
                        P.act(lambda e, c=c, bank=bank: e.copy(out=raws[:, c, :, 3:7], in_=bank[:, 0:64].rearrange("p (s l) -> p s l", s=16)),
                              [bkey], ["raws"])
                    else:
                        P.act(lambda e, c=c, bank=bank: e.copy(out=raw[:, c, 3:515], in_=bank[:, 0:512]), [bkey], ["raw"])
                elif c < 8:
                    P.act(lambda e, c=c, bank=bank: e.activation(out=qaT[:, c - 4, 0:GN], in_=bank[:, 0:GN], func=AF.Copy, scale=0.125),
                          [bkey], ["qaT"])
                else:
                    for kv in range(2):
                        P.act(lambda e, bank=bank, kv=kv: e.copy(out=kaT[kslot][kv][kv * 64:(kv + 1) * 64, 0:GN],
                                                                 in_=bank[kv * 64:(kv + 1) * 64, 0:GN]), [bkey], ["kaT%d_%d" % (kslot, kv)])
            for i in range(NT):
                parts = tok_tiles.get(i, ())
                xk = [("xT", i)]
                if "v" in parts:
                    for kc in range(8):
                        P.pe(lambda e, kc=kc, i=i: e.matmul(pb[2][0:Tt, :], lhsT=xT[:, kc, i * Tt:(i + 1) * Tt], rhs=w_res[:, kc, 256:768],
                                                            start=(kc == 0), stop=(kc == 7)), xk + W_RES_KEYS, ["pb2"])
                    P.act(lambda e, i=i: e.copy(out=ctx["vaug"][i][0:Tt, :, 0:128], in_=pb[2][0:Tt, :].rearrange("p (h v) -> p h v", h=4)),
                          ["pb2"], [ctx["vk"][i]])
                    P.pool(lambda e, i=i: e.memset(ctx["vaug"][i][0:Tt, :, 128:129], 1.0), [], [ctx["vk"][i]])
                if "o" in parts:
                    for kc in range(8):
                        P.pe(lambda e, kc=kc, i=i: e.matmul(pb[3][0:Tt, :], lhsT=xT[:, kc, i * Tt:(i + 1) * Tt], rhs=w_rest[:, kc, 256:768],
                                                            start=(kc == 0), stop=(kc == 7)), xk + W_REST_KEYS, ["pb3"])
                    P.act(lambda e, i=i: e.activation(out=sig[i][0:Tt, :], in_=pb[3][0:Tt, :], func=AF.Tanh, scale=0.5), ["pb3"], ["sig%d" % i])
                if "g" in parts:
                    for kc in range(8):
                        P.pe(lambda e, kc=kc, i=i: e.matmul(pb[4][0:Tt, 0:8], lhsT=xT[:, kc, i * Tt:(i + 1) * Tt], rhs=w_res[:, kc, 768:776],
                                                            start=(kc == 0), stop=(kc == 7)), xk + W_RES_KEYS, ["pb4"])
                    P.dve(lambda e, i=i: e.tensor_tensor(out=ctx["gt"][0:Tt, i, :], in0=pb[4][0:Tt, 0:8], in1=bgate[0:Tt, :], op=ALU.add),
                          ["pb4", "bgate"], [(ctx["gk"], i)])
                if "kv" in parts:
                    vs = vslots[i]
                    for kc in range(8):
                        P.pe(lambda e, kc=kc, i=i: e.matmul(pb[2][0:Tt, 0:256], lhsT=xT[:, kc, i * Tt:(i + 1) * Tt],
                                                            rhs=w_rest[:, kc, 1280:1536], start=(kc == 0), stop=(kc == 7)),
                             xk + W_REST_KEYS, ["pb2"])
                    P.act(lambda e, vs=vs: e.copy(out=va[vs][0:Tt, :, :], in_=pb[2][0:Tt, 128:256].rearrange("p (k d) -> p k d", k=2)),
                          ["pb2"], ["va%d" % vs])
                    if "kvout" in parts:
                        P.dve(lambda e: e.tensor_copy(out=kvout[0:Tt, :], in_=pb[2][0:Tt, 0:256]), ["pb2"], ["kvout"])

        def conv_group(chunks, sample, GN):
            for c in chunks:
                if sample:
                    srcs = [raws[:, c, :, j:j + 4] for j in range(4)]
                    accv = cacc[:, 0:64].rearrange("p (s l) -> p s l", s=16)
                    rk = "raws"
                else:
                    srcs = [raw[:, c, j:j + 512] for j in range(4)]
                    accv = cacc[:, 0:512]
                    rk = "raw"
                P.dve(lambda e, c=c, accv=accv, srcs=srcs: e.tensor_scalar(out=accv, in0=srcs[0], scalar1=cw[:, c, 0:1], scalar2=None,
                                                                          op0=ALU.mult), [rk, "cw"], ["cacc"])
                for j in range(1, 4):
                    P.dve(lambda e, c=c, j=j, accv=accv, srcs=srcs: e.scalar_tensor_tensor(
                        out=accv, in0=srcs[j], scalar=cw[:, c, j:j + 1], in1=accv, op0=ALU.mult, op1=ALU.add),
                        [rk, "cw", "cacc"], ["cacc"])
                tsc = 8.0 if c < 2 else 1.0
                P.act(lambda e, tsc=tsc: e.activation(out=csg[:, 0:GN], in_=cacc[:, 0:GN], func=AF.Tanh, scale=tsc), ["cacc"], ["csg"])
                if c < 2:
                    for hp in range(2):
                        r0 = hp * 64
                        P.pool(lambda e, c=c, hp=hp: e.memset(qz[(1 - hp) * 64:(2 - hp) * 64, 2 * c + hp, 0:GN], 0.0), [], ["qz"])
                        P.dve(lambda e, c=c, hp=hp, r0=r0: e.scalar_tensor_tensor(
                            out=qz[r0:r0 + 64, 2 * c + hp, 0:GN], in0=csg[r0:r0 + 64, 0:GN], scalar=1.0, in1=cacc[r0:r0 + 64, 0:GN],
                            op0=ALU.add, op1=ALU.mult), ["cacc", "csg"], ["qz"])
                else:
                    P.dve(lambda e, c=c: e.scalar_tensor_tensor(out=kT2[:, c - 2, 0:GN], in0=csg[:, 0:GN], scalar=1.0, in1=cacc[:, 0:GN],
                                                                op0=ALU.add, op1=ALU.mult), ["cacc", "csg"], ["kT2"])

        CTX_A = {"vaug": vaug, "vk": ["vaug%d" % i for i in range(4)], "gt": gtA, "gk": "gtA", "ktm": ktm, "ktmk": "ktm", "pu": 1}
        CTX_B = {"vaug": vaugB, "vk": ["vaugB%d" % i for i in range(4)], "gt": gtB, "gk": "gtB", "ktm": ktmB, "ktmk": "ktmB", "pu": 3}

        def plain_prefix(g):
            return g < n_prefix - 1

        def group_ctx(g):
            if plain_prefix(g):
                c = dict(CTX_A if g % 2 == 0 else CTX_B)
                c["pu"] = 3
                c["need_cbf"] = False
                return c
            return CTX_A

        def xset(g):
            if g % 2 == 0:
                return xt, ["xt%d" % i for i in range(4)]
            return xtB, ["xtB%d" % i for i in range(4)]

        def load_x(g):
            X, XK = xset(g)
            for i in range(4):
                P.dma("sp", lambda e, g=g, i=i, X=X: e.dma_start(out=X[i], in_=xs_d[(4 * g + i) * 128:(4 * g + i + 1) * 128, :]),
                      [], [XK[i]])

        def front(g):
            ctx = group_ctx(g)
            full = g >= n_prefix
            lastpre = (g == n_prefix - 1)
            lastg = (g == n_groups - 1)
            if (full or lastpre) and not w_rest_state.get("prefetched", False):
                load_w_rest()
            w_rest_state["prefetched"] = False
            X, XK = xset(g)
            if g == 0:
                load_x(0)
            if g + 1 < n_groups:
                load_x(g + 1)
            norm_group(128, 4, X, XK, g_pre, "g_pre")
            kslot = g % 2
            vslots = {i: (4 * g + i) % 8 for i in range(4)}
            if full:
                chunks = [0, 1, 2, 3, 4, 5, 6, 7, 8]
                toks = {i: {"v", "o", "g", "kv"} for i in range(4)}
                if lastg:
                    toks[3] = toks[3] | {"kvout"}
            elif lastpre:
                chunks = [0, 1, 2, 3, 8]
                toks = {i: {"v", "g"} for i in range(4)}
                toks[3] = {"v", "g", "kv"}
            else:
                chunks = [2, 3]
                toks = {i: {"v", "g"} for i in range(4)}
            win_group(128, 4, chunks, toks, False, kslot, vslots, ctx)

            def conv_part():
                conv_group([0, 1, 2, 3] if full else [2, 3], False, 512)
                if lastg:
                    P.dve(lambda e: e.tensor_copy(out=cvoP, in_=raw[:, :, 512:515]), ["raw"], ["cvoP"])
                    for c in range(4):
                        P.pe(lambda e, c=c: e.matmul(pb[2][0:3, c * 128:(c + 1) * 128], lhsT=cvoP[:, c, :], rhs=identf,
                                                     start=True, stop=True), ["cvoP", "identf"], ["pb2"])
                    P.dve(lambda e: e.tensor_copy(out=cvtP, in_=pb[2][0:3, :]), ["pb2"], ["cvtP"])
                    P.dma("sp", lambda e: e.dma_start(out=conv_p_d, in_=cvtP), ["cvtP"], [])
                P.dve(lambda e: e.tensor_copy(out=raw[:, :, 0:3], in_=raw[:, :, 512:515]), ["raw"], ["raw"])
                ktm_stage(128, 4, False, ctx)

            if full or lastpre:
                interleave(capture(conv_part), capture(lambda: gate_stage(128, 4, 1, mskp, "mskp", mTp[:, 0:1], "mT", ctx)))
            else:
                conv_part()

        def back(g):
            ctx = group_ctx(g)
            full = g >= n_prefix
            lastg = (g == n_groups - 1)
            kslot = g % 2
            if plain_prefix(g):
                gate_stage(128, 4, 1, mskp, "mskp", mTp[:, 0:1], "mT", ctx)
            for i in range(4):
                last_tile = lastg and i == 3
                if not full:
                    mlstm_tile(128, 1, i * 128, i, full, mskp, "mskp", False, last_tile, ytm[i], "ytm%d" % i, ctx)
                    continue
                vs_cur = (4 * g + i) % 8
                vs_prev = (4 * g + i - 1) % 8
                if i == 0:
                    kprev = [kaT[1 - kslot][kv][:, 384:512] for kv in range(2)]
                    kpk = ["kaT%d_%d" % (1 - kslot, kv) for kv in range(2)]
                else:
                    kprev = [kaT[kslot][kv][:, (i - 1) * 128:i * 128] for kv in range(2)]
                    kpk = ["kaT%d_%d" % (kslot, kv) for kv in range(2)]
                bufA = capture(lambda: mlstm_tile(128, 1, i * 128, i, full, mskp, "mskp", False, last_tile, ytm[i], "ytm%d" % i, ctx))
                kv0, kv1, atail = attn_tile(128, i * 128, False, kprev, kpk,
                                            [kaT[kslot][kv][:, i * 128:(i + 1) * 128] for kv in range(2)],
                                            ["kaT%d_%d" % (kslot, kv) for kv in range(2)],
                                            va[vs_prev], "va%d" % vs_prev, va[vs_cur], "va%d" % vs_cur, abias, "abias",
                                            first_tile=(g == n_prefix and i == 0), yt=ytm[i], ytkey="ytm%d" % i, split=True)
                interleave(bufA, capture(kv0), capture(kv1))
                atail()
            if full:
                X, xkeys = xset(g)
                WMODE["save"] = bool(lastg and do_sample)
                wout_group(128, 4, X, xkeys)
                ffn_group(128, 4, X, xkeys)
                if (not lastg) or do_sample:
                    load_w_rest()
                    w_rest_state["prefetched"] = True
                rows = [((g - n_prefix) * 4 + i) * 128 for i in range(4)]
                pgate_group(128, 4, X, xkeys, [pp_d[r:r + 128, :] for r in rows], [y_d[r:r + 128, :] for r in rows])
            if lastg:
                P.dma("sp", lambda e: e.dma_start(out=k_p_d, in_=kvout[:, 0:128]), ["kvout"], [])
                P.dma("sp", lambda e: e.dma_start(out=v_p_d, in_=kvout[:, 128:256]), ["kvout"], [])
                for hp in range(2):
                    P.dma("sp", lambda e, hp=hp: e.dma_start(
                        out=c_p_d.rearrange("(pr hp) d v -> hp d pr v", hp=2)[hp], in_=Cst[hp * 64:(hp + 1) * 64, :, 0:128]),
                        ["Cst"], [])
                    P.dma("sp", lambda e, hp=hp: e.dma_start(
                        out=n_p_d.rearrange("(pr hp) d -> hp d pr", hp=2)[hp], in_=Cst[hp * 64:(hp + 1) * 64, :, 128],
                        allow_slow_non_contiguous=True), ["Cst"], [])
                P.dma("sp", lambda e: e.dma_start(out=m_p_d, in_=mTp[:, 0:1]), ["mT"], [])

        late_scr = n_prefix > 3
        if not late_scr:
            emit_w_rest_scr()
        if n_groups > 0:
            front(0)
        for g in range(n_groups):
            nxt = g + 1
            if late_scr and g == 1:
                emit_w_rest_scr()
            if nxt < n_groups and plain_prefix(g) and plain_prefix(nxt):
                interleave(capture(lambda: back(g)), capture(lambda: front(nxt)))
            else:
                back(g)
                if nxt < n_groups:
                    front(nxt)

        if do_sample:
            xs0 = xt0[0:64, :]
            ld("sp", msks, cst["c_msks"], "msks")
            ld("sp", sbias, cst["c_sbias"], "sbias")
            ld("sp", rowsel, cst["c_rowsel"], "rowsel")
            ld("pool", colsel, cst["c_colsel"], "colsel")
            P.pool(lambda e: e.memset(evm, 0.0), [], ["evm"])
            if not w_rest_state.get("prefetched", False):
                load_w_rest()
            w_rest_state["prefetched"] = False
            P.dma("sp", lambda e: e.dma_start(out=xs0, in_=xsm_d), [], ["xt0"])
            P.dma("sp", lambda e: e.dma_start(out=hist, in_=stv_d), [], ["hist"])
            P.dma("sp", lambda e: e.dma_start(out=mTs, in_=stm_d.rearrange("s h -> h s"), allow_slow_non_contiguous=True),
                  [], ["mTs"])
            for hp in range(2):
                P.dma("sp", lambda e, hp=hp: e.dma_start(
                    out=Css[hp * 64:(hp + 1) * 64, :, :, 0:128],
                    in_=stc_d.rearrange("s (pr hp) d v -> hp d s pr v", hp=2)[hp]), [], ["Css"])
                P.dma("sp", lambda e, hp=hp: e.dma_start(
                    out=Css[hp * 64:(hp + 1) * 64, :, :, 128],
                    in_=stn_d.rearrange("s (pr hp) d -> hp d s pr", hp=2)[hp], allow_slow_non_contiguous=True), [], ["Css"])
            P.dma("pool", lambda e: e.dma_start(out=kcb, in_=ck_d.rearrange("s j f -> j s f")), [], ["kcb"])
            P.dma("pool", lambda e: e.dma_start(out=vcb, in_=cv_d.rearrange("s j f -> j s f")), [], ["vcb"])
            for s in range(16):
                P.pe(lambda e, s=s: e.transpose(out=ptr[:, s % 8, :], in_=kcb[:, s, :], identity=ident), ["kcb", "ident"], ["ptr"])
                if s % 8 == 7:
                    P.act(lambda e, s=s: e.copy(out=kcT[:, s - 7:s + 1, :], in_=ptr[:]), ["ptr"], ["kcT"])
            P.dma("sp", lambda e: e.dma_start(out=k_s_d[:, 0:124, :], in_=ck_d[:, 4:128, :]), [], [])
            P.dma("sp", lambda e: e.dma_start(out=v_s_d[:, 0:124, :], in_=cv_d[:, 4:128, :]), [], [])
            for c in range(4):
                P.pe(lambda e, c=c: e.matmul(pb[5][:, c * 48:(c + 1) * 48], lhsT=hist[:, c * 128:(c + 1) * 128], rhs=identf[0:48, 0:48],
                                             start=True, stop=True), ["hist", "identf"], ["pb5"])
            P.dve(lambda e: e.tensor_copy(out=raws[:, :, :, 0:3], in_=pb[5][:, 0:192].rearrange("p (c s l) -> p c s l", c=4, s=16)),
                  ["pb5"], ["raws"])
            norm_transpose(xs0, "xt0", g_pre, "g_pre", 64, 0, ("xT", 0))
            win_group(64, 1, [0, 1, 2, 3, 4, 5, 6, 7, 8], {0: {"v", "o", "g", "kv", "kvout"}}, True, 0, {0: 0}, CTX_A)
            conv_group([0, 1, 2, 3], True, 64)
            P.dve(lambda e: e.tensor_copy(out=cvo.rearrange("p c (s l) -> p c s l", s=16), in_=raws[:, :, :, 4:7]), ["raws"], ["cvo"])
            for c in range(4):
                P.pe(lambda e, c=c: e.matmul(pb[5][0:48, c * 128:(c + 1) * 128], lhsT=cvo[:, c, :], rhs=identf, start=True, stop=True),
                     ["cvo", "identf"], ["pb5"])
            P.dve(lambda e: e.tensor_copy(out=cvt, in_=pb[5][0:48, :]), ["pb5"], ["cvt"])
            P.dma("sp", lambda e: e.dma_start(out=conv_s_d, in_=cvt), ["cvt"], [])
            ktm_stage(64, 1, True, CTX_A)
            gate_stage(64, 1, 16, msks, "msks", mTs, "mTs", CTX_A)
            mlstm_tile(64, 16, 0, 0, True, msks, "msks", True, True, ytm[0], "ytm0", CTX_A)
            attn_tile(64, 0, True, None, None, [kaT[0][kv][:, 0:64] for kv in range(2)], ["kaT0_0", "kaT0_1"], None, None, va[0], "va0",
                      sbias, "sbias", first_tile=False, yt=ytm[0], ytkey="ytm0")
            for hp in range(2):
                P.dma("sp", lambda e, hp=hp: e.dma_start(
                    out=c_s_d.rearrange("s (pr hp) d v -> hp d s pr v", hp=2)[hp], in_=Css[hp * 64:(hp + 1) * 64, :, :, 0:128]),
                    ["Css"], [])
                P.dma("sp", lambda e, hp=hp: e.dma_start(
                    out=n_s_d.rearrange("s (pr hp) d -> hp d s pr", hp=2)[hp], in_=Css[hp * 64:(hp + 1) * 64, :, :, 128],
                    allow_slow_non_contiguous=True), ["Css"], [])
            P.dma("sp", lambda e: e.dma_start(out=m_s_d.rearrange("s h -> h s"), in_=mTs, allow_slow_non_contiguous=True),
                  ["mTs"], [])
            P.dma("sp", lambda e: e.dma_start(out=k_s_d[:, 124:128, :], in_=kvout[0:64, 0:128]), ["kvout"], [])
            P.dma("sp", lambda e: e.dma_start(out=v_s_d[:, 124:128, :], in_=kvout[0:64, 128:256]), ["kvout"], [])
            WMODE["save"] = False
            WMODE["load"] = n_groups > n_prefix
            wout_group(64, 1, [xs0], ["xt0"])
            ffn_group(64, 1, [xs0], ["xt0"])
            pgate_group(64, 1, [xs0], ["xt0"], [psm_d], [ys_d])
        stats = P.emit()
    return nc, stats


_CACHE = {}


def _prog():
    if "nc" not in _CACHE:
        _CACHE["nc"] = build_program()
    return _CACHE["nc"]


def kernel(x_prompt, x_sample, p_prompt, p_sample, state_mlstm_c, state_mlstm_n, state_mlstm_m,
           state_mlstm_conv, cache_swa_k, cache_swa_v, norm_mix_pre, w_in, b_gates, conv_w,
           mlstm_norm, attn_sinks, attn_norm, w_out, norm_mix_post, norm_ffn_pre, w_up, w_down,
           norm_ffn_post, w_pgate, w_pproj):
    f = lambda a: np.ascontiguousarray(np.asarray(a, dtype=np.float32))
    x_prompt, x_sample, p_prompt, p_sample = f(x_prompt), f(x_sample), f(p_prompt), f(p_sample)
    nc, _ = _prog()
    consts = make_consts()
    shared = {
        "norm_mix_pre": f(norm_mix_pre), "w_in": f(w_in)[0], "b_gates": f(b_gates), "conv_w": f(conv_w)[0],
        "mlstm_norm": f(mlstm_norm), "attn_sinks": f(attn_sinks), "attn_norm": f(attn_norm), "w_out": f(w_out)[0],
        "norm_mix_post": f(norm_mix_post), "norm_ffn_pre": f(norm_ffn_pre), "w_up": f(w_up)[0], "w_down": f(w_down)[0],
        "norm_ffn_post": f(norm_ffn_post), "w_pgate": f(w_pgate)[0], "w_pproj": f(w_pproj)[0],
    }
    shared.update(consts)
    in_maps = []
    for c in range(NCORES):
        b, j = c // 4, c % 4
        xs = np.zeros((8192, D), np.float32)
        n_real = (j + 1) * SEG
        xs[8192 - n_real:] = x_prompt[b, 0:n_real]
        sl = slice(16 * c, 16 * c + 16)
        m = dict(shared)
        m.update({
            "xs": xs,
            "pp": np.ascontiguousarray(p_prompt[0, b, j * SEG:(j + 1) * SEG]),
            "xsm": np.ascontiguousarray(x_sample[sl].reshape(64, D)),
            "psm": np.ascontiguousarray(p_sample[0, sl].reshape(64, 256)),
            "st_c": f(state_mlstm_c[0, sl]), "st_n": f(state_mlstm_n[0, sl]), "st_m": f(state_mlstm_m[0, sl]),
            "st_conv": f(state_mlstm_conv[0, sl]).reshape(48, 512),
            "ck": f(cache_swa_k[0, sl]).reshape(16, 128, 128), "cv": f(cache_swa_v[0, sl]).reshape(16, 128, 128),
            "prevbias": np.full((128, 1), NEG if j == 0 else 0.0, np.float32),
        })
        in_maps.append(m)
    res = run_bass_kernel_spmd(nc, in_maps, core_ids=list(range(NCORES)))
    R = res.results
    y = np.stack([np.concatenate([R[4 * b + j]["y"] for j in range(4)], axis=0) for b in range(2)], axis=0)
    ysm = np.concatenate([R[c]["ysm"] for c in range(NCORES)], axis=0).reshape(128, 4, D)
    last = [3, 7]
    c_p = np.stack([R[c]["c_p"] for c in last])[None]
    n_p = np.stack([R[c]["n_p"] for c in last])[None]
    m_p = np.stack([R[c]["m_p"].reshape(4) for c in last])[None]
    conv_p = np.stack([R[c]["conv_p"] for c in last])[None]
    k_p = np.stack([R[c]["k_p"].reshape(128, 2, 64) for c in last])[None]
    v_p = np.stack([R[c]["v_p"].reshape(128, 2, 64) for c in last])[None]
    cat = lambda k: np.concatenate([R[c][k] for c in range(NCORES)], axis=0)
    c_s = cat("c_s")[None]
    n_s = cat("n_s")[None]
    m_s = cat("m_s")[None]
    conv_s = cat("conv_s").reshape(128, 3, 512)[None]
    k_s = cat("k_s").reshape(128, 128, 2, 64)[None]
    v_s = cat("v_s").reshape(128, 128, 2, 64)[None]
    outs = (y, ysm, c_p, n_p, m_p, conv_p, k_p, v_p, c_s, n_s, m_s, conv_s, k_s, v_s)
    return tuple(np.ascontiguousarray(o, dtype=np.float32) for o in outs)
```

```python
import contextlib
import numpy as np
import concourse.bass as bass
import concourse.mybir as mybir
from concourse.bass_utils import run_bass_kernel_spmd

F32 = mybir.dt.float32
BF16 = mybir.dt.bfloat16
ALU = mybir.AluOpType
AF = mybir.ActivationFunctionType
AX = mybir.AxisListType

ENGS = ("pe", "act", "dve", "pool", "sp")
DMA_NS = 8

D = 1024
PROJ = 2312
NCORES = 8
SEG = 2048
NEG = -1.0e30
EPS = 1e-6


class Op:
    __slots__ = ("eng", "fn", "dma", "deps", "mile", "slot", "idx", "needed")

    def __init__(self, eng, fn, dma):
        self.eng = eng
        self.fn = fn
        self.dma = dma
        self.deps = ()
        self.mile = None
        self.slot = None
        self.idx = None
        self.needed = False


class Prog:
    def __init__(self, nc):
        self.nc = nc
        self.ops = []
        self.last_w = {}
        self.rd_eng = {}
        self.rd_dma = {}
        self.ranges = {}
        self.by_base = {}
        self.ovl = {}

    @staticmethod
    def base(k):
        return k[0] if isinstance(k, tuple) else k

    def overlapping(self, b):
        if b not in self.ovl:
            r = self.ranges.get(b)
            res = []
            if r is not None:
                for b2, r2 in self.ranges.items():
                    if b2 != b and r2[0] < r[1] and r[0] < r2[1]:
                        res.append(b2)
            self.ovl[b] = res
        return self.ovl[b]

    def op(self, eng, fn, reads=(), writes=(), dma=False):
        o = Op(eng, fn, dma)
        o.idx = len(self.ops)
        deps = set()
        ps_r = [k for k in reads if isinstance(k, str) and (k.startswith("pb") or k == "ptr")]
        if ps_r:
            writes = list(writes) + [k for k in ps_r if k not in writes]

        def dep_w(k):
            w = self.last_w.get(k)
            if w is not None:
                deps.add(w)

        def dep_r(k):
            for v in self.rd_eng.get(k, {}).values():
                deps.add(v)
            for v in self.rd_dma.get(k, ()):
                deps.add(v)

        for r in reads:
            dep_w(r)
            for b2 in self.overlapping(self.base(r)):
                for k2 in self.by_base.get(b2, ()):
                    dep_w(k2)
        for k in writes:
            dep_w(k)
            dep_r(k)
            for b2 in self.overlapping(self.base(k)):
                for k2 in self.by_base.get(b2, ()):
                    dep_w(k2)
                    dep_r(k2)
        o.deps = tuple(sorted(deps))
        for r in reads:
            self.by_base.setdefault(self.base(r), set()).add(r)
            if dma:
                self.rd_dma.setdefault(r, []).append(o.idx)
            else:
                self.rd_eng.setdefault(r, {})[eng] = o.idx
        for k in writes:
            self.by_base.setdefault(self.base(k), set()).add(k)
            self.last_w[k] = o.idx
            self.rd_eng[k] = {}
            self.rd_dma[k] = []
        self.ops.append(o)
        return o

    def pe(self, fn, reads=(), writes=()):
        return self.op("pe", fn, reads, writes)

    def act(self, fn, reads=(), writes=()):
        return self.op("act", fn, reads, writes)

    def dve(self, fn, reads=(), writes=()):
        return self.op("dve", fn, reads, writes)

    def pool(self, fn, reads=(), writes=()):
        return self.op("pool", fn, reads, writes)

    def dma(self, q, fn, reads=(), writes=()):
        return self.op(q, fn, reads, writes, dma=True)

    @staticmethod
    def _skip(p, o):
        return (not p.dma) and (not o.dma) and p.eng == "pe" and o.eng == "pe"

    def emit(self):
        nc = self.nc
        ops = self.ops
        import os as _os
        _lim = int(_os.environ.get("MK_STOP_AFTER", "0"))
        if _lim > 0:
            ops = ops[:_lim]
        print("emitting %d of %d ops" % (len(ops), len(self.ops)))
        for o in ops:
            for d in o.deps:
                p = ops[d]
                if p.dma or self._skip(p, o):
                    continue
                p.needed = True
        cnt = {e: 0 for e in ENGS}
        dcnt = {e: 0 for e in ENGS}
        per_eng = {e: [] for e in ENGS}
        for o in ops:
            if o.dma:
                o.slot = dcnt[o.eng]
                dcnt[o.eng] += 1
            elif o.needed:
                cnt[o.eng] += 1
                o.mile = cnt[o.eng]
            per_eng[o.eng].append(o)
        with contextlib.ExitStack() as st:
            csem = {e: st.enter_context(nc.semaphore("c_" + e)) for e in ENGS if e != "sp"}
            dsem = {
                e: [st.enter_context(nc.semaphore("d_%s_%d" % (e, i))) for i in range(DMA_NS)]
                for e in ("sp", "act", "pool")
                if dcnt[e] > 0
            }
            block = st.enter_context(nc.Block())

            def need_of(p):
                if p.dma:
                    return dsem[p.eng][p.slot % DMA_NS], 16 * (p.slot // DMA_NS + 1)
                return csem[p.eng], p.mile

            def run_engine(ename, engobj):
                seen = {}

                def wait(sem, val):
                    k = id(sem)
                    if seen.get(k, 0) >= val:
                        return
                    seen[k] = val
                    engobj.wait_ge(sem, val)

                for o in per_eng[ename]:
                    for d in o.deps:
                        p = ops[d]
                        if self._skip(p, o):
                            continue
                        s, v = need_of(p)
                        wait(s, v)
                    if o.dma:
                        if o.slot >= DMA_NS:
                            wait(dsem[ename][o.slot % DMA_NS], 16 * (o.slot // DMA_NS))
                        ins = o.fn(engobj)
                        ins.then_inc(dsem[ename][o.slot % DMA_NS], 16)
                    else:
                        ins = o.fn(engobj)
                        if o.mile is not None:
                            ins.then_inc(csem[ename], 1)
                n = dcnt.get(ename, 0)
                for i in range(min(DMA_NS, n)):
                    uses = (n - i + DMA_NS - 1) // DMA_NS
                    wait(dsem[ename][i], 16 * uses)

            @block.tensor
            def _(e):
                run_engine("pe", e)

            @block.scalar
            def _(e):
                run_engine("act", e)

            @block.vector
            def _(e):
                run_engine("dve", e)

            @block.gpsimd
            def _(e):
                run_engine("pool", e)

            @block.sync
            def _(e):
                run_engine("sp", e)
        return cnt, dcnt


def _slopes():
    return np.exp2(-8.0 * np.arange(1, 9, dtype=np.float64) / 8.0)


def make_consts():
    c = {}
    c["c_ident"] = np.eye(128, dtype=np.float32)
    s = np.arange(128)
    tri = (s[:, None] <= s[None, :]).astype(np.float32)
    mp = np.stack([tri, np.ones((128, 128), np.float32)], axis=1)
    c["c_mskp"] = np.ascontiguousarray(mp)
    s64 = np.arange(64)
    same = (s64[:, None] // 4 == s64[None, :] // 4)
    ms = np.stack([(same & (s64[:, None] <= s64[None, :])).astype(np.float32), same.astype(np.float32)], axis=1)
    c["c_msks"] = np.ascontiguousarray(ms)
    sl = _slopes()
    t = np.arange(128)[:, None]
    j = np.arange(256)[None, :]
    dist = t + 128 - j
    valid = (dist >= 0) & (dist <= 128)
    ab = np.where(valid[:, None, :], -sl[None, :, None] * dist[:, None, :], NEG)
    c["c_abias"] = ab.astype(np.float32)
    r = (np.arange(64) % 4)[:, None]
    jj = np.arange(128)[None, :]
    dist = 128 + r - jj
    valid = dist <= 128
    sb = np.where(valid[:, None, :], -sl[None, :, None] * dist[:, None, :], NEG)
    tq = np.arange(64)[:, None]
    tk = np.arange(64)[None, :]
    dist2 = (tq % 4) - (tk % 4)
    valid2 = (tq // 4 == tk // 4) & (dist2 >= 0)
    sn = np.where(valid2[:, None, :], -sl[None, :, None] * dist2[:, None, :], NEG)
    c["c_sbias"] = np.concatenate([sb, sn], axis=2).astype(np.float32)
    seq_of = np.arange(64) // 4
    rowsel = (seq_of[:, None] == np.arange(16)[None, :]).astype(np.float32)
    c["c_rowsel"] = rowsel
    colsel = np.broadcast_to(rowsel.T[None, :, :], (128, 16, 64)).astype(np.float32)
    c["c_colsel"] = np.ascontiguousarray(colsel)
    esel = np.zeros((4, 128), np.float32)
    for k in range(4):
        esel[k, (k % 2) * 64:(k % 2) * 64 + 64] = 1.0
    c["c_esel"] = esel
    sel2 = np.zeros((4, 2), np.float32)
    for k in range(4):
        sel2[k, k // 2] = 1.0
    c["c_sel2"] = sel2
    return c


def build_program(n_groups=16, n_prefix=12, do_sample=True):
    nc = bass.Bass("TRN2", target_bir_lowering=False)
    P = Prog(nc)
    st = contextlib.ExitStack()

    def din(name, shape):
        return nc.dram_tensor(name, list(shape), F32, kind="ExternalInput").ap()

    def dout(name, shape):
        return nc.dram_tensor(name, list(shape), F32, kind="ExternalOutput").ap()

    NTOK = n_groups * 512
    NMAIN = (n_groups - n_prefix) * 512
    xs_d = din("xs", [NTOK, D])
    pp_d = din("pp", [NMAIN, 256])
    xsm_d = din("xsm", [64, D])
    psm_d = din("psm", [64, 256])
    stc_d = din("st_c", [16, 4, 64, 128])
    stn_d = din("st_n", [16, 4, 64])
    stm_d = din("st_m", [16, 4])
    stv_d = din("st_conv", [48, 512])
    ck_d = din("ck", [16, 128, 128])
    cv_d = din("cv", [16, 128, 128])
    prevb_d = din("prevbias", [128, 1])
    g_pre_d = din("norm_mix_pre", [1, D])
    w_in_d = din("w_in", [D, PROJ])
    bg_d = din("b_gates", [1, 8])
    cw_d = din("conv_w", [4, 512])
    g_ml_d = din("mlstm_norm", [1, 512])
    sink_d = din("attn_sinks", [1, 8])
    g_at_d = din("attn_norm", [1, 512])
    w_out_d = din("w_out", [D, D])
    g_post_d = din("norm_mix_post", [1, D])
    g_fpre_d = din("norm_ffn_pre", [1, D])
    w_up_d = din("w_up", [D, 4096])
    w_dn_d = din("w_down", [4096, D])
    g_fpost_d = din("norm_ffn_post", [1, D])
    w_pg_d = din("w_pgate", [D, D])
    w_pp_d = din("w_pproj", [256, D])
    cst = {k: din(k, v.shape) for k, v in make_consts().items()}

    y_d = dout("y", [NMAIN, D])
    ys_d = dout("ysm", [64, D])
    c_p_d = dout("c_p", [4, 64, 128])
    n_p_d = dout("n_p", [4, 64])
    m_p_d = dout("m_p", [4, 1])
    conv_p_d = dout("conv_p", [3, 512])
    k_p_d = dout("k_p", [128, 128])
    v_p_d = dout("v_p", [128, 128])
    c_s_d = dout("c_s", [16, 4, 64, 128])
    n_s_d = dout("n_s", [16, 4, 64])
    m_s_d = dout("m_s", [16, 4])
    conv_s_d = dout("conv_s", [48, 512])
    k_s_d = dout("k_s", [16, 128, 128])
    v_s_d = dout("v_s", [16, 128, 128])

    ARENA_WORDS = 53200
    with st:
        arena = st.enter_context(nc.sbuf_tensor("arena", [128, ARENA_WORDS], F32))
        off = [0]

        def alloc(name, shape, dt=F32, at=None):
            nb = 4 if dt == F32 else 2
            n = 1
            for d_ in shape[1:]:
                n *= d_
            size = (n * nb + 31) // 32 * 32
            o = off[0] if at is None else at
            assert o % 32 == 0
            if at is None:
                off[0] = o + size
            assert o + size <= ARENA_WORDS * 4, (name, o, size)
            P.ranges[name] = (o, o + size)
            v = arena[0:shape[0], o // 4:(o + size) // 4]
            if dt != F32:
                v = v.bitcast(dt)
            v = v[:, 0:n]
            if len(shape) == 3:
                v = v.rearrange("p (a b) -> p a b", a=shape[1])
            elif len(shape) == 4:
                v = v.rearrange("p (a b c) -> p a b c", a=shape[1], b=shape[2])
            return v

        def psum(name, shape, dt=F32):
            return st.enter_context(nc.psum_tensor(name, list(shape), dt))

        identf = alloc("identf", [128, 128])
        ident = alloc("ident", [128, 128], BF16)
        g_pre = alloc("g_pre", [128, D])
        g_post = alloc("g_post", [128, D])
        g_fpre = alloc("g_fpre", [128, D])
        g_fpost = alloc("g_fpost", [128, D])
        g_ml = alloc("g_ml", [128, 512])
        g_at = alloc("g_at", [128, 512])
        bgate = alloc("bgate", [128, 8])
        sinks = alloc("sinks", [128, 8])
        cw = alloc("cw", [128, 4, 4])
        esel = alloc("esel", [4, 128])
        sel2 = alloc("sel2", [4, 2])
        prevb = alloc("prevb", [128, 1])
        mhalf = alloc("mhalf", [128, 16])
        w_res = alloc("w_res", [128, 8, 776], BF16)
        w_pp_sb = alloc("w_pp", [128, 2, D], BF16)
        wu = [alloc("wu%d" % i, [128, 8, 512], BF16) for i in range(2)]
        wd = [alloc("wd%d" % i, [128, 4, 512], BF16) for i in range(2)]
        xt0 = alloc("xt0", [128, D])
        xnb = alloc("xnb", [128, D], BF16)
        xnb2 = alloc("xnb2", [128, D], BF16)
        xT = alloc("xT", [128, 8, 512], BF16)
        tmp = alloc("tmp", [128, D])
        junk = alloc("junk", [128, D], BF16, at=P.ranges["tmp"][0])
        cvtP = alloc("cvtP", [3, 512], at=P.ranges["tmp"][0])
        cvoP = alloc("cvoP", [128, 4, 3], at=P.ranges["tmp"][0] + 2048)
        raw = alloc("raw", [128, 4, 515])
        kaT = [[alloc("kaT%d_%d" % (i, kv), [128, 512], BF16) for kv in range(2)] for i in range(2)]
        va = [alloc("va%d" % i, [128, 2, 64], BF16) for i in range(8)]
        kvout = alloc("kvout", [128, 256])
        ptb = alloc("ptb", [128, 4, 256], BF16)
        ppT = alloc("ppT", [128, 2, 512], BF16)
        ytm = [alloc("ytm%d" % i, [128, D], BF16) for i in range(4)]
        xT2 = alloc("xT2", [128, 8, 512], BF16, at=P.ranges["ytm0"][0])
        sm = alloc("sm", [128, 128])
        lfn = alloc("lfn", [128, 4, 4])
        av = alloc("av", [128, 4, 4])
        bneg = alloc("bneg", [128, 4, 4])
        ev = alloc("ev", [128, 4, 4])
        fl = alloc("fl", [128, 4, 4])
        wi = alloc("wi", [128, 4, 4])
        tk = alloc("tk", [128, 4, 8])
        tsm = alloc("tsm", [4, 160])
        rhsd = alloc("rhsd", [4, 16, 2])
        dstt = alloc("dstt", [128, 16, 2])
        gtA = alloc("gtA", [128, 4, 8])
        gtB = alloc("gtB", [128, 4, 8])
        mTp = alloc("mT", [4, 1])
        mTs = alloc("mTs", [4, 16])
        Cst = alloc("Cst", [128, 2, 129])
        Cbf = alloc("Cbf", [128, 2, 129], BF16)
        ktmB = alloc("ktmB", [128, 4, 2, 128], BF16)
        p0 = off[0]
        qaT = alloc("qaT", [128, 4, 512], BF16)
        vaug = [alloc("vaug%d" % i, [128, 4, 129], BF16) for i in range(4)]
        sig = [alloc("sig%d" % i, [128, 512], BF16) for i in range(4)]
        qz = alloc("qz", [128, 4, 512], BF16)
        vaugB = [alloc("vaugB%d" % i, [128, 4, 129], BF16, at=P.ranges["sig0"][0] + i * 1056) for i in range(4)]
        kT2 = alloc("kT2", [128, 2, 512], BF16)
        p1 = off[0]
        w_rest = alloc("w_rest", [128, 8, 1536], BF16)
        pW = off[0]
        off[0] = p1
        cacc = alloc("cacc", [128, 512])
        csg = alloc("csg", [128, 512])
        sc = alloc("sc", [128, 4, 256])
        pex = alloc("pex", [128, 4, 256], BF16)
        pT = alloc("pT", [128, 8, 128], BF16)
        att = alloc("att", [128, 512])
        wT = alloc("wT", [128, 4, 128], BF16)
        tot = alloc("tot", [128, 4, 129])
        gs = alloc("gs", [128, 512])
        gx = alloc("gx", [4, 2, 512], at=P.ranges["tot"][0])
        evaug = alloc("evaug", [128, 4, 129], BF16)
        ktm = alloc("ktm", [128, 4, 2, 128], BF16)
        pM = off[0]
        off[0] = p0
        hT = alloc("hT", [128, 32, 512], BF16)
        hr = alloc("hr", [128, 512])
        f1 = alloc("f1", [128, 4, 512])
        pF = off[0]
        off[0] = max(pW, pM, pF)
        pS = off[0]
        abias = alloc("abias", [128, 8, 256])
        mskp = alloc("mskp", [128, 2, 128])
        xt123 = [alloc("xt%d" % i, [128, D]) for i in (1, 2, 3)]
        xt = [xt0] + xt123
        pPend = off[0]
        sc2 = alloc("sc2", [128, 4, 256], at=pPend)
        pex2 = alloc("pex2", [128, 4, 256], BF16, at=pPend + 4096)
        pT2 = alloc("pT2", [128, 8, 128], BF16, at=pPend + 6144)
        xtB = [alloc("xtB%d" % i, [128, D], at=pPend + 8192 + i * 4096) for i in range(4)]
        off[0] = pS
        Css = alloc("Css", [128, 16, 2, 129])
        kcT = alloc("kcT", [128, 16, 128], BF16)
        vcb = alloc("vcb", [128, 16, 128], BF16)
        kcb = alloc("kcb", [128, 16, 128], BF16, at=p0)
        sbias = alloc("sbias", [64, 8, 192])
        colsel = alloc("colsel", [128, 16, 64], BF16)
        qam = [alloc("qam%d" % i, [128, 16, 64], BF16) for i in range(2)]
        qm = alloc("qm", [128, 2, 16, 64], BF16)
        pTm = alloc("pTm", [128, 16, 64], BF16)
        evm = alloc("evm", [128, 4, 129], BF16)
        raws = alloc("raws", [128, 4, 16, 7])
        hist = alloc("hist", [48, 512], at=p0 + 4096)
        cvt = alloc("cvt", [48, 512])
        cvo = alloc("cvo", [128, 4, 48])
        msks = alloc("msks", [64, 2, 64])
        rowsel = alloc("rowsel", [64, 16])
        Cbr = [alloc("Cbr%d" % i, [128, 2, 129], BF16) for i in range(4)]
        pSend = off[0]
        print("arena bytes: G+phase=%d prompt_end=%d sample_end=%d limit=%d" % (pS, pPend, pSend, ARENA_WORDS * 4))
        pb = [psum("pb%d" % i, [128, 512]) for i in range(7)]
        ptr = psum("ptr", [128, 8, 128], BF16)

        def ld(q, dst, src, key, **kw):
            P.dma(q, lambda e: e.dma_start(out=dst, in_=src, **kw), writes=[key])

        ld("sp", identf, cst["c_ident"], "identf")
        ld("sp", g_pre, g_pre_d[0].partition_broadcast(128), "g_pre")
        ld("sp", bgate, bg_d[0].partition_broadcast(128), "bgate")
        for j in range(4):
            ld("sp", cw[:, :, j], cw_d[j].rearrange("(c p) -> p c", p=128), "cw", allow_slow_non_contiguous=True)
        ld("sp", mskp, cst["c_mskp"], "mskp")
        ld("sp", esel, cst["c_esel"], "esel")
        ld("sp", sel2, cst["c_sel2"], "sel2")
        ld("sp", prevb, prevb_d, "prevb")
        ld("sp", sinks, sink_d[0].partition_broadcast(128), "sinks")
        ld("sp", g_post, g_post_d[0].partition_broadcast(128), "g_post")
        ld("sp", g_fpre, g_fpre_d[0].partition_broadcast(128), "g_fpre")
        ld("sp", g_fpost, g_fpost_d[0].partition_broadcast(128), "g_fpost")
        ld("sp", g_ml, g_ml_d[0].partition_broadcast(128), "g_ml")
        ld("sp", g_at, g_at_d[0].partition_broadcast(128), "g_at")
        ld("sp", abias, cst["c_abias"], "abias")
        P.dve(lambda e: e.tensor_copy(out=ident, in_=identf), ["identf"], ["ident"])
        w_in_v = w_in_d.rearrange("(kc p) n -> p kc n", p=128)
        for kc in range(8):
            ld("pool", w_res[:, kc, 0:768], w_in_v[:, kc, 256:1024], ("w_res", kc, 0))
            ld("pool", w_res[:, kc, 768:776], w_in_v[:, kc, 1536:1544], ("w_res", kc, 1))
        W_RES_KEYS = [("w_res", kc, i) for kc in range(8) for i in range(2)]
        ld("pool", w_pp_sb, w_pp_d.rearrange("(kc p) n -> p kc n", p=128), "w_pp")
        w_out_v = w_out_d.rearrange("(kc p) n -> p kc n", p=128)
        w_pg_v = w_pg_d.rearrange("(kc p) n -> p kc n", p=128)
        w_up_v = w_up_d.rearrange("(kc p) n -> p kc n", p=128)
        w_dn_v = w_dn_d.rearrange("(jc p) n -> p jc n", p=128)

        w_rest_scr = nc.dram_tensor("w_rest_scr", [128, 8, 1536], BF16).ap()
        w_rest_state = {"prefetched": False}
        def emit_w_rest_scr():
            for kc in range(8):
                P.dma("pool", lambda e, kc=kc: e.dma_start(out=w_rest_scr[:, kc, 0:256], in_=w_in_v[:, kc, 0:256]), [], [("w_rest_scr", kc, 0)])
                P.dma("pool", lambda e, kc=kc: e.dma_start(out=w_rest_scr[:, kc, 256:768], in_=w_in_v[:, kc, 1024:1536]), [], [("w_rest_scr", kc, 1)])
                for j in range(4):
                    P.dma("pool", lambda e, kc=kc, j=j: e.dma_start(
                        out=w_rest_scr[:, kc, 768 + j * 128:768 + (j + 1) * 128].rearrange("p (kv d) -> p kv d", kv=2),
                        in_=w_in_v[:, kc, 1544:2056].rearrange("p (kv j d) -> p j kv d", kv=2, j=4)[:, j]), [], [("w_rest_scr", kc, 2, j)])
                P.dma("pool", lambda e, kc=kc: e.dma_start(out=w_rest_scr[:, kc, 1280:1536], in_=w_in_v[:, kc, 2056:2312]), [], [("w_rest_scr", kc, 3)])

        W_SCR_KEYS = [("w_rest_scr", kc, i) for kc in range(8) for i in (0, 1, 3)] + [("w_rest_scr", kc, 2, j) for kc in range(8) for j in range(4)]

        def load_w_rest():
            P.dma("sp", lambda e: e.dma_start(out=w_rest, in_=w_rest_scr), W_SCR_KEYS, W_REST_KEYS)

        W_REST_KEYS = [("w_rest", kc, i) for kc in range(8) for i in (0, 1, 3)] + [("w_rest", kc, 2, j) for kc in range(8) for j in range(4)]

        wu_ctr = [0]
        scr_big = nc.dram_tensor("scr_big", [12, 128, 8 * 512], BF16).ap()
        scr_dn = nc.dram_tensor("scr_dn", [16, 128, 4 * 512], BF16).ap()
        WMODE = {"save": False, "load": False}
        BIGTAG = {"wo": 0, "up": 2, "pg": 10}

        def WUK(b):
            return [("wu", b, 0), ("wu", b, 1)]

        def next_wu(src, tag):
            b = wu_ctr[0] % 2
            wu_ctr[0] += 1
            slot = BIGTAG[tag[0]] + tag[1]
            if WMODE["load"]:
                P.dma("sp", lambda e: e.dma_start(out=wu[b].rearrange("p a b -> p (a b)"), in_=scr_big[slot]), [("scr_big", slot)], WUK(b))
                return b
            P.dma("pool", lambda e: e.dma_start(out=wu[b], in_=src), [], WUK(b))
            if WMODE["save"]:
                P.dma("sp", lambda e: e.dma_start(out=scr_big[slot], in_=wu[b].rearrange("p a b -> p (a b)")), WUK(b), [("scr_big", slot)])
            return b

        def load_wd(wbuf, wkey, src, slot):
            if WMODE["load"]:
                P.dma("sp", lambda e: e.dma_start(out=wbuf.rearrange("p a b -> p (a b)"), in_=scr_dn[slot]), [("scr_dn", slot)], [wkey])
                return
            P.dma("pool", lambda e: e.dma_start(out=wbuf, in_=src), [], [wkey])
            if WMODE["save"]:
                P.dma("sp", lambda e: e.dma_start(out=scr_dn[slot], in_=wbuf.rearrange("p a b -> p (a b)")), [wkey], [("scr_dn", slot)])

        P.pool(lambda e: e.memset(mhalf, -0.5), [], ["mhalf"])
        P.dve(lambda e: e.tensor_scalar(out=cw[:, 0:2, :], in0=cw[:, 0:2, :], scalar1=0.0625, scalar2=None, op0=ALU.mult), ["cw"], ["cw"])
        P.dve(lambda e: e.tensor_scalar(out=cw[:, 2:4, :], in0=cw[:, 2:4, :], scalar1=0.5, scalar2=None, op0=ALU.mult), ["cw"], ["cw"])
        P.dve(lambda e: e.tensor_scalar(out=g_ml, in0=g_ml, scalar1=0.5, scalar2=None, op0=ALU.mult), ["g_ml"], ["g_ml"])
        P.pool(lambda e: e.memset(raw, 0.0), [], ["raw"])
        P.pool(lambda e: e.memset(Cst, 0.0), [], ["Cst"])
        P.pool(lambda e: e.memset(Cbf, 0.0), [], ["Cbf"])
        P.pool(lambda e: e.memset(mTp, 0.0), [], ["mT"])
        for i in range(2):
            for kv in range(2):
                P.pool(lambda e, i=i, kv=kv: e.memset(kaT[i][kv], 0.0), [], ["kaT%d_%d" % (i, kv)])
        for i in range(8):
            P.pool(lambda e, i=i: e.memset(va[i], 0.0), [], [("va%d" % i)])

        def rms_rstd(ss_col, out_col, n, keys_r, key_w):
            P.dve(lambda e: e.tensor_scalar(out=out_col, in0=ss_col, scalar1=1.0 / n, scalar2=EPS,
                                            op0=ALU.mult, op1=ALU.add), keys_r, [key_w])
            shp = list(out_col.shape)
            P.pool(lambda e: e.tensor_tensor(out=out_col, in0=out_col, in1=mhalf[0:shp[0], 0:shp[1]], op=ALU.pow),
                   [key_w, "mhalf"], [key_w])

        def capture(fn):
            buf = []
            P.op = lambda eng, f, reads=(), writes=(), dma=False: buf.append((eng, f, tuple(reads), tuple(writes), dma))
            try:
                fn()
            finally:
                del P.op
            return buf

        def interleave(*bufs):
            pos = [0] * len(bufs)
            total = sum(len(b) for b in bufs)
            import os as _os2
            _ck = int(_os2.environ.get("MK_CHUNK", "0"))
            if _ck > 0:
                while any(pos[k] < len(b) for k, b in enumerate(bufs)):
                    for k, b in enumerate(bufs):
                        n_ = max(1, int(round(_ck * len(b) / max(len(x) for x in bufs))))
                        for item in b[pos[k]:pos[k] + n_]:
                            P.op(*item)
                        pos[k] = min(len(b), pos[k] + n_)
                return
            for _ in range(total):
                best, bestf = None, None
                for k, b in enumerate(bufs):
                    if pos[k] < len(b):
                        fr = pos[k] / len(b)
                        if best is None or fr < bestf:
                            best, bestf = k, fr
                item = bufs[best][pos[best]]
                pos[best] += 1
                P.op(*item)

        def transpose_to(srcb, srckeys, Tt, dstT, c0, dstkey, nk=8):
            for kc in range(nk):
                P.pe(lambda e, kc=kc: e.transpose(out=ptr[:, kc, 0:Tt], in_=srcb[0:Tt, kc * 128:(kc + 1) * 128],
                                                  identity=ident[0:Tt, 0:Tt]), list(srckeys) + ["ident"], ["ptr"])
            P.act(lambda e: e.copy(out=dstT[:, 0:nk, c0:c0 + Tt], in_=ptr[:, 0:nk, 0:Tt]), ["ptr"], [dstkey])

        def norm_transpose(src, srckey, gain, gkey, Tt, c0, dstkey):
            P.act(lambda e: e.activation(out=junk[0:Tt, :], in_=src, func=AF.Square, accum_out=sm[0:Tt, 0:1]),
                  [srckey], ["junk", ("sm", 0)])
            rms_rstd(sm[0:Tt, 0:1], sm[0:Tt, 1:2], D, [("sm", 0)], ("sm", 1))
            P.dve(lambda e: e.scalar_tensor_tensor(out=xnb[0:Tt, :], in0=src, scalar=sm[0:Tt, 1:2], in1=gain[0:Tt, :],
                                                   op0=ALU.mult, op1=ALU.mult), [srckey, ("sm", 1), gkey], ["xnb"])
            transpose_to(xnb, ["xnb"], Tt, xT, c0, dstkey)

        def norm_group(Tt, NT, srcs, srckeys, gain, gkey, xTd=None, xTn="xT"):
            xTd = xT if xTd is None else xTd
            for i in range(NT):
                P.act(lambda e, i=i: e.activation(out=junk[0:Tt, :], in_=srcs[i], func=AF.Square, accum_out=sm[0:Tt, 76 + i:77 + i]),
                      [srckeys[i]], ["junk", ("sm", "ns", i)])
            rms_rstd(sm[0:Tt, 76:76 + NT], sm[0:Tt, 80:80 + NT], D, [("sm", "ns", i) for i in range(NT)], ("sm", "nr"))
            for i in range(NT):
                xb, xbk = (xnb, "xnb") if i % 2 == 0 else (xnb2, "xnb2")
                P.dve(lambda e, i=i, xb=xb: e.scalar_tensor_tensor(out=xb[0:Tt, :], in0=srcs[i], scalar=sm[0:Tt, 80 + i:81 + i],
                                                                   in1=gain[0:Tt, :], op0=ALU.mult, op1=ALU.mult),
                      [srckeys[i], ("sm", "nr"), gkey], [xbk])
                transpose_to(xb, [xbk], Tt, xTd, i * Tt, (xTn, i))

        def resid_group(Tt, NT, xs_, xkeys, srcA, keysA, srcB, keysB, gain, gkey):
            for i in range(NT):
                for half, (src, k) in enumerate(((srcA[i], keysA[i]), (srcB[i], keysB[i]))):
                    P.act(lambda e, i=i, half=half, src=src: e.activation(
                        out=junk[0:Tt, 0:512], in_=src[0:Tt, 0:512], func=AF.Square,
                        accum_out=sm[0:Tt, 64 + 2 * i + half:65 + 2 * i + half]), [k], ["junk", ("sm", "rs", i, half)])
            P.dve(lambda e: e.tensor_reduce(out=sm[0:Tt, 72:72 + NT], in_=sm[0:Tt, 64:64 + 2 * NT].rearrange("p (i h) -> p i h", h=2),
                                            axis=AX.X, op=ALU.add), [("sm", "rs", i, h) for i in range(NT) for h in range(2)],
                  [("sm", "rr2")])
            rms_rstd(sm[0:Tt, 72:72 + NT], sm[0:Tt, 72:72 + NT], D, [("sm", "rr2")], ("sm", "rr2"))
            for i in range(NT):
                for half, (src, k) in enumerate(((srcA[i], keysA[i]), (srcB[i], keysB[i]))):
                    P.dve(lambda e, i=i, half=half, src=src: e.scalar_tensor_tensor(
                        out=tmp[0:Tt, half * 512:(half + 1) * 512], in0=src[0:Tt, 0:512], scalar=sm[0:Tt, 72 + i:73 + i],
                        in1=gain[0:Tt, half * 512:(half + 1) * 512], op0=ALU.mult, op1=ALU.mult),
                        [k, ("sm", "rr2"), gkey], [("tmp", half)])
                P.dve(lambda e, i=i: e.tensor_tensor(out=xs_[i], in0=xs_[i], in1=tmp[0:Tt, :], op=ALU.add),
                      [xkeys[i], ("tmp", 0), ("tmp", 1)], [xkeys[i]])

        def gate_stage(Tt, NT, nseq, msk, mskkey, mT, mkey, ctx):
            L = Tt // nseq
            NC = NT * nseq
            gtA_ = ctx["gt"]
            gkeys = [(ctx["gk"], i) for i in range(NT)]
            bA, bB = pb[5], pb[6]
            P.act(lambda e: e.activation(out=lfn[0:Tt, 0:NT, :], in_=gtA_[0:Tt, 0:NT, 4:8], func=AF.Exp, scale=-1.0), gkeys, ["lfn"])
            P.dve(lambda e: e.tensor_scalar(out=lfn[0:Tt, 0:NT, :], in0=lfn[0:Tt, 0:NT, :], scalar1=1.0, scalar2=None, op0=ALU.add),
                  ["lfn"], ["lfn"])
            P.act(lambda e: e.activation(out=lfn[0:Tt, 0:NT, :], in_=lfn[0:Tt, 0:NT, :], func=AF.Ln), ["lfn"], ["lfn"])
            P.pe(lambda e: e.matmul(bA[0:Tt, 0:4 * NT], lhsT=msk[0:Tt, 0, :], rhs=lfn[0:Tt, 0:NT, :].rearrange("p a b -> p (a b)"),
                                    start=True, stop=True), ["lfn", mskkey], ["pb5"])
            P.dve(lambda e: e.tensor_copy(out=bneg[0:Tt, 0:NT, :].rearrange("p a b -> p (a b)"), in_=bA[0:Tt, 0:4 * NT]), ["pb5"], ["bneg"])
            P.dve(lambda e: e.tensor_tensor(out=av[0:Tt, 0:NT, :], in0=gtA_[0:Tt, 0:NT, 0:4], in1=bneg[0:Tt, 0:NT, :], op=ALU.add),
                  gkeys + ["bneg"], ["av"])
            for t in range(NT):
                P.pe(lambda e, t=t: e.matmul(bA[0:4, t * Tt:(t + 1) * Tt], lhsT=av[0:Tt, t, :], rhs=identf[0:Tt, 0:Tt], start=True, stop=True),
                     ["av", "identf"], ["pb5"])
            for t in range(NT):
                P.pe(lambda e, t=t: e.matmul(bB[0:4, t * Tt:(t + 1) * Tt], lhsT=lfn[0:Tt, t, :], rhs=identf[0:Tt, 0:Tt], start=True, stop=True),
                     ["lfn", "identf"], ["pb6"])
            amax = tsm[:, 0:NC]
            bsn = tsm[:, 16:16 + NC]
            GT = tsm[:, 32:32 + NC]
            dT = tsm[:, 48:48 + NC]
            mseq = tsm[:, 64:64 + (NT + 1) * nseq]
            P.dve(lambda e: e.tensor_reduce(out=amax, in_=bA[0:4, 0:NT * Tt].rearrange("p (s l) -> p s l", s=NC),
                                            axis=AX.X, op=ALU.max), ["pb5"], [("tsm", "a")])
            P.dve(lambda e: e.tensor_reduce(out=bsn, in_=bB[0:4, 0:NT * Tt].rearrange("p (s l) -> p s l", s=NC),
                                            axis=AX.X, op=ALU.add), ["pb6"], [("tsm", "b")])
            P.dve(lambda e: e.tensor_copy(out=mseq[:, 0:nseq], in_=mT), [mkey], [("tsm", "m")])
            for t in range(NT):
                c0_, c1_ = t * nseq, (t + 1) * nseq
                P.dve(lambda e, c0_=c0_, c1_=c1_: e.tensor_tensor(out=GT[:, c0_:c1_], in0=amax[:, c0_:c1_], in1=mseq[:, c0_:c1_], op=ALU.max),
                      [("tsm", "a"), ("tsm", "m")], [("tsm", "g")])
                P.dve(lambda e, c0_=c0_, c1_=c1_: e.tensor_tensor(out=mseq[:, c1_:c1_ + nseq], in0=GT[:, c0_:c1_], in1=bsn[:, c0_:c1_],
                                                                 op=ALU.subtract), [("tsm", "g"), ("tsm", "b")], [("tsm", "m")])
            P.dve(lambda e: e.tensor_tensor(out=dT, in0=mseq[:, 0:NC], in1=GT, op=ALU.subtract), [("tsm", "g"), ("tsm", "m")], [("tsm", "d")])
            P.act(lambda e: e.activation(out=dT, in_=dT, func=AF.Exp), [("tsm", "d")], [("tsm", "d")])
            P.dve(lambda e: e.tensor_copy(out=mT, in_=mseq[:, NC:NC + nseq]), [("tsm", "m")], [mkey])
            if L > 4:
                P.dve(lambda e: e.tensor_copy(out=gx[:, 0, 0:NT * Tt].rearrange("p (s l) -> p s l", s=NC),
                                              in_=GT.unsqueeze(2).to_broadcast([4, NC, L])), [("tsm", "g")], [("gx", 0)])
                P.dve(lambda e: e.tensor_copy(out=gx[:, 1, 0:NT * Tt].rearrange("p (s l) -> p s l", s=NC),
                                              in_=dT.unsqueeze(2).to_broadcast([4, NC, L])), [("tsm", "d")], [("gx", 1)])
            else:
                for l in range(L):
                    P.dve(lambda e, l=l: e.tensor_copy(out=gx[:, 0, 0:NT * Tt].rearrange("p (s l) -> p s l", s=NC)[:, :, l], in_=GT),
                          [("tsm", "g")], [("gx", 0)])
                    P.dve(lambda e, l=l: e.tensor_copy(out=gx[:, 1, 0:NT * Tt].rearrange("p (s l) -> p s l", s=NC)[:, :, l], in_=dT),
                          [("tsm", "d")], [("gx", 1)])
            for t in range(NT):
                P.pe(lambda e, t=t: e.matmul(bB[0:Tt, t * 8:t * 8 + 4], lhsT=gx[:, 0, t * Tt:(t + 1) * Tt], rhs=identf[0:4, 0:4],
                                             start=True, stop=True), [("gx", 0), "identf"], ["pb6"])
                P.pe(lambda e, t=t: e.matmul(bB[0:Tt, t * 8 + 4:t * 8 + 8], lhsT=gx[:, 1, t * Tt:(t + 1) * Tt], rhs=identf[0:4, 0:4],
                                             start=True, stop=True), [("gx", 1), "identf"], ["pb6"])
            bBv = bB[0:Tt, 0:8 * NT].rearrange("p (a b) -> p a b", a=NT)
            P.dve(lambda e: e.tensor_tensor(out=tk[0:Tt, 0:NT, 0:4], in0=av[0:Tt, 0:NT, :], in1=bBv[:, :, 0:4], op=ALU.subtract),
                  ["av", "pb6"], [("tk", 0)])
            P.dve(lambda e: e.tensor_tensor(out=tk[0:Tt, 0:NT, 4:8], in0=bneg[0:Tt, 0:NT, :], in1=bBv[:, :, 0:4], op=ALU.subtract),
                  ["bneg", "pb6"], [("tk", 1)])
            P.dve(lambda e: e.tensor_copy(out=wi[0:Tt, 0:NT, :], in_=bBv[:, :, 4:8]), ["pb6"], ["wi"])
            P.act(lambda e: e.activation(out=ev[0:Tt, 0:NT, :], in_=tk[0:Tt, 0:NT, 0:4], func=AF.Exp), [("tk", 0)], ["ev"])
            P.act(lambda e: e.activation(out=fl[0:Tt, 0:NT, :], in_=tk[0:Tt, 0:NT, 4:8], func=AF.Exp), [("tk", 1)], ["fl"])
            for pr_ in range(2):
                P.dve(lambda e, pr_=pr_: e.tensor_scalar(out=rhsd[:, 0:NC, pr_], in0=dT, scalar1=sel2[:, pr_:pr_ + 1], scalar2=None,
                                                         op0=ALU.mult), [("tsm", "d"), "sel2"], ["rhsd"])
            P.pe(lambda e: e.matmul(bA[:, 0:2 * NC], lhsT=esel, rhs=rhsd[:, 0:NC, :].rearrange("p s t -> p (s t)"),
                                    start=True, stop=True), ["rhsd", "esel"], ["pb5"])
            P.dve(lambda e: e.tensor_copy(out=dstt[:, 0:NC, :].rearrange("p s t -> p (s t)"), in_=bA[:, 0:2 * NC]),
                  ["pb5"], ["dstt"])

        def ktm_stage(Tt, NT, sample, ctx):
            ktm_, ktmk = ctx["ktm"], ctx["ktmk"]
            trv, trk = ctx.get("trv", (ptr, "ptr"))
            if sample:
                P.pool(lambda e: e.memset(ktm_, 0.0), [], [ktmk])
            for t in range(NT):
                for c in range(2):
                    P.pe(lambda e, c=c, t=t: e.transpose(out=trv[0:Tt, 2 * t + c, :], in_=kT2[:, c, t * Tt:(t + 1) * Tt], identity=ident),
                         ["kT2", "ident"], [trk])
            P.act(lambda e: e.copy(out=ktm_[0:Tt, 0:NT].rearrange("p t c f -> p (t c) f"), in_=trv[0:Tt, 0:2 * NT, :]), [trk], [ktmk])

        def mlstm_tile(Tt, nseq, c0, ti, full, msk, mskkey, sample, last_tile, yt, ytkey, ctx):
            cols = slice(c0, c0 + Tt)
            IB = [1, 2]
            vk = ctx["vk"][ti]
            vaug_t = ctx["vaug"][ti]
            ktm_, ktmk = ctx["ktm"], ctx["ktmk"]
            PU = ctx["pu"]
            ev_t = ev[0:Tt, ti, :]
            wi_t = wi[0:Tt, ti, :]
            fl_t = fl[0:Tt, ti, :]
            P.dve(lambda e: e.tensor_tensor(out=evaug[0:Tt], in0=vaug_t[0:Tt],
                                            in1=ev_t.unsqueeze(2).to_broadcast([Tt, 4, 129]), op=ALU.mult),
                  [vk, "ev"], ["evaug"])
            if full:
                for h in range(4):
                    r0 = (h % 2) * 64
                    P.pe(lambda e, h=h: e.matmul(pb[0][0:Tt, h * Tt:(h + 1) * Tt], lhsT=kT2[:, h // 2, cols],
                                                 rhs=qz[:, h, cols], start=True, stop=True),
                         ["kT2", "qz"], ["pb0"])
                for h in range(4):
                    P.dve(lambda e, h=h: e.scalar_tensor_tensor(out=wT[0:Tt, h, 0:Tt], in0=pb[0][0:Tt, h * Tt:(h + 1) * Tt],
                                                                scalar=ev_t[:, h:h + 1], in1=msk[0:Tt, 0, :],
                                                                op0=ALU.mult, op1=ALU.mult), ["pb0", "ev", mskkey], ["wT"])
                for h in range(4):
                    P.pe(lambda e, h=h: e.matmul(pb[1 + h // 2][0:Tt, (h % 2) * 129:(h % 2) * 129 + 129], lhsT=wT[0:Tt, h, 0:Tt],
                                                 rhs=vaug_t[0:Tt, h, :], start=True, stop=True),
                         ["wT", vk], ["pb%d" % (1 + h // 2)])
                for pr in range(2):
                    P.act(lambda e, pr=pr: e.copy(out=tot[0:Tt, 2 * pr:2 * pr + 2, :].rearrange("p a b -> p (a b)"),
                                                  in_=pb[1 + pr][0:Tt, 0:258]), ["pb%d" % (1 + pr)], ["tot"])
                if sample:
                    for pr in range(2):
                        for hp in range(2):
                            P.dve(lambda e, pr=pr, hp=hp: e.tensor_tensor(
                                out=qm[:, hp], in0=qz[:, 2 * pr + hp, cols].unsqueeze(1).to_broadcast([128, 16, 64]),
                                in1=colsel, op=ALU.mult), ["qz", "colsel"], ["qm"])
                        for s in range(nseq):
                            cb = Cbr[s % 4]
                            cbk = "Cbr%d" % (s % 4)
                            P.act(lambda e, s=s, cb=cb: e.copy(out=cb, in_=Css[:, s]), ["Css"], [cbk])
                            for hp in range(2):
                                P.pe(lambda e, pr=pr, hp=hp, s=s, cb=cb: e.matmul(
                                    pb[IB[pr]][0:Tt, hp * 129:hp * 129 + 129], lhsT=qm[:, hp, s, :],
                                    rhs=cb[:, pr, :], start=(s == 0 and hp == 0), stop=(s == nseq - 1),
                                    skip_group_check=True),
                                    ["qm", cbk], ["pb%d" % IB[pr]])
                else:
                    for h in range(4):
                        r0 = (h % 2) * 64
                        P.pe(lambda e, h=h: e.matmul(
                            pb[IB[h // 2]][0:Tt, (h % 2) * 129:(h % 2) * 129 + 129], lhsT=qz[:, h, cols],
                            rhs=Cbf[:, h // 2, :], start=True, stop=True), ["qz", "Cbf"], ["pb%d" % IB[h // 2]])
                for h in range(4):
                    P.dve(lambda e, h=h: e.scalar_tensor_tensor(
                        out=tot[0:Tt, h, :], in0=pb[IB[h // 2]][0:Tt, (h % 2) * 129:(h % 2) * 129 + 129],
                        scalar=wi_t[:, h:h + 1], in1=tot[0:Tt, h, :], op0=ALU.mult, op1=ALU.add),
                        ["pb%d" % IB[h // 2], "wi", "tot"], ["tot"])
                dd = sm[0:Tt, 8:12]
                rr = sm[0:Tt, 12:16]
                ssq = sm[0:Tt, 16:20]
                t1 = sm[0:Tt, 20:24]
                scl = sm[0:Tt, 24:28]
                P.dve(lambda e: e.tensor_scalar(out=dd, in0=tot[0:Tt, :, 128], scalar1=-1.0, scalar2=None, op0=ALU.mult),
                      ["tot"], [("sm", "dd")])
                P.dve(lambda e: e.tensor_tensor(out=dd, in0=dd, in1=tot[0:Tt, :, 128], op=ALU.max),
                      ["tot", ("sm", "dd")], [("sm", "dd")])
                P.dve(lambda e: e.tensor_tensor(out=dd, in0=dd, in1=fl_t, op=ALU.max),
                      [("sm", "dd"), "fl"], [("sm", "dd")])
                P.dve(lambda e: e.reciprocal(out=rr, in_=dd), [("sm", "dd")], [("sm", "rr")])
                for h in range(4):
                    P.act(lambda e, h=h: e.activation(out=junk[0:Tt, 0:128], in_=tot[0:Tt, h, 0:128], func=AF.Square,
                                                      accum_out=sm[0:Tt, 16 + h:17 + h]), ["tot"], ["junk", ("sm", "ssq")])
                P.dve(lambda e: e.tensor_tensor(out=t1, in0=rr, in1=rr, op=ALU.mult), [("sm", "rr")], [("sm", "t1")])
                P.dve(lambda e: e.tensor_tensor(out=t1, in0=t1, in1=ssq, op=ALU.mult), [("sm", "t1"), ("sm", "ssq")], [("sm", "t1")])
                rms_rstd(t1, t1, 128, [("sm", "t1")], ("sm", "t1"))
                P.dve(lambda e: e.tensor_tensor(out=scl, in0=t1, in1=rr, op=ALU.mult), [("sm", "t1"), ("sm", "rr")], [("sm", "scl")])
                P.dve(lambda e: e.scalar_tensor_tensor(out=gs[0:Tt, :], in0=sig[ti][0:Tt, :], scalar=1.0, in1=g_ml[0:Tt, :],
                                                       op0=ALU.add, op1=ALU.mult), ["sig%d" % ti, "g_ml"], ["gs"])
                for h in range(4):
                    P.dve(lambda e, h=h: e.scalar_tensor_tensor(
                        out=yt[0:Tt, h * 128:(h + 1) * 128], in0=tot[0:Tt, h, 0:128], scalar=sm[0:Tt, 24 + h:25 + h],
                        in1=gs[0:Tt, h * 128:(h + 1) * 128], op0=ALU.mult, op1=ALU.mult),
                        ["tot", ("sm", "scl"), "gs"], [(ytkey, 0)])
            for s in range(nseq):
                if sample:
                    P.dve(lambda e, s=s: e.tensor_scalar(out=evm[0:Tt], in0=evaug[0:Tt], scalar1=rowsel[:, s:s + 1], scalar2=None,
                                                         op0=ALU.mult), ["evaug", "rowsel"], ["evm"])
                    rsrc, rkey = evm, "evm"
                else:
                    rsrc, rkey = evaug, "evaug"
                for pr in range(2):
                    P.pe(lambda e, pr=pr, rsrc=rsrc: e.matmul(pb[PU][:, 0:258], lhsT=ktm_[:, ti, pr, :],
                                                             rhs=rsrc[:, 2 * pr:2 * pr + 2, :].rearrange("p a b -> p (a b)"),
                                                             start=True, stop=True), [ktmk, rkey], ["pb%d" % PU])
                    for hp in range(2):
                        r0 = hp * 64
                        if sample:
                            cs = Css[r0:r0 + 64, s, pr, :]
                            ckey = "Css"
                        else:
                            cs = Cst[r0:r0 + 64, pr, :]
                            ckey = "Cst"
                        P.dve(lambda e, cs=cs, r0=r0, hp=hp, s=s, pr=pr: e.scalar_tensor_tensor(
                            out=cs, in0=cs, scalar=dstt[r0:r0 + 64, ti * nseq + s, pr:pr + 1], in1=pb[PU][r0:r0 + 64, hp * 129:hp * 129 + 129],
                            op0=ALU.mult, op1=ALU.add), [ckey, "dstt", "pb%d" % PU], [ckey])
            if (not sample) and (not last_tile) and ctx.get("need_cbf", True):
                P.act(lambda e: e.copy(out=Cbf, in_=Cst), ["Cst"], ["Cbf"])

        def attn_tile(Tt, c0, sample, kprev, kprev_key, kcur, kcur_key, vprev, vprev_key, vcur, vcur_key, tbl, tblkey,
                      first_tile, yt, ytkey, split=False):
            NK = 128 + Tt
            cols = slice(c0, c0 + Tt)
            RES0 = {"sb": (3, 4), "tr": ptr, "trk": "ptr", "pv": 3, "sc": sc, "sck": "sc", "pex": pex, "pexk": "pex",
                    "pT": pT, "pTk": "pT", "smo": 0, "smt": "k0"}
            RES1 = {"sb": (5, 6), "tr": pb[6][:, :].bitcast(BF16).rearrange("p (a b) -> p a b", a=8), "trk": "pb6", "pv": 5,
                    "sc": sc2, "sck": "sc2", "pex": pex2, "pexk": "pex2", "pT": pT2, "pTk": "pT2", "smo": 52, "smt": "k1"}

            def do_kv(kv, res):
                r0 = kv * 64
                SB = res["sb"]
                sc_, sck, pex_, pexk, pT_, pTk = res["sc"], res["sck"], res["pex"], res["pexk"], res["pT"], res["pTk"]
                trv, trk, PV, so, smt = res["tr"], res["trk"], res["pv"], res["smo"], res["smt"]
                for j in range(4):
                    bank = pb[SB[j // 2]]
                    o0 = (j % 2) * 256
                    bkey = "pb%d" % SB[j // 2]
                    if sample:
                        qa = qam[j % 2]
                        qak = "qam%d" % (j % 2)
                        P.pool(lambda e, qa=qa: e.memset(qa[(1 - kv) * 64:(2 - kv) * 64], 0.0), [], [qak])
                        P.dve(lambda e, j=j, qa=qa: e.tensor_tensor(
                            out=qa[r0:r0 + 64], in0=qaT[r0:r0 + 64, j, cols].unsqueeze(1).to_broadcast([64, 16, 64]),
                            in1=colsel[r0:r0 + 64], op=ALU.mult), ["qaT", "colsel"], [qak])
                        for s in range(16):
                            P.pe(lambda e, s=s, bank=bank, o0=o0, qa=qa: e.matmul(
                                bank[0:Tt, o0:o0 + 128], lhsT=qa[:, s, :], rhs=kcT[:, s, :],
                                start=(s == 0), stop=(s == 15)), [qak, "kcT"], [bkey])
                    else:
                        P.pe(lambda e, j=j, bank=bank, o0=o0: e.matmul(bank[0:Tt, o0:o0 + 128], lhsT=qaT[:, j, cols],
                                                                       rhs=kprev[kv], start=True, stop=True),
                             ["qaT", kprev_key[kv]], [bkey])
                    P.pe(lambda e, j=j, bank=bank, o0=o0: e.matmul(bank[0:Tt, o0 + 128:o0 + 128 + Tt], lhsT=qaT[:, j, cols],
                                                                   rhs=kcur[kv], start=True, stop=True),
                         ["qaT", kcur_key[kv]], [bkey])
                for half in range(2):
                    P.dve(lambda e, half=half: e.tensor_tensor(
                        out=sc_[0:Tt, 2 * half:2 * half + 2, 0:NK],
                        in0=pb[SB[half]][0:Tt, :].rearrange("p (a b) -> p a b", a=2)[:, :, 0:NK],
                        in1=tbl[0:Tt, 4 * kv + 2 * half:4 * kv + 2 * half + 2, 0:NK], op=ALU.add),
                        ["pb%d" % SB[half], tblkey], [sck])
                if first_tile:
                    P.dve(lambda e: e.tensor_scalar(out=sc_[0:Tt, :, 0:128], in0=sc_[0:Tt, :, 0:128], scalar1=prevb[0:Tt, 0:1],
                                                    scalar2=None, op0=ALU.add), [sck, "prevb"], [sck])
                mx = sm[0:Tt, 32 + so:36 + so]
                nmx = sm[0:Tt, 36 + so:40 + so]
                ssum = sm[0:Tt, 40 + so:44 + so]
                es = sm[0:Tt, 44 + so:48 + so]
                P.dve(lambda e: e.tensor_reduce(out=mx, in_=sc_[0:Tt, :, 0:NK], axis=AX.X, op=ALU.max), [sck], [("sm", "mx", smt)])
                P.dve(lambda e: e.tensor_tensor(out=mx, in0=mx, in1=sinks[0:Tt, 4 * kv:4 * kv + 4], op=ALU.max),
                      [("sm", "mx", smt), "sinks"], [("sm", "mx", smt)])
                P.dve(lambda e: e.tensor_scalar(out=nmx, in0=mx, scalar1=-1.0, scalar2=None, op0=ALU.mult),
                      [("sm", "mx", smt)], [("sm", "nmx", smt)])
                for j in range(4):
                    P.act(lambda e, j=j: e.activation(out=pex_[0:Tt, j, 0:NK], in_=sc_[0:Tt, j, 0:NK], func=AF.Exp,
                                                      bias=sm[0:Tt, 36 + so + j:37 + so + j], accum_out=sm[0:Tt, 40 + so + j:41 + so + j]),
                          [sck, ("sm", "nmx", smt)], [pexk, ("sm", "ssum", smt)])
                P.dve(lambda e: e.tensor_tensor(out=es, in0=sinks[0:Tt, 4 * kv:4 * kv + 4], in1=nmx, op=ALU.add),
                      ["sinks", ("sm", "nmx", smt)], [("sm", "es", smt)])
                P.act(lambda e: e.activation(out=es, in_=es, func=AF.Exp), [("sm", "es", smt)], [("sm", "es", smt)])
                P.dve(lambda e: e.tensor_tensor(out=ssum, in0=ssum, in1=es, op=ALU.add), [("sm", "ssum", smt), ("sm", "es", smt)],
                      [("sm", "ssum", smt)])
                P.dve(lambda e: e.reciprocal(out=ssum, in_=ssum), [("sm", "ssum", smt)], [("sm", "ssum", smt)])
                for j in range(4):
                    P.pe(lambda e, j=j: e.transpose(out=trv[:, 2 * j, 0:Tt], in_=pex_[0:Tt, j, 0:128], identity=ident[0:Tt, 0:Tt]),
                         [pexk, "ident"], [trk])
                    P.pe(lambda e, j=j: e.transpose(out=trv[0:Tt, 2 * j + 1, 0:Tt], in_=pex_[0:Tt, j, 128:128 + Tt],
                                                    identity=ident[0:Tt, 0:Tt]), [pexk, "ident"], [trk])
                P.act(lambda e: e.copy(out=pT_[:, :, 0:Tt], in_=trv[:, :, 0:Tt]), [trk], [pTk])
                for j in range(4):
                    oap = pb[PV][0:Tt, j * 64:(j + 1) * 64]
                    if sample:
                        P.dve(lambda e, j=j: e.tensor_tensor(out=pTm, in0=pT_[:, 2 * j, 0:64].unsqueeze(1).to_broadcast([128, 16, 64]),
                                                             in1=colsel, op=ALU.mult), [pTk, "colsel"], ["pTm"])
                        for s in range(16):
                            P.pe(lambda e, s=s, oap=oap: e.matmul(oap, lhsT=pTm[:, s, :], rhs=vcb[:, s, r0:r0 + 64],
                                                                  start=(s == 0), stop=False), ["pTm", "vcb"], ["pb%d" % PV])
                    else:
                        P.pe(lambda e, j=j, oap=oap: e.matmul(oap, lhsT=pT_[:, 2 * j, 0:Tt], rhs=vprev[:, kv, :], start=True, stop=False),
                             [pTk, vprev_key], ["pb%d" % PV])
                    P.pe(lambda e, j=j, oap=oap: e.matmul(oap, lhsT=pT_[0:Tt, 2 * j + 1, 0:Tt], rhs=vcur[0:Tt, kv, :],
                                                          start=False, stop=True), [pTk, vcur_key], ["pb%d" % PV])
                P.dve(lambda e: e.tensor_tensor(out=att[0:Tt, kv * 256:(kv + 1) * 256].rearrange("p (a b) -> p a b", a=4),
                                                in0=pb[PV][0:Tt, 0:256].rearrange("p (a b) -> p a b", a=4),
                                                in1=ssum.unsqueeze(2).to_broadcast([Tt, 4, 64]), op=ALU.mult),
                      ["pb%d" % PV, ("sm", "ssum", smt)], [("att", kv)])

            def tail():
                P.act(lambda e: e.activation(out=junk[0:Tt, 0:512], in_=att[0:Tt, :], func=AF.Square, accum_out=sm[0:Tt, 48:49]),
                      [("att", 0), ("att", 1)], ["junk", ("sm", "a0")])
                rms_rstd(sm[0:Tt, 48:49], sm[0:Tt, 49:50], 512, [("sm", "a0")], ("sm", "a1"))
                P.dve(lambda e: e.scalar_tensor_tensor(out=yt[0:Tt, 512:1024], in0=att[0:Tt, :], scalar=sm[0:Tt, 49:50],
                                                       in1=g_at[0:Tt, :], op0=ALU.mult, op1=ALU.mult),
                      [("att", 0), ("att", 1), ("sm", "a1"), "g_at"], [(ytkey, 1)])

            if split:
                return (lambda: do_kv(0, RES0)), (lambda: do_kv(1, RES1)), tail
            do_kv(0, RES0)
            do_kv(1, RES0)
            tail()
            return None

        def resid_norm_add(Tt, x, xkey, srcs, skeys, gain, gkey):
            for half in range(2):
                P.act(lambda e, half=half: e.activation(out=junk[0:Tt, 0:512], in_=srcs[half][0:Tt, 0:512], func=AF.Square,
                                                        accum_out=sm[0:Tt, 52 + half:53 + half]), [skeys[half]],
                      ["junk", ("sm", "r%d" % half)])
            P.dve(lambda e: e.tensor_tensor(out=sm[0:Tt, 54:55], in0=sm[0:Tt, 52:53], in1=sm[0:Tt, 53:54], op=ALU.add),
                  [("sm", "r0"), ("sm", "r1")], [("sm", "r2")])
            rms_rstd(sm[0:Tt, 54:55], sm[0:Tt, 55:56], D, [("sm", "r2")], ("sm", "r3"))
            for half in range(2):
                P.dve(lambda e, half=half: e.scalar_tensor_tensor(
                    out=tmp[0:Tt, half * 512:(half + 1) * 512], in0=srcs[half][0:Tt, 0:512], scalar=sm[0:Tt, 55:56],
                    in1=gain[0:Tt, half * 512:(half + 1) * 512], op0=ALU.mult, op1=ALU.mult),
                    [skeys[half], ("sm", "r3"), gkey], [("tmp", half)])
            P.dve(lambda e: e.tensor_tensor(out=x, in0=x, in1=tmp[0:Tt, :], op=ALU.add), [xkey, ("tmp", 0), ("tmp", 1)], [xkey])

        def wout_group(Tt, NT, xs_, xkeys):
            for i in range(NT):
                transpose_to(ytm[i], [("ytm%d" % i, 0), ("ytm%d" % i, 1)], Tt, xT, i * Tt, ("xT", i))
            banks0 = [(pb[i], "pb%d" % i) for i in range(4)]
            banks1 = [(pb[4], "pb4"), (pb[5], "pb5"), (pb[6], "pb6"), (pb[0], "pb0")]
            for half, banks in ((0, banks0), (1, banks1)):
                b = next_wu(w_out_v[:, :, half * 512:(half + 1) * 512], ("wo", half))
                for i in range(NT):
                    bk, bkey = banks[i]
                    for kc in range(8):
                        P.pe(lambda e, i=i, kc=kc, b=b, bk=bk: e.matmul(bk[0:Tt, :], lhsT=xT[:, kc, i * Tt:(i + 1) * Tt],
                                                                       rhs=wu[b][:, kc, :], start=(kc == 0), stop=(kc == 7)),
                             [("xT", i)] + WUK(b), [bkey])
                    if half == 0:
                        P.act(lambda e, i=i, bk=bk: e.copy(out=f1[0:Tt, i, :], in_=bk[0:Tt, :]), [bkey], [("f1", i)])
            resid_group(Tt, NT, xs_, xkeys, [f1[:, i, :] for i in range(NT)], [("f1", i) for i in range(NT)],
                        [banks1[i][0] for i in range(NT)], [banks1[i][1] for i in range(NT)], g_post, "g_post")

        def ffn_group(Tt, NT, xs_, xkeys):
            GN = Tt * NT
            norm_group(Tt, NT, xs_, xkeys, g_fpre, "g_fpre")
            xTkeys = [("xT", i) for i in range(NT)]
            fbank = [pb[0], pb[1], pb[2], pb[3]]
            fkeys = ["pb0", "pb1", "pb2", "pb3"]

            def f_mm(j, wslot, banks, keys):
                wbuf, wkey = wslot
                for i in range(NT):
                    P.pe(lambda e, j=j, i=i, wbuf=wbuf: e.matmul(banks[i][0:Tt, :], lhsT=hT[:, j, i * Tt:(i + 1) * Tt],
                                                                 rhs=wbuf[:, j % 4, :], start=(j == 0), stop=(j == 31),
                                                                 skip_group_check=True),
                         [("hT", j), wkey], [keys[i]])

            pending = None
            for s in range(8):
                b = next_wu(w_up_v[:, :, s * 512:(s + 1) * 512], ("up", s))
                bd = s % 2
                load_wd(wd[bd], ("wd", bd), w_dn_v[:, 4 * s:4 * s + 4, 0:512], s)
                for jc in range(4):
                    j = 4 * s + jc
                    hb = pb[5 + (jc % 2)]
                    hk = "pb%d" % (5 + (jc % 2))
                    for kc in range(8):
                        P.pe(lambda e, jc=jc, kc=kc, b=b, hb=hb: e.matmul(hb[:, 0:GN], lhsT=wu[b][:, kc, jc * 128:(jc + 1) * 128],
                                                                          rhs=xT[:, kc, 0:GN], start=(kc == 0), stop=(kc == 7)),
                             xTkeys + WUK(b), [hk])
                    if pending is not None:
                        f_mm(pending[0], pending[1], fbank, fkeys)
                    P.act(lambda e, hb=hb: e.activation(out=hr[:, 0:GN], in_=hb[:, 0:GN], func=AF.Relu), [hk], ["hr"])
                    P.dve(lambda e, j=j: e.tensor_tensor(out=hT[:, j, 0:GN], in0=hr[:, 0:GN], in1=hr[:, 0:GN], op=ALU.mult),
                          ["hr"], [("hT", j)])
                    pending = (j, (wd[bd], ("wd", bd)))
            f_mm(pending[0], pending[1], fbank, fkeys)
            for i in range(NT):
                P.act(lambda e, i=i: e.copy(out=f1[0:Tt, i, :], in_=fbank[i][0:Tt, :]), [fkeys[i]], [("f1", i)])
            f2bank = [pb[4], pb[5], pb[6], pb[0]]
            f2keys = ["pb4", "pb5", "pb6", "pb0"]
            slots2 = [(wd[0], ("wd", 0)), (wd[1], ("wd", 1)), (wu[0][:, 0:4, :], ("wu", 0, 0)), (wu[0][:, 4:8, :], ("wu", 0, 1)),
                      (wu[1][:, 0:4, :], ("wu", 1, 0)), (wu[1][:, 4:8, :], ("wu", 1, 1))]
            for s in range(8):
                wbuf, wkey = slots2[s % 6]
                load_wd(wbuf, wkey, w_dn_v[:, 4 * s:4 * s + 4, 512:1024], 8 + s)
                for jc in range(4):
                    f_mm(4 * s + jc, (wbuf, wkey), f2bank, f2keys)
            resid_group(Tt, NT, xs_, xkeys, [f1[:, i, :] for i in range(NT)], [("f1", i) for i in range(NT)],
                        f2bank[0:NT], f2keys[0:NT], g_fpost, "g_fpost")

        def pgate_group(Tt, NT, xs_, xkeys, p_srcs, y_dsts):
            for i in range(NT):
                P.dma("pool", lambda e, i=i: e.dma_start(out=ptb[0:Tt, i, :], in_=p_srcs[i]), [], [("ptb", i)])
                P.dve(lambda e, i=i: e.tensor_copy(out=xnb[0:Tt, :], in_=xs_[i]), [xkeys[i]], ["xnb"])
                transpose_to(xnb, ["xnb"], Tt, xT, i * Tt, ("xT", i))
                transpose_to(ptb[:, i, :], [("ptb", i)], Tt, ppT, i * Tt, ("ppT", i), nk=2)
            for half in range(2):
                b = next_wu(w_pg_v[:, :, half * 512:(half + 1) * 512], ("pg", half))
                for i in range(NT):
                    gb, gk_ = pb[i % 2], "pb%d" % (i % 2)
                    qb, qk_ = pb[2 + i % 2], "pb%d" % (2 + i % 2)
                    for kc in range(8):
                        P.pe(lambda e, i=i, kc=kc, b=b, gb=gb: e.matmul(gb[0:Tt, :], lhsT=xT[:, kc, i * Tt:(i + 1) * Tt],
                                                                       rhs=wu[b][:, kc, :], start=(kc == 0), stop=(kc == 7)),
                             [("xT", i)] + WUK(b), [gk_])
                    for kc in range(2):
                        P.pe(lambda e, i=i, kc=kc, half=half, qb=qb: e.matmul(qb[0:Tt, :], lhsT=ppT[:, kc, i * Tt:(i + 1) * Tt],
                                                                             rhs=w_pp_sb[:, kc, half * 512:(half + 1) * 512],
                                                                             start=(kc == 0), stop=(kc == 1)),
                             [("ppT", i), "w_pp"], [qk_])
                    th = i % 2
                    tv = tmp[0:Tt, th * 512:(th + 1) * 512]
                    P.act(lambda e, gb=gb, tv=tv: e.activation(out=tv, in_=gb[0:Tt, :], func=AF.Tanh, scale=0.5), [gk_], [("tmp", th)])
                    P.dve(lambda e, qb=qb, tv=tv: e.scalar_tensor_tensor(out=tv, in0=tv, scalar=1.0, in1=qb[0:Tt, :],
                                                                         op0=ALU.add, op1=ALU.mult),
                          [("tmp", th), qk_], [("tmp", th)])
                    xv = xs_[i][:, half * 512:(half + 1) * 512]
                    P.dve(lambda e, xv=xv, tv=tv: e.scalar_tensor_tensor(out=xv, in0=tv, scalar=0.5, in1=xv, op0=ALU.mult, op1=ALU.add),
                          [xkeys[i], ("tmp", th)], [xkeys[i]])
            for i in range(NT):
                P.dma("sp", lambda e, i=i: e.dma_start(out=y_dsts[i], in_=xs_[i]), [xkeys[i]], [])

        def win_group(Tt, NT, chunks, tok_tiles, sample, kslot, vslots, ctx):
            GN = Tt * NT
            xTd = ctx.get("xT", xT)
            xTn = ctx.get("xTn", "xT")
            xTkeys = [(xTn, i) for i in range(NT)]
            for n, c in enumerate(chunks):
                bank = pb[n % 2]
                bkey = "pb%d" % (n % 2)
                if c < 2:
                    wsrc, c0, wkeys = w_rest, c * 128, W_REST_KEYS
                elif c < 4:
                    wsrc, c0, wkeys = w_res, (c - 2) * 128, W_RES_KEYS
                elif c < 8:
                    wsrc, c0, wkeys = w_rest, 768 + (c - 4) * 128, W_REST_KEYS
                else:
                    wsrc, c0, wkeys = w_rest, 1280, W_REST_KEYS
                for kc in range(8):
                    P.pe(lambda e, kc=kc, c0=c0, bank=bank, wsrc=wsrc: e.matmul(bank[:, 0:GN], lhsT=wsrc[:, kc, c0:c0 + 128],
                                                                                 rhs=xTd[:, kc, 0:GN], start=(kc == 0), stop=(kc == 7)),
                         xTkeys + wkeys, [bkey])
                if c < 4:
                    if sample:
                        P.act(lambda e, c=c, bank=bank: e.copy(out=raws[:, c, :, 3:7], in_=bank[:, 0:64].rearrange("p (s l) -> p s l", s=16)),
                              [bkey], ["raws"])
                    else:
                        P.act(lambda e, c=c, bank=bank: e.copy(out=raw[:, c, 3:515], in_=bank[:, 0:512]), [bkey], ["raw"])
                elif c < 8:
                    P.act(lambda e, c=c, bank=bank: e.activation(out=qaT[:, c - 4, 0:GN], in_=bank[:, 0:GN], func=AF.Copy, scale=0.125),
                          [bkey], ["qaT"])
                else:
                    for kv in range(2):
                        P.act(lambda e, bank=bank, kv=kv: e.copy(out=kaT[kslot][kv][kv * 64:(kv + 1) * 64, 0:GN],
                                                                 in_=bank[kv * 64:(kv + 1) * 64, 0:GN]), [bkey], ["kaT%d_%d" % (kslot, kv)])
            for i in range(NT):
                parts = tok_tiles.get(i, ())
                xk = [(xTn, i)]
                if "v" in parts:
                    for kc in range(8):
                        P.pe(lambda e, kc=kc, i=i: e.matmul(pb[2][0:Tt, :], lhsT=xTd[:, kc, i * Tt:(i + 1) * Tt], rhs=w_res[:, kc, 256:768],
                                                            start=(kc == 0), stop=(kc == 7)), xk + W_RES_KEYS, ["pb2"])
                    P.act(lambda e, i=i: e.copy(out=ctx["vaug"][i][0:Tt, :, 0:128], in_=pb[2][0:Tt, :].rearrange("p (h v) -> p h v", h=4)),
                          ["pb2"], [ctx["vk"][i]])
                    P.pool(lambda e, i=i: e.memset(ctx["vaug"][i][0:Tt, :, 128:129], 1.0), [], [ctx["vk"][i]])
                if "o" in parts:
                    for kc in range(8):
                        P.pe(lambda e, kc=kc, i=i: e.matmul(pb[3][0:Tt, :], lhsT=xTd[:, kc, i * Tt:(i + 1) * Tt], rhs=w_rest[:, kc, 256:768],
                                                            start=(kc == 0), stop=(kc == 7)), xk + W_REST_KEYS, ["pb3"])
                    P.act(lambda e, i=i: e.activation(out=sig[i][0:Tt, :], in_=pb[3][0:Tt, :], func=AF.Tanh, scale=0.5), ["pb3"], ["sig%d" % i])
                if "g" in parts:
                    for kc in range(8):
                        P.pe(lambda e, kc=kc, i=i: e.matmul(pb[4][0:Tt, 0:8], lhsT=xTd[:, kc, i * Tt:(i + 1) * Tt], rhs=w_res[:, kc, 768:776],
                                                            start=(kc == 0), stop=(kc == 7)), xk + W_RES_KEYS, ["pb4"])
                    P.dve(lambda e, i=i: e.tensor_tensor(out=ctx["gt"][0:Tt, i, :], in0=pb[4][0:Tt, 0:8], in1=bgate[0:Tt, :], op=ALU.add),
                          ["pb4", "bgate"], [(ctx["gk"], i)])
                if "kv" in parts:
                    vs = vslots[i]
                    for kc in range(8):
                        P.pe(lambda e, kc=kc, i=i: e.matmul(pb[2][0:Tt, 0:256], lhsT=xTd[:, kc, i * Tt:(i + 1) * Tt],
                                                            rhs=w_rest[:, kc, 1280:1536], start=(kc == 0), stop=(kc == 7)),
                             xk + W_REST_KEYS, ["pb2"])
                    P.act(lambda e, vs=vs: e.copy(out=va[vs][0:Tt, :, :], in_=pb[2][0:Tt, 128:256].rearrange("p (k d) -> p k d", k=2)),
                          ["pb2"], ["va%d" % vs])
                    if "kvout" in parts:
                        P.dve(lambda e: e.tensor_copy(out=kvout[0:Tt, :], in_=pb[2][0:Tt, 0:256]), ["pb2"], ["kvout"])

        def conv_group(chunks, sample, GN):
            for c in chunks:
                if sample:
                    srcs = [raws[:, c, :, j:j + 4] for j in range(4)]
                    accv = cacc[:, 0:64].rearrange("p (s l) -> p s l", s=16)
                    rk = "raws"
                else:
                    srcs = [raw[:, c, j:j + 512] for j in range(4)]
                    accv = cacc[:, 0:512]
                    rk = "raw"
                P.dve(lambda e, c=c, accv=accv, srcs=srcs: e.tensor_scalar(out=accv, in0=srcs[0], scalar1=cw[:, c, 0:1], scalar2=None,
                                                                          op0=ALU.mult), [rk, "cw"], ["cacc"])
                for j in range(1, 4):
                    P.dve(lambda e, c=c, j=j, accv=accv, srcs=srcs: e.scalar_tensor_tensor(
                        out=accv, in0=srcs[j], scalar=cw[:, c, j:j + 1], in1=accv, op0=ALU.mult, op1=ALU.add),
                        [rk, "cw", "cacc"], ["cacc"])
                tsc = 8.0 if c < 2 else 1.0
                P.act(lambda e, tsc=tsc: e.activation(out=csg[:, 0:GN], in_=cacc[:, 0:GN], func=AF.Tanh, scale=tsc), ["cacc"], ["csg"])
                if c < 2:
                    for hp in range(2):
                        r0 = hp * 64
                        P.pool(lambda e, c=c, hp=hp: e.memset(qz[(1 - hp) * 64:(2 - hp) * 64, 2 * c + hp, 0:GN], 0.0), [], ["qz"])
                        P.dve(lambda e, c=c, hp=hp, r0=r0: e.scalar_tensor_tensor(
                            out=qz[r0:r0 + 64, 2 * c + hp, 0:GN], in0=csg[r0:r0 + 64, 0:GN], scalar=1.0, in1=cacc[r0:r0 + 64, 0:GN],
                            op0=ALU.add, op1=ALU.mult), ["cacc", "csg"], ["qz"])
                else:
                    P.dve(lambda e, c=c: e.scalar_tensor_tensor(out=kT2[:, c - 2, 0:GN], in0=csg[:, 0:GN], scalar=1.0, in1=cacc[:, 0:GN],
                                                                op0=ALU.add, op1=ALU.mult), ["cacc", "csg"], ["kT2"])

        CTX_A = {"vaug": vaug, "vk": ["vaug%d" % i for i in range(4)], "gt": gtA, "gk": "gtA", "ktm": ktm, "ktmk": "ktm", "pu": 1}
        CTX_B = {"vaug": vaugB, "vk": ["vaugB%d" % i for i in range(4)], "gt": gtB, "gk": "gtB", "ktm": ktmB, "ktmk": "ktmB", "pu": 3}

        def plain_prefix(g):
            return g < n_prefix - 1

        def group_ctx(g):
            if plain_prefix(g):
                c = dict(CTX_A if g % 2 == 0 else CTX_B)
                c["pu"] = 3
                c["need_cbf"] = False
                if g % 2 == 1:
                    c["xT"], c["xTn"] = xT2, "xT2"
                c["trv"] = (pb[0][:, :].bitcast(BF16).rearrange("p (a b) -> p a b", a=8), "pb0")
                return c
            return CTX_A

        def xset(g):
            if g % 2 == 0:
                return xt, ["xt%d" % i for i in range(4)]
            return xtB, ["xtB%d" % i for i in range(4)]

        def load_x(g):
            X, XK = xset(g)
            for i in range(4):
                P.dma("sp", lambda e, g=g, i=i, X=X: e.dma_start(out=X[i], in_=xs_d[(4 * g + i) * 128:(4 * g + i + 1) * 128, :]),
                      [], [XK[i]])

        def front1(g):
            ctx = group_ctx(g)
            full = g >= n_prefix
            lastpre = (g == n_prefix - 1)
            if (full or lastpre) and not w_rest_state.get("prefetched", False):
                load_w_rest()
            w_rest_state["prefetched"] = False
            X, XK = xset(g)
            if g == 0:
                load_x(0)
            if g + 1 < n_groups:
                load_x(g + 1)
            norm_group(128, 4, X, XK, g_pre, "g_pre", xTd=ctx.get("xT"), xTn=ctx.get("xTn", "xT"))

        def front2(g):
            ctx = group_ctx(g)
            full = g >= n_prefix
            lastpre = (g == n_prefix - 1)
            lastg = (g == n_groups - 1)
            kslot = g % 2
            vslots = {i: (4 * g + i) % 8 for i in range(4)}
            if full:
                chunks = [0, 1, 2, 3, 4, 5, 6, 7, 8]
                toks = {i: {"v", "o", "g", "kv"} for i in range(4)}
                if lastg:
                    toks[3] = toks[3] | {"kvout"}
            elif lastpre:
                chunks = [0, 1, 2, 3, 8]
                toks = {i: {"v", "g"} for i in range(4)}
                toks[3] = {"v", "g", "kv"}
            else:
                chunks = [2, 3]
                toks = {i: {"v", "g"} for i in range(4)}
            win_group(128, 4, chunks, toks, False, kslot, vslots, ctx)

            def conv_part():
                conv_group([0, 1, 2, 3] if full else [2, 3], False, 512)
                if lastg:
                    P.dve(lambda e: e.tensor_copy(out=cvoP, in_=raw[:, :, 512:515]), ["raw"], ["cvoP"])
                    for c in range(4):
                        P.pe(lambda e, c=c: e.matmul(pb[2][0:3, c * 128:(c + 1) * 128], lhsT=cvoP[:, c, :], rhs=identf,
                                                     start=True, stop=True), ["cvoP", "identf"], ["pb2"])
                    P.dve(lambda e: e.tensor_copy(out=cvtP, in_=pb[2][0:3, :]), ["pb2"], ["cvtP"])
                    P.dma("sp", lambda e: e.dma_start(out=conv_p_d, in_=cvtP), ["cvtP"], [])
                P.dve(lambda e: e.tensor_copy(out=raw[:, :, 0:3], in_=raw[:, :, 512:515]), ["raw"], ["raw"])
                ktm_stage(128, 4, False, ctx)

            if full or lastpre:
                interleave(capture(conv_part), capture(lambda: gate_stage(128, 4, 1, mskp, "mskp", mTp[:, 0:1], "mT", ctx)))
            else:
                conv_part()

        def back(g):
            ctx = group_ctx(g)
            full = g >= n_prefix
            lastg = (g == n_groups - 1)
            kslot = g % 2
            if plain_prefix(g):
                gate_stage(128, 4, 1, mskp, "mskp", mTp[:, 0:1], "mT", ctx)
            for i in range(4):
                last_tile = lastg and i == 3
                if not full:
                    mlstm_tile(128, 1, i * 128, i, full, mskp, "mskp", False, last_tile, ytm[i], "ytm%d" % i, ctx)
                    continue
                vs_cur = (4 * g + i) % 8
                vs_prev = (4 * g + i - 1) % 8
                if i == 0:
                    kprev = [kaT[1 - kslot][kv][:, 384:512] for kv in range(2)]
                    kpk = ["kaT%d_%d" % (1 - kslot, kv) for kv in range(2)]
                else:
                    kprev = [kaT[kslot][kv][:, (i - 1) * 128:i * 128] for kv in range(2)]
                    kpk = ["kaT%d_%d" % (kslot, kv) for kv in range(2)]
                bufA = capture(lambda: mlstm_tile(128, 1, i * 128, i, full, mskp, "mskp", False, last_tile, ytm[i], "ytm%d" % i, ctx))
                kv0, kv1, atail = attn_tile(128, i * 128, False, kprev, kpk,
                                            [kaT[kslot][kv][:, i * 128:(i + 1) * 128] for kv in range(2)],
                                            ["kaT%d_%d" % (kslot, kv) for kv in range(2)],
                                            va[vs_prev], "va%d" % vs_prev, va[vs_cur], "va%d" % vs_cur, abias, "abias",
                                            first_tile=(g == n_prefix and i == 0), yt=ytm[i], ytkey="ytm%d" % i, split=True)
                interleave(bufA, capture(kv0), capture(kv1))
                atail()
            if full:
                X, xkeys = xset(g)
                WMODE["save"] = bool(lastg and do_sample)
                wout_group(128, 4, X, xkeys)
                ffn_group(128, 4, X, xkeys)
                if (not lastg) or do_sample:
                    load_w_rest()
                    w_rest_state["prefetched"] = True
                rows = [((g - n_prefix) * 4 + i) * 128 for i in range(4)]
                pgate_group(128, 4, X, xkeys, [pp_d[r:r + 128, :] for r in rows], [y_d[r:r + 128, :] for r in rows])
            if lastg:
                P.dma("sp", lambda e: e.dma_start(out=k_p_d, in_=kvout[:, 0:128]), ["kvout"], [])
                P.dma("sp", lambda e: e.dma_start(out=v_p_d, in_=kvout[:, 128:256]), ["kvout"], [])
                for hp in range(2):
                    P.dma("sp", lambda e, hp=hp: e.dma_start(
                        out=c_p_d.rearrange("(pr hp) d v -> hp d pr v", hp=2)[hp], in_=Cst[hp * 64:(hp + 1) * 64, :, 0:128]),
                        ["Cst"], [])
                    P.dma("sp", lambda e, hp=hp: e.dma_start(
                        out=n_p_d.rearrange("(pr hp) d -> hp d pr", hp=2)[hp], in_=Cst[hp * 64:(hp + 1) * 64, :, 128],
                        allow_slow_non_contiguous=True), ["Cst"], [])
                P.dma("sp", lambda e: e.dma_start(out=m_p_d, in_=mTp[:, 0:1]), ["mT"], [])

        late_scr = n_prefix > 3
        if not late_scr:
            emit_w_rest_scr()
        done1, done2 = set(), set()

        def do1(g):
            if g not in done1:
                front1(g)
                done1.add(g)

        def do2(g):
            do1(g)
            if g not in done2:
                front2(g)
                done2.add(g)

        for g in range(n_groups):
            if late_scr and g == 1:
                emit_w_rest_scr()
            do2(g)
            n1, n2 = g + 1, g + 2
            streams = [lambda: back(g)]
            if plain_prefix(g) and n1 < n_groups and plain_prefix(n1):
                do1(n1)
                if n1 not in done2:
                    streams.append(lambda: front2(n1))
                    done2.add(n1)
                if n2 < n_groups and plain_prefix(n2) and n2 not in done1:
                    streams.append(lambda: front1(n2))
                    done1.add(n2)
            if len(streams) == 1:
                back(g)
            else:
                interleave(*[capture(f) for f in streams])

        if do_sample:
            xs0 = xt0[0:64, :]
            ld("sp", msks, cst["c_msks"], "msks")
            ld("sp", sbias, cst["c_sbias"], "sbias")
            ld("sp", rowsel, cst["c_rowsel"], "rowsel")
            ld("pool", colsel, cst["c_colsel"], "colsel")
            P.pool(lambda e: e.memset(evm, 0.0), [], ["evm"])
            if not w_rest_state.get("prefetched", False):
                load_w_rest()
            w_rest_state["prefetched"] = False
            P.dma("sp", lambda e: e.dma_start(out=xs0, in_=xsm_d), [], ["xt0"])
            P.dma("sp", lambda e: e.dma_start(out=hist, in_=stv_d), [], ["hist"])
            P.dma("sp", lambda e: e.dma_start(out=mTs, in_=stm_d.rearrange("s h -> h s"), allow_slow_non_contiguous=True),
                  [], ["mTs"])
            for hp in range(2):
                P.dma("sp", lambda e, hp=hp: e.dma_start(
                    out=Css[hp * 64:(hp + 1) * 64, :, :, 0:128],
                    in_=stc_d.rearrange("s (pr hp) d v -> hp d s pr v", hp=2)[hp]), [], ["Css"])
                P.dma("sp", lambda e, hp=hp: e.dma_start(
                    out=Css[hp * 64:(hp + 1) * 64, :, :, 128],
                    in_=stn_d.rearrange("s (pr hp) d -> hp d s pr", hp=2)[hp], allow_slow_non_contiguous=True), [], ["Css"])
            P.dma("pool", lambda e: e.dma_start(out=kcb, in_=ck_d.rearrange("s j f -> j s f")), [], ["kcb"])
            P.dma("pool", lambda e: e.dma_start(out=vcb, in_=cv_d.rearrange("s j f -> j s f")), [], ["vcb"])
            for s in range(16):
                P.pe(lambda e, s=s: e.transpose(out=ptr[:, s % 8, :], in_=kcb[:, s, :], identity=ident), ["kcb", "ident"], ["ptr"])
                if s % 8 == 7:
                    P.act(lambda e, s=s: e.copy(out=kcT[:, s - 7:s + 1, :], in_=ptr[:]), ["ptr"], ["kcT"])
            P.dma("sp", lambda e: e.dma_start(out=k_s_d[:, 0:124, :], in_=ck_d[:, 4:128, :]), [], [])
            P.dma("sp", lambda e: e.dma_start(out=v_s_d[:, 0:124, :], in_=cv_d[:, 4:128, :]), [], [])
            for c in range(4):
                P.pe(lambda e, c=c: e.matmul(pb[5][:, c * 48:(c + 1) * 48], lhsT=hist[:, c * 128:(c + 1) * 128], rhs=identf[0:48, 0:48],
                                             start=True, stop=True), ["hist", "identf"], ["pb5"])
            P.dve(lambda e: e.tensor_copy(out=raws[:, :, :, 0:3], in_=pb[5][:, 0:192].rearrange("p (c s l) -> p c s l", c=4, s=16)),
                  ["pb5"], ["raws"])
            norm_transpose(xs0, "xt0", g_pre, "g_pre", 64, 0, ("xT", 0))
            win_group(64, 1, [0, 1, 2, 3, 4, 5, 6, 7, 8], {0: {"v", "o", "g", "kv", "kvout"}}, True, 0, {0: 0}, CTX_A)
            conv_group([0, 1, 2, 3], True, 64)
            P.dve(lambda e: e.tensor_copy(out=cvo.rearrange("p c (s l) -> p c s l", s=16), in_=raws[:, :, :, 4:7]), ["raws"], ["cvo"])
            for c in range(4):
                P.pe(lambda e, c=c: e.matmul(pb[5][0:48, c * 128:(c + 1) * 128], lhsT=cvo[:, c, :], rhs=identf, start=True, stop=True),
                     ["cvo", "identf"], ["pb5"])
            P.dve(lambda e: e.tensor_copy(out=cvt, in_=pb[5][0:48, :]), ["pb5"], ["cvt"])
            P.dma("sp", lambda e: e.dma_start(out=conv_s_d, in_=cvt), ["cvt"], [])
            ktm_stage(64, 1, True, CTX_A)
            gate_stage(64, 1, 16, msks, "msks", mTs, "mTs", CTX_A)
            mlstm_tile(64, 16, 0, 0, True, msks, "msks", True, True, ytm[0], "ytm0", CTX_A)
            attn_tile(64, 0, True, None, None, [kaT[0][kv][:, 0:64] for kv in range(2)], ["kaT0_0", "kaT0_1"], None, None, va[0], "va0",
                      sbias, "sbias", first_tile=False, yt=ytm[0], ytkey="ytm0")
            for hp in range(2):
                P.dma("sp", lambda e, hp=hp: e.dma_start(
                    out=c_s_d.rearrange("s (pr hp) d v -> hp d s pr v", hp=2)[hp], in_=Css[hp * 64:(hp + 1) * 64, :, :, 0:128]),
                    ["Css"], [])
                P.dma("sp", lambda e, hp=hp: e.dma_start(
                    out=n_s_d.rearrange("s (pr hp) d -> hp d s pr", hp=2)[hp], in_=Css[hp * 64:(hp + 1) * 64, :, :, 128],
                    allow_slow_non_contiguous=True), ["Css"], [])
            P.dma("sp", lambda e: e.dma_start(out=m_s_d.rearrange("s h -> h s"), in_=mTs, allow_slow_non_contiguous=True),
                  ["mTs"], [])
            P.dma("sp", lambda e: e.dma_start(out=k_s_d[:, 124:128, :], in_=kvout[0:64, 0:128]), ["kvout"], [])
            P.dma("sp", lambda e: e.dma_start(out=v_s_d[:, 124:128, :], in_=kvout[0:64, 128:256]), ["kvout"], [])
            WMODE["save"] = False
            WMODE["load"] = n_groups > n_prefix
            wout_group(64, 1, [xs0], ["xt0"])
            ffn_group(64, 1, [xs0], ["xt0"])
            pgate_group(64, 1, [xs0], ["xt0"], [psm_d], [ys_d])
        stats = P.emit()
    return nc, stats


_CACHE = {}


def _prog():
    if "nc" not in _CACHE:
        _CACHE["nc"] = build_program()
    return _CACHE["nc"]


def kernel(x_prompt, x_sample, p_prompt, p_sample, state_mlstm_c, state_mlstm_n, state_mlstm_m,
           state_mlstm_conv, cache_swa_k, cache_swa_v, norm_mix_pre, w_in, b_gates, conv_w,
           mlstm_norm, attn_sinks, attn_norm, w_out, norm_mix_post, norm_ffn_pre, w_up, w_down,
           norm_ffn_post, w_pgate, w_pproj):
    f = lambda a: np.ascontiguousarray(np.asarray(a, dtype=np.float32))
    x_prompt, x_sample, p_prompt, p_sample = f(x_prompt), f(x_sample), f(p_prompt), f(p_sample)
    nc, _ = _prog()
    consts = make_consts()
    shared = {
        "norm_mix_pre": f(norm_mix_pre), "w_in": f(w_in)[0], "b_gates": f(b_gates), "conv_w": f(conv_w)[0],
        "mlstm_norm": f(mlstm_norm), "attn_sinks": f(attn_sinks), "attn_norm": f(attn_norm), "w_out": f(w_out)[0],
        "norm_mix_post": f(norm_mix_post), "norm_ffn_pre": f(norm_ffn_pre), "w_up": f(w_up)[0], "w_down": f(w_down)[0],
        "norm_ffn_post": f(norm_ffn_post), "w_pgate": f(w_pgate)[0], "w_pproj": f(w_pproj)[0],
    }
    shared.update(consts)
    in_maps = []
    for c in range(NCORES):
        b, j = c // 4, c % 4
        xs = np.zeros((8192, D), np.float32)
        n_real = (j + 1) * SEG
        xs[8192 - n_real:] = x_prompt[b, 0:n_real]
        sl = slice(16 * c, 16 * c + 16)
        m = dict(shared)
        m.update({
            "xs": xs,
            "pp": np.ascontiguousarray(p_prompt[0, b, j * SEG:(j + 1) * SEG]),
            "xsm": np.ascontiguousarray(x_sample[sl].reshape(64, D)),
            "psm": np.ascontiguousarray(p_sample[0, sl].reshape(64, 256)),
            "st_c": f(state_mlstm_c[0, sl]), "st_n": f(state_mlstm_n[0, sl]), "st_m": f(state_mlstm_m[0, sl]),
            "st_conv": f(state_mlstm_conv[0, sl]).reshape(48, 512),
            "ck": f(cache_swa_k[0, sl]).reshape(16, 128, 128), "cv": f(cache_swa_v[0, sl]).reshape(16, 128, 128),
            "prevbias": np.full((128, 1), NEG if j == 0 else 0.0, np.float32),
        })
        in_maps.append(m)
    res = run_bass_kernel_spmd(nc, in_maps, core_ids=list(range(NCORES)))
    R = res.results
    y = np.stack([np.concatenate([R[4 * b + j]["y"] for j in range(4)], axis=0) for b in range(2)], axis=0)
    ysm = np.concatenate([R[c]["ysm"] for c in range(NCORES)], axis=0).reshape(128, 4, D)
    last = [3, 7]
    c_p = np.stack([R[c]["c_p"] for c in last])[None]
    n_p = np.stack([R[c]["n_p"] for c in last])[None]
    m_p = np.stack([R[c]["m_p"].reshape(4) for c in last])[None]
    conv_p = np.stack([R[c]["conv_p"] for c in last])[None]
    k_p = np.stack([R[c]["k_p"].reshape(128, 2, 64) for c in last])[None]
    v_p = np.stack([R[c]["v_p"].reshape(128, 2, 64) for c in last])[None]
    cat = lambda k: np.concatenate([R[c][k] for c in range(NCORES)], axis=0)
    c_s = cat("c_s")[None]
    n_s = cat("n_s")[None]
    m_s = cat("m_s")[None]
    conv_s = cat("conv_s").reshape(128, 3, 512)[None]
    k_s = cat("k_s").reshape(128, 128, 2, 64)[None]
    v_s = cat("v_s").reshape(128, 128, 2, 64)[None]
    outs = (y, ysm, c_p, n_p, m_p, conv_p, k_p, v_p, c_s, n_s, m_s, conv_s, k_s, v_s)
    return tuple(np.ascontiguousarray(o, dtype=np.float32) for o in outs)
```
